# Optimizing a Trainium2 kernel written in Bass

```python
import math
import jax, jax.numpy as jnp
from jax import lax
import numpy as np

D_MODEL = 1024
BATCH = 4
SEQ = 4096
DEPTH = 2

HEAD_DIM = 64
BLOCK = 128
A_HEADS = 8
A_PATTERNS = ((128, 1), (512, 4), (2048, 16))
A_WIDTH = A_HEADS * HEAD_DIM
B_HEADS = 4
B_QK_DIM = HEAD_DIM
B_V_DIM = 2 * HEAD_DIM
B_WIDTH = B_HEADS * B_V_DIM
C_HEADS = 8
C_Q_RANK = 768
C_KV_RANK = 256
C_NOPE = 64
C_ROPE = 32
C_V = 64
C_WIDTH = C_HEADS * C_V
ROPE_THETA = 10000.0
REL_BUCKETS = 32
REL_MAX_DIST = 2048
IN_SIZES = (A_WIDTH, A_WIDTH, A_WIDTH,
            2 * B_HEADS * B_QK_DIM, 2 * B_HEADS * B_QK_DIM, B_WIDTH,
            C_Q_RANK, C_KV_RANK, C_ROPE)
D_IN = 3 * A_WIDTH + 4 * B_HEADS * B_QK_DIM + B_WIDTH + C_Q_RANK + C_KV_RANK + C_ROPE
N_BRANCH = 3
FFN_HIDDEN = -(-8 * D_MODEL // (3 * 256)) * 256
EPS = 1e-6
NEG = -1e30

kernel_name = 'hybrid_gated_dilated_diff_mla_block'


def rmsnorm(x, g):
    xf = x.astype(jnp.float32)
    y = xf * lax.rsqrt(jnp.mean(xf * xf, axis=-1, keepdims=True) + EPS)
    return (y * g.astype(jnp.float32)).astype(x.dtype)


def t5_bucket(dist):
    dist = jnp.maximum(dist, 0)
    exact = REL_BUCKETS // 2
    log_ratio = jnp.log(jnp.maximum(dist, 1).astype(jnp.float32) / exact) / math.log(REL_MAX_DIST / exact)
    large = jnp.minimum(exact + (log_ratio * (REL_BUCKETS - exact)).astype(jnp.int32), REL_BUCKETS - 1)
    return jnp.where(dist < exact, dist, large)


def rope(x, cos, sin):
    half = x.shape[-1] // 2
    xf = x.astype(jnp.float32)
    x1, x2 = xf[..., :half], xf[..., half:]
    return jnp.concatenate([x1 * cos - x2 * sin, x2 * cos + x1 * sin], axis=-1).astype(x.dtype)


def to_query_blocks(t):
    b, s = t.shape[:2]
    return t.reshape(b, s // BLOCK, BLOCK, *t.shape[2:]).swapaxes(0, 1)


def from_query_blocks(o):
    nq, b = o.shape[:2]
    return o.swapaxes(0, 1).reshape(b, nq * BLOCK, *o.shape[3:])


def dilated_window_attention(q, k, v, bias_table):
    b, s, h, hd = q.shape
    scale = hd ** -0.5
    outs, lses = [], []
    for window, dil in A_PATTERNS:
        n_back = window // dil
        sub_len = s // dil
        nb = -(-sub_len // BLOCK)
        lp = nb * BLOCK

        def to_sub(t):
            t = t.reshape(b, sub_len, dil, h, hd).transpose(0, 2, 1, 3, 4)
            t = jnp.pad(t, ((0, 0), (0, 0), (0, lp - sub_len), (0, 0), (0, 0)))
            return t.reshape(b, dil, nb, BLOCK, h, hd)

        def window_keys(t):
            prev = jnp.pad(t, ((0, 0), (0, 0), (1, 0), (0, 0), (0, 0), (0, 0)))[:, :, :-1]
            return jnp.concatenate([prev, t], axis=3)

        qb = to_sub(q)
        kw = window_keys(to_sub(k))
        vw = window_keys(to_sub(v))
        qi = jnp.arange(BLOCK)[:, None]
        kj = jnp.arange(2 * BLOCK)[None, :]
        step = qi + BLOCK - kj
        in_band = (step >= 0) & (step <= n_back)
        blk = jnp.arange(nb)[:, None, None]
        valid = in_band[None] & ((blk > 0) | (kj[None] >= BLOCK))
        bias = bias_table[t5_bucket(step * dil)].transpose(2, 0, 1)
        sc = jnp.einsum('brnqhc,brnkhc->brnhqk', qb, kw).astype(jnp.float32) * scale + bias
        sc = jnp.where(valid[:, None], sc, NEG)
        lse = jax.nn.logsumexp(sc, axis=-1)
        p = jnp.exp(sc - lse[..., None])
        o = jnp.einsum('brnhqk,brnkhc->brnqhc', p.astype(v.dtype), vw)
        o = o.reshape(b, dil, lp, h, hd)[:, :, :sub_len].transpose(0, 2, 1, 3, 4).reshape(b, s, h, hd)
        lse = lse.transpose(0, 1, 2, 4, 3).reshape(b, dil, lp, h)[:, :, :sub_len]
        lse = lse.transpose(0, 2, 1, 3).reshape(b, s, h)
        outs.append(o)
        lses.append(lse)
    w = jax.nn.softmax(jnp.stack(lses, axis=0), axis=0)
    out = jnp.einsum('gbsh,gbshc->bshc', w, jnp.stack(outs, axis=0).astype(jnp.float32))
    return out.astype(q.dtype).reshape(b, s, h * hd)


def differential_attention(q1, q2, k1, k2, v, lam, bias_table):
    s = q1.shape[1]
    scale = q1.shape[-1] ** -0.5
    kpos = jnp.arange(s)

    def block(args):
        i, qa, qb = args
        qpos = i * BLOCK + jnp.arange(BLOCK)
        dist = qpos[:, None] - kpos[None, :]
        causal = dist >= 0
        bias = bias_table[t5_bucket(dist)].transpose(2, 0, 1)

        def probs(qx, kx):
            sc = jnp.einsum('bqhc,bkhc->bhqk', qx, kx).astype(jnp.float32) * scale + bias
            return jax.nn.softmax(jnp.where(causal, sc, NEG), axis=-1)

        p = probs(qa, k1) - lam * probs(qb, k2)
        return jnp.einsum('bhqk,bkhc->bqhc', p.astype(v.dtype), v)

    o = lax.map(block, (jnp.arange(s // BLOCK), to_query_blocks(q1), to_query_blocks(q2)))
    return from_query_blocks(o)


def mla_attention(q_nope, q_rope, k_nope, k_rope, v):
    s = q_nope.shape[1]
    scale = (C_NOPE + C_ROPE) ** -0.5
    kpos = jnp.arange(s)

    def block(args):
        i, qn, qr = args
        qpos = i * BLOCK + jnp.arange(BLOCK)
        causal = qpos[:, None] >= kpos[None, :]
        sc = (jnp.einsum('bqhc,bkhc->bhqk', qn, k_nope)
              + jnp.einsum('bqhr,bkr->bhqk', qr, k_rope)).astype(jnp.float32) * scale
        p = jax.nn.softmax(jnp.where(causal, sc, NEG), axis=-1)
        return jnp.einsum('bhqk,bkhc->bqhc', p.astype(v.dtype), v)

    o = lax.map(block, (jnp.arange(s // BLOCK), to_query_blocks(q_nope), to_query_blocks(q_rope)))
    return from_query_blocks(o)


def setup_inputs(seed: int = 0) -> dict:
    key = jax.random.key(seed)
    ks = jax.random.split(key, 24)
    f32 = jnp.float32

    def nrm(k, shape, scale):
        return jax.random.normal(k, shape, f32) * scale

    def gain(k, shape):
        return 1.0 + 0.05 * jax.random.normal(k, shape, f32)

    return {
        'x': nrm(ks[0], (BATCH, SEQ, D_MODEL), 1.0),
        'rel_bias_table': nrm(ks[1], (REL_BUCKETS, A_HEADS + B_HEADS), 0.5),
        'ln_mix_g': gain(ks[2], (DEPTH, D_MODEL)),
        'w_in': nrm(ks[3], (DEPTH, D_MODEL, D_IN), D_MODEL ** -0.5),
        'lambda_q1': nrm(ks[4], (DEPTH, B_QK_DIM), 0.1),
        'lambda_k1': nrm(ks[5], (DEPTH, B_QK_DIM), 0.1),
        'lambda_q2': nrm(ks[6], (DEPTH, B_QK_DIM), 0.1),
        'lambda_k2': nrm(ks[7], (DEPTH, B_QK_DIM), 0.1),
        'diff_subln_g': gain(ks[8], (DEPTH, B_V_DIM)),
        'mla_q_norm_g': gain(ks[9], (DEPTH, C_Q_RANK)),
        'w_uq': nrm(ks[10], (DEPTH, C_Q_RANK, C_HEADS * (C_NOPE + C_ROPE)), C_Q_RANK ** -0.5),
        'mla_kv_norm_g': gain(ks[11], (DEPTH, C_KV_RANK)),
        'w_ukv': nrm(ks[12], (DEPTH, C_KV_RANK, C_HEADS * (C_NOPE + C_V)), C_KV_RANK ** -0.5),
        'w_gate': nrm(ks[13], (DEPTH, D_MODEL, N_BRANCH * D_MODEL), D_MODEL ** -0.5),
        'b_gate': nrm(ks[14], (DEPTH, N_BRANCH * D_MODEL), 0.1),
        'w_br_a': nrm(ks[15], (DEPTH, A_WIDTH, D_MODEL), A_WIDTH ** -0.5),
        'w_br_b': nrm(ks[16], (DEPTH, B_WIDTH, D_MODEL), B_WIDTH ** -0.5),
        'w_br_c': nrm(ks[17], (DEPTH, C_WIDTH, D_MODEL), C_WIDTH ** -0.5),
        'w_o': nrm(ks[18], (DEPTH, D_MODEL, D_MODEL), D_MODEL ** -0.5),
        'ln_ffn_g': gain(ks[19], (DEPTH, D_MODEL)),
        'w_ffn_gate': nrm(ks[20], (DEPTH, D_MODEL, FFN_HIDDEN), D_MODEL ** -0.5),
        'w_ffn_up': nrm(ks[21], (DEPTH, D_MODEL, FFN_HIDDEN), D_MODEL ** -0.5),
        'w_ffn_down': nrm(ks[22], (DEPTH, FFN_HIDDEN, D_MODEL), FFN_HIDDEN ** -0.5),
        'final_norm_g': gain(ks[23], (D_MODEL,)),
    }


def reference(x, rel_bias_table, ln_mix_g, w_in, lambda_q1, lambda_k1, lambda_q2, lambda_k2,
              diff_subln_g, mla_q_norm_g, w_uq, mla_kv_norm_g, w_ukv, w_gate, b_gate,
              w_br_a, w_br_b, w_br_c, w_o, ln_ffn_g, w_ffn_gate, w_ffn_up, w_ffn_down,
              final_norm_g):
    b, s, _ = x.shape
    f32 = jnp.float32
    split_points = [int(c) for c in np.cumsum(IN_SIZES)[:-1]]
    pos = jnp.arange(s, dtype=f32)
    inv_freq = ROPE_THETA ** (-jnp.arange(0, C_ROPE, 2, dtype=f32) / C_ROPE)
    ang = pos[:, None] * inv_freq[None, :]
    cos, sin = jnp.cos(ang), jnp.sin(ang)
    bias_a = rel_bias_table[:, :A_HEADS]
    bias_b = rel_bias_table[:, A_HEADS:]

    for l in range(DEPTH):
        h = rmsnorm(x, ln_mix_g[l])
        proj = h @ w_in[l]
        qa, ka, va, qb, kb, vb, cq, ckv, kr = jnp.split(proj, split_points, axis=-1)

        ya = dilated_window_attention(qa.reshape(b, s, A_HEADS, HEAD_DIM),
                                      ka.reshape(b, s, A_HEADS, HEAD_DIM),
                                      va.reshape(b, s, A_HEADS, HEAD_DIM), bias_a)

        lam_init = 0.8 - 0.6 * math.exp(-0.3 * l)
        lam = (jnp.exp(jnp.sum(lambda_q1[l].astype(f32) * lambda_k1[l].astype(f32)))
               - jnp.exp(jnp.sum(lambda_q2[l].astype(f32) * lambda_k2[l].astype(f32))) + lam_init)
        q_pair = qb.reshape(b, s, B_HEADS, 2, B_QK_DIM)
        k_pair = kb.reshape(b, s, B_HEADS, 2, B_QK_DIM)
        ob = differential_attention(q_pair[..., 0, :], q_pair[..., 1, :],
                                    k_pair[..., 0, :], k_pair[..., 1, :],
                                    vb.reshape(b, s, B_HEADS, B_V_DIM), lam, bias_b)
        yb = (rmsnorm(ob, diff_subln_g[l]) * (1.0 - lam_init)).reshape(b, s, B_WIDTH)

        qc = (rmsnorm(cq, mla_q_norm_g[l]) @ w_uq[l]).reshape(b, s, C_HEADS, C_NOPE + C_ROPE)
        q_nope = qc[..., :C_NOPE]
        q_rope = rope(qc[..., C_NOPE:], cos[:, None, :], sin[:, None, :])
        kvc = (rmsnorm(ckv, mla_kv_norm_g[l]) @ w_ukv[l]).reshape(b, s, C_HEADS, C_NOPE + C_V)
        k_nope, vc = kvc[..., :C_NOPE], kvc[..., C_NOPE:]
        k_rope = rope(kr, cos, sin)
        yc = mla_attention(q_nope, q_rope, k_nope, k_rope, vc).reshape(b, s, C_WIDTH)

        gates = jax.nn.sigmoid((h @ w_gate[l] + b_gate[l]).astype(f32)).astype(x.dtype)
        gates = gates.reshape(b, s, N_BRANCH, D_MODEL)
        merged = (gates[:, :, 0] * (ya @ w_br_a[l]) + gates[:, :, 1] * (yb @ w_br_b[l])
                  + gates[:, :, 2] * (yc @ w_br_c[l]))
        x = x + merged @ w_o[l]

        h = rmsnorm(x, ln_ffn_g[l])
        x = x + (jax.nn.silu(h @ w_ffn_gate[l]) * (h @ w_ffn_up[l])) @ w_ffn_down[l]

    return rmsnorm(x, final_norm_g)
```

```python
import math
import numpy as np
from contextlib import ExitStack
import concourse.bass as bass
import concourse.mybir as mybir
from concourse.bass_utils import run_bass_kernel_spmd

F32 = mybir.dt.float32
BF16 = mybir.dt.bfloat16
AF = mybir.ActivationFunctionType
ALU = mybir.AluOpType

D = 1024
DIN = 4128
FH = 2816
LA = 384
EPS = 1e-6
PATS = ((128, 1), (512, 4), (2048, 16))


class _Sem:
    def __init__(self, name):
        self.name = name
        self.h = None


class _Op:
    __slots__ = ("eng", "fn", "deps", "dma", "stream", "needed", "ev", "cc")

    def __init__(self, eng, fn, dma, stream):
        self.cc = False
        self.eng = eng
        self.fn = fn
        self.dma = dma
        self.stream = stream
        self.deps = set()
        self.needed = False
        self.ev = None


class Prog:
    def __init__(self, nc):
        self.nc = nc
        self.ops = []
        self.lastw = {}
        self.readers = {}
        self.sems = []
        self.last_of_stream = {}
        self.bar = None
        self.bar_done = set()
        self.bar_positions = []
        self.bank_of = {}
        self.bank_last = {}

    def barrier(self):
        self.bar = {k: v for k, v in self.last_of_stream.items() if not (isinstance(k, tuple) and k[0] == "cc")}
        self.bar_done = set()
        self.bar_positions.append(len(self.ops))

    def op(self, eng, fn, reads=(), writes=(), dma=False, waw=True, cc=False):
        i = len(self.ops)
        stream = ("dma", writes[0]) if dma else eng
        if cc:
            stream = ("cc", writes[0])
        o = _Op(eng, fn, dma, stream)
        o.cc = cc
        if self.bar is not None and eng not in self.bar_done:
            self.bar_done.add(eng)
            for s, j in self.bar.items():
                if s == eng and not dma and eng == "pe":
                    continue
                o.deps.add(j)
        for r in reads:
            w = self.lastw.get(r)
            if w is not None:
                self._dep(o, w, "raw")
        for r in writes:
            w = self.lastw.get(r)
            if w is not None and waw:
                self._dep(o, w, "waw")
            for x in self.readers.get(r, {}).values():
                self._dep(o, x, "war")
        for r in reads:
            self.readers.setdefault(r, {})[stream] = i
        for r in writes:
            self.lastw[r] = i
            self.readers[r] = {}
        if not dma:
            banks = set()
            for r in list(reads) + list(writes):
                b = self.bank_of.get(r)
                if b is not None:
                    banks.add(b)
            for b in banks:
                bl = self.bank_last.setdefault(b, {})
                for f_eng, j in bl.items():
                    if f_eng != eng:
                        o.deps.add(j)
                bl[eng] = i
        self.last_of_stream[stream] = i
        self.ops.append(o)
        return i

    def _dep(self, o, j, kind):
        p = self.ops[j]
        if not p.dma and not o.dma and p.eng == o.eng:
            if o.eng == "pe":
                return
            if kind == "war":
                return
        o.deps.add(j)

    def emit(self, final_wait_streams=()):
        nc = self.nc
        ops = self.ops
        MAXV = 8000
        for o in ops:
            for j in o.deps:
                ops[j].needed = True
        bars = sorted(set(self.bar_positions))
        seg_of = []
        bi = 0
        for i in range(len(ops)):
            while bi < len(bars) and bars[bi] <= i:
                bi += 1
            seg_of.append(bi)
        cnt = {}
        for i, o in enumerate(ops):
            if o.dma and not o.cc:
                k = (seg_of[i], o.stream)
                cnt[k] = cnt.get(k, 0) + 1
        free = []
        ccmap = {}
        dmap = {}
        emap = {}
        last_ev = {}

        def new_phys():
            s_ = _Sem("s%d" % len(self.sems))
            self.sems.append(s_)
            return [s_, 0]

        cur_seg = 0
        for i, o in enumerate(ops):
            if seg_of[i] != cur_seg:
                cur_seg = seg_of[i]
                for ph in dmap.values():
                    free.append(ph)
                dmap = {}
            if o.cc:
                ph = ccmap.get(o.stream)
                if ph is None:
                    ph = new_phys()
                    ccmap[o.stream] = ph
                ph[1] += 1
                o.ev = (ph[0], ph[1])
                last_ev[o.stream] = o.ev
            elif o.dma:
                ph = dmap.get(o.stream)
                if ph is None:
                    need = 16 * cnt[(cur_seg, o.stream)]
                    for fi, cand in enumerate(free):
                        if cand[1] + need <= MAXV:
                            ph = free.pop(fi)
                            break
                    if ph is None:
                        ph = new_phys()
                    dmap[o.stream] = ph
                ph[1] += 16
                o.ev = (ph[0], ph[1])
                last_ev[o.stream] = o.ev
            elif o.needed:
                ph = emap.get(o.stream)
                if ph is None or ph[1] + 1 > MAXV:
                    ph = new_phys()
                    emap[o.stream] = ph
                ph[1] += 1
                o.ev = (ph[0], ph[1])
        per_eng = {}
        for o in ops:
            per_eng.setdefault(o.eng, []).append(o)
        finals = [last_ev[s] for s in final_wait_streams if s in last_ev]
        self.nsem = len(self.sems)
        with ExitStack() as st:
            for s in self.sems:
                s.h = st.enter_context(nc.semaphore(s.name))
            block = st.enter_context(nc.Block())

            def run(eng_name, e):
                seen = {}
                for o in per_eng.get(eng_name, []):
                    need = {}
                    for j in o.deps:
                        s, v = ops[j].ev
                        if need.get(s, 0) < v:
                            need[s] = v
                    for s, v in need.items():
                        if seen.get(s, 0) < v:
                            e.wait_ge(s.h, v)
                            seen[s] = v
                    ins = o.fn(e)
                    if o.cc:
                        ins.then_inc(o.ev[0].h)
                    elif o.ev is not None:
                        ins.then_inc(o.ev[0].h, 16 if o.dma else 1)
                if eng_name == "sp":
                    for s, v in finals:
                        e.wait_ge(s.h, v)

            @block.tensor
            def _(e):
                run("pe", e)

            @block.scalar
            def _(e):
                run("act", e)

            @block.vector
            def _(e):
                run("dve", e)

            @block.gpsimd
            def _(e):
                run("pool", e)

            @block.sync
            def _(e):
                run("sp", e)


class Arena:
    def __init__(self, base, nbytes):
        self.base = base
        self.nbytes = nbytes
        self.off = 0

    def reset(self):
        self.off = 0

    def alloc(self, shape, dt):
        n = 1
        for s in shape:
            n *= s
        nb = n * (4 if dt == F32 else 2)
        nb = (nb + 63) // 64 * 64
        assert self.off + nb <= self.nbytes, (self.off, nb, self.nbytes)
        v = self.base[:, self.off // 2:(self.off + nb) // 2]
        self.off += nb
        if dt == F32:
            v = v.bitcast(F32)
        v = v[:, 0:n]
        if len(shape) == 2:
            v = v.rearrange("p (a b) -> p a b", b=shape[1])
        elif len(shape) == 3:
            v = v.rearrange("p (a b c) -> p a b c", b=shape[1], c=shape[2])
        return v


class Rot:
    def __init__(self, name, aps, P=None, banks=None):
        self.name = name
        self.aps = aps
        self.i = 0
        if banks is not None:
            for k, b in enumerate(banks):
                P.bank_of["%s#%d" % (name, k)] = b

    def next(self):
        k = self.i % len(self.aps)
        self.i += 1
        return "%s#%d" % (self.name, k), self.aps[k]


def t5_bucket_np(dist):
    dist = np.maximum(dist, 0)
    exact = 16
    lr = np.log(np.maximum(dist, 1).astype(np.float32) / np.float32(exact)) / np.float32(math.log(2048 / exact))
    large = np.minimum(exact + (lr.astype(np.float32) * np.float32(16)).astype(np.int32), 31)
    return np.where(dist < exact, dist, large)


def t5_bucket_exact(dist):
    return t5_bucket_np(np.asarray(dist))


FM_QA, FM_KA, FM_QB, FM_KB, FM_QC, FM_KC, FM_ROWS = 0, 512, 1024, 1536, 2048, 2816, 3584
TM_VA, TM_VB, TM_VC, TM_COLS = 0, 520, 1036, 1556


def build(S, L, dbg=False, ncores=8):
    NT = S // 128
    NG = S // 512
    SL = S // 2
    NGL = SL // 512
    groups = [[2 * i, 2 * i + 1] for i in range(ncores // 2)]
    LB = ((S + 127 + 383) // 384) * 384
    nc = bass.Bass("TRN2", target_bir_lowering=False)
    P = Prog(nc)

    def din(name, shape, dt=F32):
        return nc.dram_tensor(name, list(shape), dt, kind="ExternalInput")

    def dscr(name, shape, dt):
        return nc.dram_tensor(name, list(shape), dt, kind="ExternalOutput" if dbg else "Internal")

    x_in = din("x", [SL, D])
    w_in = din("w_in", [L, D, DIN])
    w_krs = din("w_krs", [L, D, 32])
    w_uqp = din("w_uqp", [L, 768, 768])
    w_uqs = din("w_uqs", [L, 768, 256])
    w_ukvp = din("w_ukvp", [L, 256, 1024])
    w_gate = din("w_gate", [L, D, 3 * D])
    w_bra = din("w_br_a", [L, 512, D])
    w_brb = din("w_br_b", [L, 512, D])
    w_brc = din("w_br_c", [L, 512, D])
    w_o = din("w_o", [L, D, D])
    w_fg = din("w_ffn_gate", [L, D, FH])
    w_fu = din("w_ffn_up", [L, D, FH])
    w_fd = din("w_ffn_down", [L, FH, D])
    g_mix = din("g_mix", [L, 128, D])
    g_q = din("g_q", [L, 128, 768])
    g_kv = din("g_kv", [L, 128, 256])
    g_ffn = din("g_ffn", [L, 128, D])
    g_fin = din("g_fin", [128, D])
    g_sub = din("g_sub", [L, 128, 128])
    lam_v = din("lam_v", [L, 128, 256])
    b_gate = din("b_gate", [L, 128, 24])
    tabrep = din("tabrep", [6, 33, 128])
    oh_b = din("oh_b", [33, LB])
    oh_a = din("oh_a", [33, 3 * LA])
    cos_d = din("cos32", [32, SL])
    sin_d = din("sin32", [32, SL])
    ident_d = din("ident", [128, 128])
    maskc_d = din("maskc", [128, 128])
    msel_d = din("msel", [128, 2])
    out_d = nc.dram_tensor("out", [SL, D], F32, kind="ExternalOutput")

    def dint(name, shape, dt):
        return nc.dram_tensor(name, list(shape), dt)

    XT = {}

    def xbuf(name, rows, cols):
        XT[name] = (dint(name, [rows, cols], BF16), dint(name + "g", [2 * rows, cols], BF16), rows)

    for n_ in ("QA", "KA", "QB", "KB"):
        xbuf(n_, 512, SL)
    for n_ in ("QC0", "QC1", "KC0", "KC1"):
        xbuf(n_, 384, SL)
    for n_ in ("VA0", "VA1", "VC0", "VC1"):
        xbuf(n_, SL, 260)
    for n_ in ("VB0", "VB1"):
        xbuf(n_, SL, 258)
    xbuf("YTA", 256, S)
    xbuf("YB", S, 256)
    xbuf("YC", S, 256)

    def XL(name):
        return XT[name][0]

    def XG(name):
        return XT[name][1]

    XMID = dscr("XMID", [SL, D], F32)
    XRES = dscr("XRES", [SL, D], F32)
    FB_D = dint("FB_D", [2, 128, LB], BF16)
    FA_D = dint("FA_D", [4, 3, 128, LA], BF16)

    st = ExitStack()
    ARENA_B = 206 * 1024
    arena_t = st.enter_context(nc.sbuf_tensor("arena", [128, ARENA_B // 2], BF16))
    A = Arena(arena_t, ARENA_B)
    idb_t = st.enter_context(nc.sbuf_tensor("idb", [128, 128], BF16))
    mkc_t = st.enter_context(nc.sbuf_tensor("mkc", [128, 128], BF16))
    onesf_t = st.enter_context(nc.sbuf_tensor("onesf", [128, 64], F32))
    sm_t = st.enter_context(nc.sbuf_tensor("smalls", [128, 64], F32))
    msel_t = st.enter_context(nc.sbuf_tensor("msel_sb", [128, 2], F32))
    msel = msel_t[:]
    idb = idb_t[:]
    mkc = mkc_t[:]
    onesf = onesf_t[:]
    psb = [st.enter_context(nc.psum_tensor("ps%d" % i, [128, 512], F32)) for i in range(8)]

    sm_i = [0]

    def small():
        k = sm_i[0] % 64
        sm_i[0] += 1
        return "sm#%d" % k, sm_t[:, k:k + 1]

    def dma(out, in_, reads, writes, eng="sp", waw=True):
        P.op(eng, lambda e: e.dma_start(out=out, in_=in_), reads=reads, writes=writes, dma=True, waw=waw)

    def sel(dst, dkey, c0, k0, c1, k1, np_=128, waw=True):
        P.op("pool", lambda e: e.tensor_scalar(out=c1, in0=c1, scalar1=msel[0:np_, 1:2], scalar2=None, op0=ALU.mult), reads=[k1, "msel"], writes=[k1])
        P.op("pool", lambda e: e.tensor_scalar(out=c0, in0=c0, scalar1=msel[0:np_, 0:1], scalar2=None, op0=ALU.mult), reads=[k0, "msel"], writes=[k0])
        P.op("pool", lambda e: e.tensor_tensor(out=dst, in0=c0, in1=c1, op=ALU.add), reads=[k0, k1], writes=[dkey], waw=waw)

    def allgather(name):
        src, dst = XL(name), XG(name)
        P.op("pool", lambda e: e.collective_compute("AllGather", ALU.bypass, replica_groups=groups, ins=[src.ap().opt()], outs=[dst.ap().opt()]),
             reads=[name], writes=[name + "g"], cc=True)

    def mm(out, lhsT, rhs, start, stop, reads, writes, skip=False):
        if skip:
            P.op("pe", lambda e: e.matmul(out, lhsT=lhsT, rhs=rhs, start=start, stop=stop, skip_group_check=True), reads=reads, writes=writes)
        else:
            P.op("pe", lambda e: e.matmul(out, lhsT=lhsT, rhs=rhs, start=start, stop=stop), reads=reads, writes=writes)

    def tr(out, in_, reads, writes):
        P.op("pe", lambda e: e.transpose(out=out, in_=in_, identity=idb), reads=list(reads) + ["idb"], writes=writes)

    def act(out, in_, func, reads, writes, scale=None, bias=None, accum=None, waw=True):
        kw = {}
        if scale is not None:
            kw["scale"] = scale
        if bias is not None:
            kw["bias"] = bias
        if accum is not None:
            kw["accum_out"] = accum
        P.op("act", lambda e: e.activation(out=out, in_=in_, func=func, **kw), reads=reads, writes=writes, waw=waw)

    def tt(eng, out, in0, in1, op, reads, writes, waw=True):
        P.op(eng, lambda e: e.tensor_tensor(out=out, in0=in0, in1=in1, op=op), reads=reads, writes=writes, waw=waw)

    def stt(out, in0, scalar, in1, op0, op1, reads, writes, waw=True):
        P.op("dve", lambda e: e.scalar_tensor_tensor(out=out, in0=in0, scalar=scalar, in1=in1, op0=op0, op1=op1), reads=reads, writes=writes, waw=waw)

    def tcopy(eng, out, in_, reads, writes, waw=True):
        P.op(eng, lambda e: e.tensor_copy(out=out, in_=in_), reads=reads, writes=writes, waw=waw)

    def recip(out, in_, reads, writes):
        P.op("dve", lambda e: e.reciprocal(out=out, in_=in_), reads=reads, writes=writes)

    def tsadd(out, in0, c, reads, writes):
        P.op("dve", lambda e: e.tensor_scalar(out=out, in0=in0, scalar1=c, scalar2=None, op0=ALU.add), reads=reads, writes=writes)

    def memset(eng, ap, v, writes):
        P.op(eng, lambda e: e.memset(ap, v), writes=writes)

    def wload(dst3, dst_key, src2d, nchunk):
        for c in range(nchunk):
            dma(dst3[:, c, :], src2d[c * 128:(c + 1) * 128, :], [], [dst_key], eng="pool", waw=False)

    evac = [0]

    def evac_copy(out, in_, reads, writes, waw=True):
        evac[0] += 1
        if evac[0] % 2:
            act(out, in_, AF.Copy, reads, writes, waw=waw)
        else:
            tcopy("dve", out, in_, reads, writes, waw=waw)

    def rms_scale(src, skey, Dn, junk, jk):
        sk, ss = small()
        rk, rs = small()
        act(junk, src, AF.Square, [skey], [jk, sk], scale=float(Dn) ** -0.5, accum=ss)
        tsadd(ss, ss, EPS, [sk], [sk])
        act(ss, ss, AF.Sqrt, [sk], [sk])
        recip(rs, ss, [sk], [rk])
        return rk, rs

    def phase0():
        A.reset()
        idf = A.alloc([128], F32)
        mkf = A.alloc([128], F32)
        dma(idf, ident_d[:, :], [], ["idf"])
        dma(mkf, maskc_d[:, :], [], ["mkf"])
        dma(msel, msel_d[:, :], [], ["msel"])
        tcopy("dve", idb, idf, ["idf"], ["idb"])
        tcopy("dve", mkc, mkf, ["mkf"], ["mkc"])
        memset("dve", onesf, 1.0, ["onesf"])
        ohb = A.alloc([LB], F32)
        oha = A.alloc([3 * LA], F32)
        dma(ohb[0:33], oh_b[:, :], [], ["ohb"])
        dma(oha[0:33], oh_a[:, :], [], ["oha"])
        tabs = Rot("tab", [A.alloc([128], F32) for _ in range(2)])
        stg = Rot("fstg", [A.alloc([LA], BF16) for _ in range(3)])
        psr = Rot("ps", [psb[i][:] for i in range(4)], P, [0, 1, 2, 3])
        for h in range(6):
            tk, tb = tabs.next()
            dma(tb[0:33], tabrep[h, :, :], [], [tk])
            if h < 4:
                chunks = [(oha, "oha", p * LA, FA_D[h, p, :, :], "FA_D") for p in range(3)]
            else:
                chunks = [(ohb, "ohb", c * LA, FB_D[h - 4, :, c * LA:(c + 1) * LA], "FB_D") for c in range(LB // LA)]
            for (src, skey, off, dst, dname) in chunks:
                pk, pp = psr.next()
                mm(pp[:, 0:LA], tb[0:33], src[0:33, off:off + LA], True, True, [tk, skey], [pk])
                sk, sg = stg.next()
                act(sg, pp[:, 0:LA], AF.Exp, [pk], [sk])
                dma(dst, sg, [sk], [dname], waw=False)

    def norm_T(src, skey, C, Dn, g_ap, gkey, dst3, dkey, rot):
        jk, junk = rot["junk"].next()
        xk, xn = rot["xn"].next()
        W = C * 128
        rk, rs = rms_scale(src, skey, Dn, junk[:, 0:W], jk)
        stt(xn[:, 0:W], src, rs, g_ap, ALU.mult, ALU.mult, [skey, rk, gkey], [xk])
        pk, pp = rot["pT"].next()
        ppb = pp.bitcast(BF16)
        for c in range(C):
            tr(ppb[:, c * 128:(c + 1) * 128], xn[:, c * 128:(c + 1) * 128], [xk], [pk])
        act(dst3, ppb[:, 0:W].rearrange("p (c t) -> p c t", t=128), AF.Copy, [pk], [dkey], waw=False)

    def std_rots():
        return {
            "junk": Rot("junk", [A.alloc([1024], BF16)]),
            "xn": Rot("xn", [A.alloc([1024], BF16) for _ in range(2)]),
            "pT": Rot("psT", [psb[6][:], psb[7][:]], P, [6, 7]),
        }

    def phase1(l, xsrc, xsrc_key):
        P.barrier()
        A.reset()
        win = A.alloc([8, DIN], BF16)
        wkrs = A.alloc([8, 32], BF16)
        wuq = A.alloc([6, 768], BF16)
        wuqs = A.alloc([6, 256], BF16)
        wukv = A.alloc([2, 1024], BF16)
        gm = A.alloc([D], F32)
        gq = A.alloc([768], F32)
        gkv = A.alloc([256], F32)
        dma(gm, g_mix[l, :, :], [], ["gm"])
        dma(gq, g_q[l, :, :], [], ["gq"])
        dma(gkv, g_kv[l, :, :], [], ["gkv"])
        wload(win, "win", w_in[l, :, :], 8)
        wload(wkrs, "wkrs", w_krs[l, :, :], 8)
        wload(wuq, "wuq", w_uqp[l, :, :], 6)
        wload(wuqs, "wuqs", w_uqs[l, :, :], 6)
        wload(wukv, "wukv", w_ukvp[l, :, :], 2)
        rot = std_rots()
        xt_r = Rot("xt", [A.alloc([D], F32) for _ in range(2)])
        hT_r = Rot("hT", [A.alloc([8, 512], BF16) for _ in range(2)])
        cq_r = Rot("cq", [A.alloc([768], F32) for _ in range(2)])
        ckv_r = Rot("ckv", [A.alloc([256], F32) for _ in range(2)])
        cqT_r = Rot("cqT", [A.alloc([6, 512], BF16)])
        ckvT_r = Rot("ckvT", [A.alloc([2, 512], BF16)])
        fst_r = Rot("fst", [A.alloc([512], BF16) for _ in range(4)])
        va_r = Rot("vast", [A.alloc([8, 65], BF16) for _ in range(2)])
        vc_r = Rot("vcst", [A.alloc([8, 65], BF16) for _ in range(2)])
        vb_r = Rot("vbst", [A.alloc([4, 129], BF16) for _ in range(2)])
        cs_r = Rot("cs", [A.alloc([512], F32) for _ in range(2)])
        sn_r = Rot("sn", [A.alloc([512], F32) for _ in range(2)])
        t1_r = Rot("t1", [A.alloc([512], F32) for _ in range(2)])
        t2_r = Rot("t2", [A.alloc([512], F32) for _ in range(2)])
        mm_r = Rot("psm", [psb[i][:] for i in range(6)], P, list(range(6)))
        for r_ in (va_r, vc_r, vb_r):
            for i_, ap_ in enumerate(r_.aps):
                memset("pool", ap_, 1.0, ["%s#%d" % (r_.name, i_)])

        def rope_evac(pm, pmk, psw, pswk, dst, dkey, ck, cs, sk, sn):
            k1, t1 = t1_r.next()
            k2, t2 = t2_r.next()
            tt("dve", t1[0:32], pm[0:32, :], cs[0:32], ALU.mult, [pmk, ck], [k1])
            tt("dve", t2[0:32], psw[0:32, :], sn[0:32], ALU.mult, [pswk, sk], [k2])
            tt("pool", dst, t1[0:32], t2[0:32], ALU.add, [k1, k2], [dkey], waw=False)

        def tokmm(hT, hk, ts_, col, n, pp, pk):
            for k in range(8):
                mm(pp[:, 0:n], hT[:, k, ts_], win[:, k, col:col + n], k == 0, k == 7, ["win", hk], [pk])

        for g in range(NGL):
            tok = slice(g * 512, (g + 1) * 512)
            hk, hT = hT_r.next()
            for t in range(4):
                xk, xt = xt_r.next()
                r0 = g * 512 + t * 128
                dma(xt, xsrc[r0:r0 + 128, :], [xsrc_key], [xk])
                norm_T(xt, xk, 8, D, gm, "gm", hT[:, :, t * 128:(t + 1) * 128], hk, rot)
            ck, cs = cs_r.next()
            sk, sn = sn_r.next()
            dma(cs[0:32], cos_d[:, tok], [], [ck])
            dma(sn[0:32], sin_d[:, tok], [], [sk])
            for (c0, dname) in ((0, "QA"), (512, "KA"), (1536, "QB"), (2048, "KB")):
                for c in range(4):
                    pk, pp = mm_r.next()
                    col = c0 + c * 128
                    for k in range(8):
                        mm(pp, win[:, k, col:col + 128], hT[:, k, :], k == 0, k == 7, ["win", hk], [pk])
                    fk, fs = fst_r.next()
                    evac_copy(fs, pp, [pk], [fk])
                    dma(XL(dname)[c * 128:(c + 1) * 128, tok], fs, [fk], [dname], waw=False)
            pk, pp = mm_r.next()
            for k in range(8):
                mm(pp[0:32, :], win[:, k, 4096:4128], hT[:, k, :], k == 0, k == 7, ["win", hk], [pk])
            pk2, pp2 = mm_r.next()
            for k in range(8):
                mm(pp2[0:32, :], wkrs[:, k, :], hT[:, k, :], k == 0, k == 7, ["wkrs", hk], [pk2])
            fk, fs = fst_r.next()
            rope_evac(pp, pk, pp2, pk2, fs[0:32], fk, ck, cs, sk, sn)
            for h in range(8):
                dma(XL("KC%d" % (h // 4))[(h % 4) * 96:(h % 4) * 96 + 32, tok], fs[0:32], [fk], ["KC%d" % (h // 4)], waw=False)
            cqk, cqT = cqT_r.next()
            ckk, ckvT = ckvT_r.next()
            for t in range(4):
                ts_ = slice(t * 128, (t + 1) * 128)
                r0 = g * 512 + t * 128
                pk, pp = mm_r.next()
                tokmm(hT, hk, ts_, 1024, 512, pp, pk)
                vk, vs = va_r.next()
                evac_copy(vs[:, :, 0:64], pp.rearrange("p (h d) -> p h d", d=64), [pk], [vk], waw=False)
                for c_ in range(2):
                    dma(XL("VA%d" % c_)[r0:r0 + 128, :], vs[:, 4 * c_:4 * c_ + 4, :].rearrange("p h d -> p (h d)"), [vk], ["VA%d" % c_], waw=False)
                pk, pp = mm_r.next()
                tokmm(hT, hk, ts_, 2560, 512, pp, pk)
                vk, vs = vb_r.next()
                evac_copy(vs[:, :, 0:128], pp.rearrange("p (h d) -> p h d", d=128), [pk], [vk], waw=False)
                for c_ in range(2):
                    dma(XL("VB%d" % c_)[r0:r0 + 128, :], vs[:, 2 * c_:2 * c_ + 2, :].rearrange("p h d -> p (h d)"), [vk], ["VB%d" % c_], waw=False)
                qk_, cq = cq_r.next()
                pk, pp = mm_r.next()
                tokmm(hT, hk, ts_, 3072, 512, pp, pk)
                evac_copy(cq[:, 0:512], pp, [pk], [qk_], waw=False)
                pk, pp = mm_r.next()
                tokmm(hT, hk, ts_, 3584, 256, pp, pk)
                evac_copy(cq[:, 512:768], pp[:, 0:256], [pk], [qk_], waw=False)
                norm_T(cq, qk_, 6, 768, gq, "gq", cqT[:, :, ts_], cqk, rot)
                kk_, ckv = ckv_r.next()
                pk, pp = mm_r.next()
                tokmm(hT, hk, ts_, 3840, 256, pp, pk)
                evac_copy(ckv, pp[:, 0:256], [pk], [kk_])
                norm_T(ckv, kk_, 2, 256, gkv, "gkv", ckvT[:, :, ts_], ckk, rot)
                pk, pp = mm_r.next()
                for k in range(2):
                    mm(pp, ckvT[:, k, ts_], wukv[:, k, 512:1024], k == 0, k == 1, ["wukv", ckk], [pk])
                vk, vs = vc_r.next()
                evac_copy(vs[:, :, 0:64], pp.rearrange("p (h d) -> p h d", d=64), [pk], [vk], waw=False)
                for c_ in range(2):
                    dma(XL("VC%d" % c_)[r0:r0 + 128, :], vs[:, 4 * c_:4 * c_ + 4, :].rearrange("p h d -> p (h d)"), [vk], ["VC%d" % c_], waw=False)
            for h in range(8):
                pk, pp = mm_r.next()
                for k in range(6):
                    mm(pp[0:96, :], wuq[:, k, h * 96:(h + 1) * 96], cqT[:, k, :], k == 0, k == 5, ["wuq", cqk], [pk])
                pk2, pp2 = mm_r.next()
                for k in range(6):
                    mm(pp2[0:32, :], wuqs[:, k, h * 32:(h + 1) * 32], cqT[:, k, :], k == 0, k == 5, ["wuqs", cqk], [pk2])
                fk, fs = fst_r.next()
                act(fs[32:64], pp[32:64, :], AF.Copy, [pk], [fk])
                tcopy("dve", fs[64:96], pp[64:96, :], [pk], [fk], waw=False)
                rope_evac(pp, pk, pp2, pk2, fs[0:32], fk, ck, cs, sk, sn)
                dma(XL("QC%d" % (h // 4))[(h % 4) * 96:(h % 4 + 1) * 96, tok], fs[0:96], [fk], ["QC%d" % (h // 4)], waw=False)
            for hp in range(4):
                pk, pp = mm_r.next()
                for k in range(2):
                    mm(pp, wukv[:, k, hp * 128:(hp + 1) * 128], ckvT[:, k, :], k == 0, k == 1, ["wukv", ckk], [pk])
                fk, fs = fst_r.next()
                evac_copy(fs, pp, [pk], [fk])
                for j in range(2):
                    h = hp * 2 + j
                    dma(XL("KC%d" % (h // 4))[(h % 4) * 96 + 32:(h % 4 + 1) * 96, tok], fs[j * 64:(j + 1) * 64], [fk], ["KC%d" % (h // 4)], waw=False)
        for n_ in ("QA", "KA", "VA0", "VA1", "QB", "KB", "VB0", "VB1", "QC0", "QC1", "KC0", "KC1", "VC0", "VC1"):
            allgather(n_)

    SKEW = 2

    def run_pipeline(steps):
        n = len(steps)
        for i in range(n + SKEW):
            if i < n:
                steps[i][0]()
            if i - SKEW >= 0:
                steps[i - SKEW][1]()

    def phase2a(l):
        P.barrier()
        A.reset()
        Vp = [A.alloc([NT, 260], BF16) for _ in range(3)]
        stg = [A.alloc([NT * 260], BF16) for _ in range(2)]
        for pi, (win_, dil) in enumerate(PATS):
            nb = S // dil // 128
            for c in range(2):
                cv = stg[c].rearrange("p (t c) -> p t c", c=260)
                for r in range(dil):
                    src = bass.AP(XG("VA%d" % c), r * 260, [[dil * 260, 128], [dil * 128 * 260, nb], [1, 260]])
                    dma(cv[:, r * nb:(r + 1) * nb, :], src, ["VA%dg" % c], ["stg%d" % c], waw=False)
            sel(Vp[pi].rearrange("p t c -> p (t c)"), "Vp%d" % pi, stg[0], "stg0", stg[1], "stg1")
        q_r = Rot("qa", [A.alloc([S], BF16) for _ in range(2)])
        k_r = Rot("ka", [A.alloc([S], BF16) for _ in range(2)])
        ea_r = Rot("ea", [A.alloc([3, 256], BF16) for _ in range(2)])
        acc_r = Rot("acc", [A.alloc([S], F32) for _ in range(2)])
        pt_r = Rot("pt", [A.alloc([256], BF16) for _ in range(4)])
        pm_r = Rot("pm", [A.alloc([256], BF16) for _ in range(4)])
        rl_r = Rot("rl", [A.alloc([512], F32) for _ in range(2)])
        ys_r = Rot("ys", [A.alloc([512], BF16) for _ in range(2)])
        ps_s = Rot("pss", [psb[i][:] for i in range(3)], P, [0, 1, 2])
        ps_o = Rot("pso", [psb[3 + i][:, 0:128] for i in range(3)], P, [3, 4, 5])
        ps_l = Rot("psl", [psb[6][:], psb[7][:]], P, [6, 7])
        steps = []

        def mk_front(kk, qk, ek, sk_, sp_, kap, qap, nq, tk, pt, mk, pm, eap):
            def f():
                mm(sp_[:, 0:nq], kap, qap, True, True, [kk, qk], [sk_])
                act(pt[:, 0:nq], sp_[:, 0:nq], AF.Exp, [sk_], [tk], scale=0.125)
                tt("dve", pm[:, 0:nq], pt[:, 0:nq], eap, ALU.mult, [tk, ek], [mk])
            return f

        def mk_back(b, nb, mk, pm, vt, vkey, okey, o_, okey_n, o_n, aap, pi, acck, tail):
            def f():
                mm(o_[0:65, :], vt, pm[:, 0:128], b == 0, True, [mk, vkey], [okey], skip=True)
                if b + 1 < nb:
                    mm(o_n[0:65, :], vt, pm[:, 128:256], True, False, [mk, vkey], [okey_n], skip=True)
                if pi == 0:
                    tcopy("dve", aap, o_[0:65, :], [okey], [acck], waw=False)
                else:
                    tt("dve", aap, aap, o_[0:65, :], ALU.add, [okey, acck], [acck], waw=False)
                if tail is not None:
                    tail()
            return f

        def mk_tail(h, acc, acck):
            def f():
                for c in range(NG):
                    cs_ = slice(c * 512, (c + 1) * 512)
                    lk, lp = ps_l.next()
                    mm(lp[0:64, :], onesf[64:65, 0:64], acc[64:65, cs_], True, True, [acck, "onesf"], [lk])
                    rk, rl = rl_r.next()
                    recip(rl[0:64], lp[0:64, :], [lk], [rk])
                    yk, ys = ys_r.next()
                    tt("dve", ys[0:64], acc[0:64, cs_], rl[0:64], ALU.mult, [rk, acck], [yk])
                    dma(XL("YTA")[h * 64:(h + 1) * 64, cs_], ys[0:64], [yk], ["YTA"], waw=False)
            return f

        def mk_loads(h, qk, qt, kk, kt, ek, ea):
            def f():
                for (dst, dk, xn_, so) in ((qt, qk, "QA", 0), (kt, kk, "KA", 4096)):
                    for c in range(2):
                        for r in range(2):
                            b0 = r * 512 + (4 * c + h) * 64
                            dma(stg[c][0:64, so + r * SL:so + (r + 1) * SL], XG(xn_)[b0:b0 + 64, :], [xn_ + "g"], ["stg%d" % c], waw=False)
                    sel(dst[0:64, :], dk, stg[0][0:64, so:so + S], "stg0", stg[1][0:64, so:so + S], "stg1", np_=64)
                for pi in range(3):
                    dma(ea[:, pi, :], bass.AP(FA_D, ((h * 3 + pi) * 128) * LA + 127, [[LA - 1, 128], [1, 256]]), ["FA_D"], [ek], waw=False)
            return f

        for h in range(4):
            qk, qt = q_r.next()
            kk, kt = k_r.next()
            ek, ea = ea_r.next()
            acck, acc = acc_r.next()
            loads = mk_loads(h, qk, qt, kk, kt, ek, ea)
            first = True
            for pi, (win_, dil) in enumerate(PATS):
                nb = S // dil // 128
                vkey = "Vp%d" % pi
                for r in range(dil):
                    okey_n, o_n = None, None
                    for b in range(nb):
                        nq = 256 if b + 1 < nb else 128
                        base = r + dil * 128 * b
                        kap = kt[0:64, base:base + dil * 127 + 1:dil]
                        qap = qt[0:64, base:base + dil * (nq - 1) + 1:dil]
                        sk_, sp_ = ps_s.next()
                        tk, pt = pt_r.next()
                        mk, pm = pm_r.next()
                        fr = mk_front(kk, qk, ek, sk_, sp_, kap, qap, nq, tk, pt, mk, pm, ea[:, pi, 0:nq])
                        if first:
                            fr = (lambda lo, f0: (lambda: (lo(), f0())))(loads, fr)
                            first = False
                        vt = Vp[pi][:, r * nb + b, h * 65:(h + 1) * 65]
                        if b == 0:
                            okey, o_ = ps_o.next()
                        else:
                            okey, o_ = okey_n, o_n
                        if b + 1 < nb:
                            okey_n, o_n = ps_o.next()
                        aap = acc[0:65, base:base + dil * 127 + 1:dil]
                        last = (pi == 2 and r == dil - 1 and b == nb - 1)
                        tail = mk_tail(h, acc, acck) if last else None
                        bk = mk_back(b, nb, mk, pm, vt, vkey, okey, o_, okey_n, o_n, aap, pi, acck, tail)
                        steps.append((fr, bk))
        run_pipeline(steps)
        allgather("YTA")

    def phase2bc(l, which):
        P.barrier()
        A.reset()
        isB = which == "B"
        nh = 2 if isB else 4
        dv = 128 if isB else 64
        dr = 128 if isB else 96
        vw = (dv + 1) * nh
        vpre, yname = ("VB", "YB") if isB else ("VC", "YC")
        scale = 0.125 if isB else 96.0 ** -0.5
        Vt = A.alloc([NT, vw], BF16)
        stg = [A.alloc([NT * 260], BF16) for _ in range(2)]
        for c in range(2):
            dma(stg[c][:, 0:NT * vw].rearrange("p (t c) -> p t c", c=vw), XG("%s%d" % (vpre, c)).ap().rearrange("(t p) c -> p t c", p=128), ["%s%dg" % (vpre, c)], ["stg%d" % c])
        sel(Vt.rearrange("p t c -> p (t c)"), "Vt", stg[0][:, 0:NT * vw], "stg0", stg[1][:, 0:NT * vw], "stg1")
        Ysb = A.alloc([NT, 256], BF16)
        q_r = Rot("qb", [A.alloc([S], BF16) for _ in range(2)])
        k_r = Rot("kb", [A.alloc([S], BF16) for _ in range(2)])
        e_r = Rot("eb", [A.alloc([512], BF16) for _ in range(4)])
        pt_r = Rot("ptb", [A.alloc([512], BF16) for _ in range(4)])
        pm_r = Rot("pmb", [A.alloc([512], BF16) for _ in range(8)])
        ps_s = Rot("pssb", [psb[i][:] for i in range(3)], P, [0, 1, 2])
        gs = nlam = knl = None
        if isB:
            gs = A.alloc([128], F32)
            lv = A.alloc([256], F32)
            lj = A.alloc([64], F32)
            t0_r = Rot("t0", [A.alloc([128], F32) for _ in range(2)])
            ob_r = Rot("ob", [A.alloc([128], F32) for _ in range(2)])
            jb_r = Rot("jb", [A.alloc([128], F32) for _ in range(2)])
            dma(gs, g_sub[l, :, :], [], ["gs"])
            dma(lv, lam_v[l, :, :], [], ["lv"])
            lam_init = 0.8 - 0.6 * math.exp(-0.3 * l)
            P.op("act", lambda e: e.mul(out=gs, in_=gs, mul=float(1.0 - lam_init)), reads=["gs"], writes=["gs"])
            k1, s1 = "lam_s1", A.alloc([1], F32)
            k2, s2 = "lam_s2", A.alloc([1], F32)
            knl, nlam = "nlam", A.alloc([1], F32)
            tt("dve", lj, lv[:, 0:64], lv[:, 64:128], ALU.mult, ["lv"], ["lj"])
            P.op("dve", lambda e: e.reduce_sum(out=s1, in_=lj, axis=mybir.AxisListType.X), reads=["lj"], writes=[k1])
            tt("dve", lj, lv[:, 128:192], lv[:, 192:256], ALU.mult, ["lv", k1], ["lj"])
            P.op("dve", lambda e: e.reduce_sum(out=s2, in_=lj, axis=mybir.AxisListType.X), reads=["lj"], writes=[k2])
            act(s1, s1, AF.Exp, [k1], [k1])
            act(s2, s2, AF.Exp, [k2], [k2])
            stt(nlam, s2, float(-lam_init), s1, ALU.add, ALU.subtract, [k1, k2], [knl])
        if isB:
            regs = [psb[3 + i // 3][:, (i % 3) * 129:(i % 3) * 129 + 129] for i in range(8)]
            okeys = ["oacc%d" % (3 + i // 3) for i in range(8)]
        else:
            regs = [psb[3 + (i // 4)][:, (i % 4) * 65:(i % 4) * 65 + 65] for i in range(8)]
            okeys = ["oacc%d" % (3 + i // 4) for i in range(8)]
        for b_ in (3, 4, 5):
            P.bank_of["oacc%d" % b_] = b_
        steps = []

        def mk_loads(h, qk, qt, kk, kt):
            def f():
                for (dst, dk, qk_sel, so) in ((qt, qk, "Q", 0), (kt, kk, "K", 4096)):
                    for c in range(2):
                        for r in range(2):
                            if isB:
                                xn_, b0 = qk_sel + "B", r * 512 + (2 * c + h) * 128
                            else:
                                xn_, b0 = "%sC%d" % (qk_sel, c), r * 384 + h * 96
                            dma(stg[c][0:dr, so + r * SL:so + (r + 1) * SL], XG(xn_)[b0:b0 + dr, :], [xn_ + "g"], ["stg%d" % c], waw=False)
                    sel(dst[0:dr, :], dk, stg[0][0:dr, so:so + S], "stg0", stg[1][0:dr, so:so + S], "stg1", np_=dr)
            return f

        def mk_front(h, g, j, qk, qt, kk, kt, bufs, loads):
            qlo = max(4 * g, j)
            W = (4 * g + 4 - qlo) * 128

            def f():
                if loads is not None:
                    loads()
                if isB:
                    ek, et = bufs["e"]
                    off = (h * 128) * LB + (qlo - j) * 128 + 127
                    dma(et[:, 0:W], bass.AP(FB_D, off, [[LB - 1, 128], [1, W]]), ["FB_D"], [ek])
                for w, (sk_, sp_, tk, pt, mk, pm) in enumerate(bufs["w"]):
                    rows = slice(w * 64, (w + 1) * 64) if isB else slice(0, 96)
                    mm(sp_[:, 0:W], kt[rows, j * 128:(j + 1) * 128], qt[rows, qlo * 128:(4 * g + 4) * 128], True, True, [kk, qk], [sk_])
                    act(pt[:, 0:W], sp_[:, 0:W], AF.Exp, [sk_], [tk], scale=float(scale))
                    if isB:
                        tt("dve", pm[:, 0:W], pt[:, 0:W], et[:, 0:W], ALU.mult, [tk, ek], [mk])
                    elif qlo == j:
                        tt("dve", pt[:, 0:128], pt[:, 0:128], mkc, ALU.mult, [tk, "mkc"], [tk])
            return f

        def mk_back(h, g, j, bufs, rsel):
            qlo = max(4 * g, j)

            def f():
                for w, (sk_, sp_, tk, pt, mk, pm) in enumerate(bufs["w"]):
                    for qb in range(qlo, 4 * g + 4):
                        ri = rsel[w * 4 + (qb - 4 * g)] if isB else rsel[qb - 4 * g]
                        c0 = (qb - qlo) * 128
                        stf = (j == 0) and (okeys[ri] not in bufs["started"])
                        bufs["started"].add(okeys[ri])
                        mm(regs[ri], pm[:, c0:c0 + 128], Vt[:, j, h * (dv + 1):(h + 1) * (dv + 1)], stf, j == qb, [mk, "Vt"], [okeys[ri]], skip=True)
                if j == 4 * g + 3:
                    epilogue(h, g, rsel)
            return f

        def epilogue(h, g, rsel):
            for qi in range(4):
                qb = 4 * g + qi
                if isB:
                    r0_, r1_ = regs[rsel[qi]], regs[rsel[4 + qi]]
                    ok0, ok1 = okeys[rsel[qi]], okeys[rsel[4 + qi]]
                    ka, ra = small()
                    kb_, rb = small()
                    recip(ra, r0_[:, 128:129], [ok0], [ka])
                    recip(rb, r1_[:, 128:129], [ok1], [kb_])
                    tt("dve", rb, rb, nlam, ALU.mult, [kb_, knl], [kb_])
                    tk0, t0 = t0_r.next()
                    act(t0, r0_[:, 0:128], AF.Copy, [ok0, ka], [tk0], scale=ra)
                    obk, ob = ob_r.next()
                    stt(ob, r1_[:, 0:128], rb, t0, ALU.mult, ALU.add, [ok1, kb_, tk0], [obk])
                    jk, jb = jb_r.next()
                    krs, rs = rms_scale(ob, obk, 128, jb, jk)
                    stt(Ysb[:, qb, h * 128:(h + 1) * 128], ob, rs, gs, ALU.mult, ALU.mult, [obk, krs, "gs"], ["Ysb"], waw=False)
                else:
                    rg = regs[rsel[qi]]
                    ok0 = okeys[rsel[qi]]
                    ka, ra = small()
                    recip(ra, rg[:, 64:65], [ok0], [ka])
                    act(Ysb[:, qb, h * 64:(h + 1) * 64], rg[:, 0:64], AF.Copy, [ok0, ka], ["Ysb"], scale=ra, waw=False)

        gpar = 0
        nw = 2 if isB else 1
        for h in range(nh):
            qk, qt = q_r.next()
            kk, kt = k_r.next()
            loads = mk_loads(h, qk, qt, kk, kt)
            for g in range(NG):
                if isB:
                    rsel = list(range(8))
                else:
                    rsel = [(gpar % 2) * 4 + i for i in range(4)]
                    gpar += 1
                started = set()
                for j in range(4 * g + 4):
                    bufs = {"w": [], "started": started}
                    if isB:
                        bufs["e"] = e_r.next()
                    for w in range(nw):
                        sk_, sp_ = ps_s.next()
                        tk, pt = pt_r.next()
                        if isB:
                            mk, pm = pm_r.next()
                        else:
                            mk, pm = tk, pt
                        bufs["w"].append((sk_, sp_, tk, pt, mk, pm))
                    steps.append((mk_front(h, g, j, qk, qt, kk, kt, bufs, loads), mk_back(h, g, j, bufs, rsel)))
                    loads = None
        run_pipeline(steps)
        dma(XL(yname).ap().rearrange("(t p) c -> p t c", p=128), Ysb, ["Ysb"], [yname])
        allgather(yname)

    def phase3a(l, xsrc, xsrc_key):
        P.barrier()
        A.reset()
        wg = A.alloc([8, 3 * D], BF16)
        wbr = [A.alloc([4, D], BF16) for _ in range(3)]
        wo = A.alloc([8, D], BF16)
        gm = A.alloc([D], F32)
        bg = A.alloc([24], F32)
        dma(gm, g_mix[l, :, :], [], ["gm"])
        dma(bg, b_gate[l, :, :], [], ["bg"])
        wload(wg, "wg", w_gate[l, :, :], 8)
        for i, wsrc in enumerate((w_bra, w_brb, w_brc)):
            wload(wbr[i], "wbr%d" % i, wsrc[l, :, :], 4)
        wload(wo, "wo", w_o[l, :, :], 8)
        rot = std_rots()
        xt_r = Rot("xt3", [A.alloc([D], F32) for _ in range(8)])
        hT_r = Rot("hT3", [A.alloc([8, 512], BF16)])
        yT_r = [Rot("yT%d" % i, [A.alloc([4, 512], BF16)]) for i in range(3)]
        yl_r = Rot("yl", [A.alloc([512], BF16) for _ in range(3)])
        ysg = [A.alloc([4, 512], BF16) for _ in range(2)]
        ylg = [A.alloc([512], BF16) for _ in range(2)]
        mT_r = Rot("mT", [A.alloc([8, 512], BF16)])
        gt_r = Rot("gt", [A.alloc([512], F32) for _ in range(3)])
        m_r = Rot("m", [A.alloc([512], F32) for _ in range(2)])
        t_r = Rot("t", [A.alloc([512], F32) for _ in range(2)])
        xo_r = Rot("xo", [A.alloc([D], F32) for _ in range(2)])
        mm_r = Rot("psm3", [psb[i][:] for i in range(6)], P, list(range(6)))
        for g in range(NGL):
            tok = slice(g * 512, (g + 1) * 512)
            hk, hT = hT_r.next()
            xts = []
            for t in range(4):
                xk, xt = xt_r.next()
                r0 = g * 512 + t * 128
                dma(xt, xsrc[r0:r0 + 128, :], [xsrc_key], [xk])
                norm_T(xt, xk, 8, D, gm, "gm", hT[:, :, t * 128:(t + 1) * 128], hk, rot)
                xts.append((xk, xt))
            yks = []
            yk, yT = yT_r[0].next()
            for c in range(2):
                dma(ysg[c], XG("YTA").ap().rearrange("(c p) s -> p c s", p=128)[:, :, c * SL + g * 512:c * SL + (g + 1) * 512], ["YTAg"], ["ysg%d" % c])
            sel(yT.rearrange("p c s -> p (c s)"), yk, ysg[0].rearrange("p c s -> p (c s)"), "ysg0", ysg[1].rearrange("p c s -> p (c s)"), "ysg1")
            yks.append((yk, yT))
            for bi, yname in enumerate(("YB", "YC")):
                yk, yT = yT_r[1 + bi].next()
                for t in range(4):
                    lk, yl = yl_r.next()
                    r0 = g * 512 + t * 128
                    for c in range(2):
                        for r in range(2):
                            t0_ = r * S + c * SL + r0
                            dma(ylg[c][:, r * 256:(r + 1) * 256], XG(yname)[t0_:t0_ + 128, :], [yname + "g"], ["ylg%d" % c], waw=False)
                    sel(yl, lk, ylg[0], "ylg0", ylg[1], "ylg1")
                    pk, pp = rot["pT"].next()
                    ppb = pp.bitcast(BF16)
                    for c in range(4):
                        tr(ppb[:, c * 128:(c + 1) * 128], yl[:, c * 128:(c + 1) * 128], [lk], [pk])
                    tcopy("dve", yT[:, :, t * 128:(t + 1) * 128], ppb[:, 0:512].rearrange("p (c t) -> p c t", t=128), [pk], [yk], waw=False)
                yks.append((yk, yT))
            mk, mT = mT_r.next()
            for fc in range(8):
                mkk, m = m_r.next()
                for br in range(3):
                    yk, yT = yks[br]
                    pgk, pg = mm_r.next()
                    col = br * D + fc * 128
                    for k in range(8):
                        mm(pg, wg[:, k, col:col + 128], hT[:, k, :], k == 0, k == 7, ["wg", hk], [pgk])
                    pbk, pb = mm_r.next()
                    for k in range(4):
                        mm(pb, wbr[br][:, k, fc * 128:(fc + 1) * 128], yT[:, k, :], k == 0, k == 3, ["wbr%d" % br, yk], [pbk])
                    gk, gt = gt_r.next()
                    act(gt, pg, AF.Sigmoid, [pgk, "bg"], [gk], bias=bg[:, br * 8 + fc:br * 8 + fc + 1])
                    if br == 0:
                        tt("dve", m, gt, pb, ALU.mult, [gk, pbk], [mkk])
                    else:
                        tk, tt_ = t_r.next()
                        tt("dve", tt_, gt, pb, ALU.mult, [gk, pbk], [tk])
                        if br == 1:
                            tt("pool", m, m, tt_, ALU.add, [mkk, tk], [mkk])
                        else:
                            tt("pool", mT[:, fc, :], m, tt_, ALU.add, [mkk, tk], [mk], waw=False)
            for t in range(4):
                xk, xt = xts[t]
                ok_, xo = xo_r.next()
                r0 = g * 512 + t * 128
                for cc in range(2):
                    pk, pp = mm_r.next()
                    for k in range(8):
                        mm(pp, mT[:, k, t * 128:(t + 1) * 128], wo[:, k, cc * 512:(cc + 1) * 512], k == 0, k == 7, ["wo", mk], [pk])
                    tt("dve", xo[:, cc * 512:(cc + 1) * 512], pp, xt[:, cc * 512:(cc + 1) * 512], ALU.add, [pk, xk], [ok_], waw=False)
                dma(XMID[r0:r0 + 128, :], xo, [ok_], ["XMID"], waw=False)

    def phase3b(l, last):
        P.barrier()
        A.reset()
        GT = 256
        wfg = A.alloc([8, FH], BF16)
        wfu = A.alloc([8, FH], BF16)
        wfd = A.alloc([22, D], BF16)
        gf = A.alloc([D], F32)
        dma(gf, g_ffn[l, :, :], [], ["gf"])
        gfin = None
        if last:
            gfin = A.alloc([D], F32)
            dma(gfin, g_fin[:, :], [], ["gfin"])
        wload(wfg, "wfg", w_fg[l, :, :], 8)
        wload(wfu, "wfu", w_fu[l, :, :], 8)
        wload(wfd, "wfd", w_fd[l, :, :], 22)
        rot = std_rots()
        xt_r = Rot("xt4", [A.alloc([D], F32) for _ in range(4)])
        hT_r = Rot("hT4", [A.alloc([8, GT], BF16)])
        aT_r = Rot("aT", [A.alloc([22, GT], BF16)])
        sg_r = Rot("sg", [A.alloc([GT], F32) for _ in range(2)])
        xo_r = Rot("xo4", [A.alloc([D], F32) for _ in range(2)])
        fo_r = Rot("fo4", [A.alloc([D], F32) for _ in range(2)])
        mm_r = Rot("psm4", [psb[i][:] for i in range(6)], P, list(range(6)))
        nt = GT // 128
        for g in range(SL // GT):
            hk, hT = hT_r.next()
            xts = []
            for t in range(nt):
                xk, xt = xt_r.next()
                r0 = g * GT + t * 128
                dma(xt, XMID[r0:r0 + 128, :], ["XMID"], [xk])
                norm_T(xt, xk, 8, D, gf, "gf", hT[:, :, t * 128:(t + 1) * 128], hk, rot)
                xts.append((xk, xt))
            ak, aT = aT_r.next()
            for hc in range(22):
                pgk, pg = mm_r.next()
                for k in range(8):
                    mm(pg[:, 0:GT], wfg[:, k, hc * 128:(hc + 1) * 128], hT[:, k, :], k == 0, k == 7, ["wfg", hk], [pgk])
                puk, pu = mm_r.next()
                for k in range(8):
                    mm(pu[:, 0:GT], wfu[:, k, hc * 128:(hc + 1) * 128], hT[:, k, :], k == 0, k == 7, ["wfu", hk], [puk])
                sk, sg = sg_r.next()
                act(sg, pg[:, 0:GT], AF.Silu, [pgk], [sk])
                tt("dve", aT[:, hc, :], sg, pu[:, 0:GT], ALU.mult, [sk, puk], [ak], waw=False)
            for t in range(nt):
                xk, xt = xts[t]
                ok_, xo = xo_r.next()
                r0 = g * GT + t * 128
                for cc in range(2):
                    pk, pp = mm_r.next()
                    for hc in range(22):
                        mm(pp, aT[:, hc, t * 128:(t + 1) * 128], wfd[:, hc, cc * 512:(cc + 1) * 512], hc == 0, hc == 21, ["wfd", ak], [pk])
                    tt("dve", xo[:, cc * 512:(cc + 1) * 512], pp, xt[:, cc * 512:(cc + 1) * 512], ALU.add, [pk, xk], [ok_], waw=False)
                if not last:
                    dma(XRES[r0:r0 + 128, :], xo, [ok_], ["XRES"], waw=False)
                else:
                    jk, junk = rot["junk"].next()
                    kr, rs = rms_scale(xo, ok_, D, junk, jk)
                    fk, fo = fo_r.next()
                    stt(fo, xo, rs, gfin, ALU.mult, ALU.mult, [ok_, kr, "gfin"], [fk])
                    dma(out_d[r0:r0 + 128, :], fo, [fk], ["out"], waw=False)

    phases = build.phases
    if "0" in phases:
        phase0()
    for l in range(L):
        xsrc, xkey = (x_in, "x") if l == 0 else (XRES, "XRES")
        if "1" in phases:
            phase1(l, xsrc, xkey)
        if "a" in phases:
            phase2a(l)
        if "b" in phases:
            phase2bc(l, "B")
        if "c" in phases:
            phase2bc(l, "C")
        if "3" in phases:
            phase3a(l, xsrc, xkey)
        if "4" in phases:
            phase3b(l, l == L - 1)
    finals = [("dma", "out")]
    if dbg:
        finals += [("dma", n) for n in ("XMID", "XRES")]
    P.emit(final_wait_streams=finals)
    st.close()
    return nc


build.phases = "01abc34"


def host_inputs(S, L, x_loc, p, rank):
    f = np.float32
    SL = S // 2
    LB = ((S + 127 + 383) // 384) * 384
    rep = lambda v: np.ascontiguousarray(np.broadcast_to(v[:, None, :], (v.shape[0], 128, v.shape[1])).astype(f))
    w_in = np.ascontiguousarray(p["w_in"][:L])
    kr = w_in[:, :, 4096:4128]
    w_krs = np.ascontiguousarray(np.concatenate([kr[:, :, 16:32], kr[:, :, 0:16]], axis=-1))
    wuq = p["w_uq"][:L].reshape(L, 768, 8, 96)
    w_uqp = np.ascontiguousarray(np.concatenate([wuq[..., 64:96], wuq[..., 0:64]], axis=-1).reshape(L, 768, 768))
    w_uqs = np.ascontiguousarray(np.concatenate([wuq[..., 80:96], wuq[..., 64:80]], axis=-1).reshape(L, 768, 256))
    wukv = p["w_ukv"][:L].reshape(L, 256, 8, 128)
    w_ukvp = np.ascontiguousarray(np.concatenate([wukv[..., 0:64].reshape(L, 256, 512), wukv[..., 64:128].reshape(L, 256, 512)], axis=-1))
    lam_v = np.concatenate([p["lambda_q1"][:L], p["lambda_k1"][:L], p["lambda_q2"][:L], p["lambda_k2"][:L]], axis=-1)
    bgt = np.ascontiguousarray(p["b_gate"][:L].reshape(L, 24, 128).transpose(0, 2, 1))
    tab = p["rel_bias_table"].astype(f)
    tabaug = np.concatenate([tab, np.full((1, 12), -30000.0, f)], axis=0)
    cols = [4 * rank + i for i in range(4)] + [8 + 2 * rank + i for i in range(2)]
    tabrep = np.ascontiguousarray(np.broadcast_to(tabaug.T[cols][:, :, None], (6, 33, 128)).astype(f))
    dist = np.arange(LB) - 127
    bk = np.where(dist >= 0, t5_bucket_np(dist), 32)
    oh_b = np.zeros((33, LB), f)
    oh_b[bk, np.arange(LB)] = 1.0
    oh_a = np.zeros((33, 3 * LA), f)
    for pi, (win_, dil) in enumerate(PATS):
        step = np.arange(LA) - 127
        ok = (step >= 0) & (step <= 128)
        b_ = np.where(ok, t5_bucket_np(step * dil), 32)
        oh_a[b_, pi * LA + np.arange(LA)] = 1.0
    pos = np.arange(rank * SL, (rank + 1) * SL).astype(f)
    inv = (np.float32(10000.0) ** (-np.arange(0, 32, 2, dtype=f) / np.float32(32))).astype(f)
    ang = (pos[:, None] * inv[None, :]).astype(f)
    cos, sin = np.cos(ang).astype(f).T, np.sin(ang).astype(f).T
    cos32 = np.ascontiguousarray(np.concatenate([cos, cos], axis=0))
    sin32 = np.ascontiguousarray(np.concatenate([-sin, sin], axis=0))
    kk = np.arange(128)
    m = {
        "x": np.ascontiguousarray(x_loc.astype(f)),
        "w_in": w_in, "w_krs": w_krs, "w_uqp": w_uqp, "w_uqs": w_uqs, "w_ukvp": w_ukvp,
        "w_gate": np.ascontiguousarray(p["w_gate"][:L]),
        "w_br_a": np.ascontiguousarray(p["w_br_a"][:L]), "w_br_b": np.ascontiguousarray(p["w_br_b"][:L]),
        "w_br_c": np.ascontiguousarray(p["w_br_c"][:L]), "w_o": np.ascontiguousarray(p["w_o"][:L]),
        "w_ffn_gate": np.ascontiguousarray(p["w_ffn_gate"][:L]), "w_ffn_up": np.ascontiguousarray(p["w_ffn_up"][:L]),
        "w_ffn_down": np.ascontiguousarray(p["w_ffn_down"][:L]),
        "g_mix": rep(p["ln_mix_g"][:L]), "g_q": rep(p["mla_q_norm_g"][:L]), "g_kv": rep(p["mla_kv_norm_g"][:L]),
        "g_ffn": rep(p["ln_ffn_g"][:L]), "g_fin": np.ascontiguousarray(np.broadcast_to(p["final_norm_g"][None, :], (128, D)).astype(f)),
        "g_sub": rep(p["diff_subln_g"][:L]), "lam_v": rep(lam_v), "b_gate": bgt.astype(f),
        "tabrep": tabrep, "oh_b": oh_b, "oh_a": oh_a, "cos32": cos32, "sin32": sin32,
        "ident": np.eye(128, dtype=f), "maskc": (kk[None, :] >= kk[:, None]).astype(f),
        "msel": np.ascontiguousarray(np.broadcast_to(np.eye(2, dtype=f)[rank][None, :], (128, 2))),
    }
    return m


_NC_CACHE = {}


def kernel(**inputs):
    p = {k: np.asarray(v) for k, v in inputs.items()}
    x = p["x"]
    B, S, _ = x.shape
    SL = S // 2
    L = p["w_in"].shape[0]
    key = (S, L)
    if key not in _NC_CACHE:
        _NC_CACHE[key] = build(S, L, ncores=2 * B)
    nc = _NC_CACHE[key]
    shared = [host_inputs(S, L, x[0, r * SL:(r + 1) * SL], p, r) for r in range(2)]
    in_maps = []
    for c in range(2 * B):
        b, r = c // 2, c % 2
        m = dict(shared[r])
        m["x"] = np.ascontiguousarray(x[b, r * SL:(r + 1) * SL].astype(np.float32))
        in_maps.append(m)
    res = run_bass_kernel_spmd(nc, in_maps, core_ids=list(range(2 * B)))
    out = np.empty((B, S, D), np.float32)
    for c in range(2 * B):
        b, r = c // 2, c % 2
        out[b, r * SL:(r + 1) * SL] = np.asarray(res.results[c]["out"])
    return out
```

```python
import math
import numpy as np
from contextlib import ExitStack
import concourse.bass as bass
import concourse.mybir as mybir
from concourse.bass_utils import run_bass_kernel_spmd

F32 = mybir.dt.float32
BF16 = mybir.dt.bfloat16
AF = mybir.ActivationFunctionType
ALU = mybir.AluOpType

D = 1024
DIN = 4128
FH = 2816
LA = 384
EPS = 1e-6
PATS = ((128, 1), (512, 4), (2048, 16))


class _Sem:
    def __init__(self, name):
        self.name = name
        self.h = None


class _Op:
    __slots__ = ("eng", "fn", "deps", "dma", "stream", "needed", "ev", "cc")

    def __init__(self, eng, fn, dma, stream):
        self.cc = False
        self.eng = eng
        self.fn = fn
        self.dma = dma
        self.stream = stream
        self.deps = set()
        self.needed = False
        self.ev = None


class Prog:
    def __init__(self, nc):
        self.nc = nc
        self.ops = []
        self.lastw = {}
        self.readers = {}
        self.sems = []
        self.last_of_stream = {}
        self.bar = None
        self.bar_done = set()
        self.bar_positions = []
        self.bank_of = {}
        self.bank_last = {}

    def barrier(self):
        self.bar = {k: v for k, v in self.last_of_stream.items() if not (isinstance(k, tuple) and k[0] == "cc")}
        self.bar_done = set()
        self.bar_positions.append(len(self.ops))

    def op(self, eng, fn, reads=(), writes=(), dma=False, waw=True, cc=False):
        i = len(self.ops)
        stream = ("dma", writes[0]) if dma else eng
        if cc:
            stream = ("cc", writes[0])
        o = _Op(eng, fn, dma, stream)
        o.cc = cc
        if self.bar is not None and eng not in self.bar_done:
            self.bar_done.add(eng)
            for s, j in self.bar.items():
                if s == eng and not dma and eng == "pe":
                    continue
                o.deps.add(j)
        for r in reads:
            w = self.lastw.get(r)
            if w is not None:
                self._dep(o, w, "raw")
        for r in writes:
            w = self.lastw.get(r)
            if w is not None and waw:
                self._dep(o, w, "waw")
            for x in self.readers.get(r, {}).values():
                self._dep(o, x, "war")
        for r in reads:
            self.readers.setdefault(r, {})[stream] = i
        for r in writes:
            self.lastw[r] = i
            self.readers[r] = {}
        if not dma:
            banks = set()
            for r in list(reads) + list(writes):
                b = self.bank_of.get(r)
                if b is not None:
                    banks.add(b)
            for b in banks:
                bl = self.bank_last.setdefault(b, {})
                for f_eng, j in bl.items():
                    if f_eng != eng:
                        o.deps.add(j)
                bl[eng] = i
        self.last_of_stream[stream] = i
        self.ops.append(o)
        return i

    def _dep(self, o, j, kind):
        p = self.ops[j]
        if not p.dma and not o.dma and p.eng == o.eng:
            if o.eng == "pe":
                return
            if kind == "war":
                return
        o.deps.add(j)

    def emit(self, final_wait_streams=()):
        nc = self.nc
        ops = self.ops
        MAXV = 8000
        for o in ops:
            for j in o.deps:
                ops[j].needed = True
        bars = sorted(set(self.bar_positions))
        seg_of = []
        bi = 0
        for i in range(len(ops)):
            while bi < len(bars) and bars[bi] <= i:
                bi += 1
            seg_of.append(bi)
        cnt = {}
        for i, o in enumerate(ops):
            if o.dma and not o.cc:
                k = (seg_of[i], o.stream)
                cnt[k] = cnt.get(k, 0) + 1
        free = []
        ccmap = {}
        dmap = {}
        emap = {}
        last_ev = {}

        def new_phys():
            s_ = _Sem("s%d" % len(self.sems))
            self.sems.append(s_)
            return [s_, 0]

        cur_seg = 0
        for i, o in enumerate(ops):
            if seg_of[i] != cur_seg:
                cur_seg = seg_of[i]
                for ph in dmap.values():
                    free.append(ph)
                dmap = {}
            if o.cc:
                ph = ccmap.get(o.stream)
                if ph is None:
                    ph = new_phys()
                    ccmap[o.stream] = ph
                ph[1] += 1
                o.ev = (ph[0], ph[1])
                last_ev[o.stream] = o.ev
            elif o.dma:
                ph = dmap.get(o.stream)
                if ph is None:
                    need = 16 * cnt[(cur_seg, o.stream)]
                    for fi, cand in enumerate(free):
                        if cand[1] + need <= MAXV:
                            ph = free.pop(fi)
                            break
                    if ph is None:
                        ph = new_phys()
                    dmap[o.stream] = ph
                ph[1] += 16
                o.ev = (ph[0], ph[1])
                last_ev[o.stream] = o.ev
            elif o.needed:
                ph = emap.get(o.stream)
                if ph is None or ph[1] + 1 > MAXV:
                    ph = new_phys()
                    emap[o.stream] = ph
                ph[1] += 1
                o.ev = (ph[0], ph[1])
        per_eng = {}
        for o in ops:
            per_eng.setdefault(o.eng, []).append(o)
        finals = [last_ev[s] for s in final_wait_streams if s in last_ev]
        self.nsem = len(self.sems)
        with ExitStack() as st:
            for s in self.sems:
                s.h = st.enter_context(nc.semaphore(s.name))
            block = st.enter_context(nc.Block())

            def run(eng_name, e):
                seen = {}
                for o in per_eng.get(eng_name, []):
                    need = {}
                    for j in o.deps:
                        s, v = ops[j].ev
                        if need.get(s, 0) < v:
                            need[s] = v
                    for s, v in need.items():
                        if seen.get(s, 0) < v:
                            e.wait_ge(s.h, v)
                            seen[s] = v
                    ins = o.fn(e)
                    if o.cc:
                        ins.then_inc(o.ev[0].h)
                    elif o.ev is not None:
                        ins.then_inc(o.ev[0].h, 16 if o.dma else 1)
                if eng_name == "sp":
                    for s, v in finals:
                        e.wait_ge(s.h, v)

            @block.tensor
            def _(e):
                run("pe", e)

            @block.scalar
            def _(e):
                run("act", e)

            @block.vector
            def _(e):
                run("dve", e)

            @block.gpsimd
            def _(e):
                run("pool", e)

            @block.sync
            def _(e):
                run("sp", e)


class Arena:
    def __init__(self, base, nbytes):
        self.base = base
        self.nbytes = nbytes
        self.off = 0

    def reset(self):
        self.off = 0

    def alloc(self, shape, dt):
        n = 1
        for s in shape:
            n *= s
        nb = n * (4 if dt == F32 else 2)
        nb = (nb + 63) // 64 * 64
        assert self.off + nb <= self.nbytes, (self.off, nb, self.nbytes)
        v = self.base[:, self.off // 2:(self.off + nb) // 2]
        self.off += nb
        if dt == F32:
            v = v.bitcast(F32)
        v = v[:, 0:n]
        if len(shape) == 2:
            v = v.rearrange("p (a b) -> p a b", b=shape[1])
        elif len(shape) == 3:
            v = v.rearrange("p (a b c) -> p a b c", b=shape[1], c=shape[2])
        return v


class Rot:
    def __init__(self, name, aps, P=None, banks=None):
        self.name = name
        self.aps = aps
        self.i = 0
        if banks is not None:
            for k, b in enumerate(banks):
                P.bank_of["%s#%d" % (name, k)] = b

    def next(self):
        k = self.i % len(self.aps)
        self.i += 1
        return "%s#%d" % (self.name, k), self.aps[k]


def t5_bucket_np(dist):
    dist = np.maximum(dist, 0)
    exact = 16
    lr = np.log(np.maximum(dist, 1).astype(np.float32) / np.float32(exact)) / np.float32(math.log(2048 / exact))
    large = np.minimum(exact + (lr.astype(np.float32) * np.float32(16)).astype(np.int32), 31)
    return np.where(dist < exact, dist, large)


def t5_bucket_exact(dist):
    return t5_bucket_np(np.asarray(dist))


FM_QA, FM_KA, FM_QB, FM_KB, FM_QC, FM_KC, FM_ROWS = 0, 512, 1024, 1536, 2048, 2816, 3584
TM_VA, TM_VB, TM_VC, TM_COLS = 0, 520, 1036, 1556


def build(S, L, dbg=False, ncores=8):
    NT = S // 128
    NG = S // 512
    SL = S // 2
    NGL = SL // 512
    groups = [[2 * i, 2 * i + 1] for i in range(ncores // 2)]
    LB = ((S + 127 + 383) // 384) * 384
    nc = bass.Bass("TRN2", target_bir_lowering=False)
    P = Prog(nc)

    def din(name, shape, dt=F32):
        return nc.dram_tensor(name, list(shape), dt, kind="ExternalInput")

    def dscr(name, shape, dt):
        return nc.dram_tensor(name, list(shape), dt, kind="ExternalOutput" if dbg else "Internal")

    x_in = din("x", [SL, D])
    w_in = din("w_in", [L, D, DIN])
    w_krs = din("w_krs", [L, D, 32])
    w_uqp = din("w_uqp", [L, 768, 768])
    w_uqs = din("w_uqs", [L, 768, 256])
    w_ukvp = din("w_ukvp", [L, 256, 1024])
    w_gate = din("w_gate", [L, D, 3 * D])
    w_bra = din("w_br_a", [L, 512, D])
    w_brb = din("w_br_b", [L, 512, D])
    w_brc = din("w_br_c", [L, 512, D])
    w_o = din("w_o", [L, D, D])
    w_fg = din("w_ffn_gate", [L, D, FH])
    w_fu = din("w_ffn_up", [L, D, FH])
    w_fd = din("w_ffn_down", [L, FH, D])
    g_mix = din("g_mix", [L, 128, D])
    g_q = din("g_q", [L, 128, 768])
    g_kv = din("g_kv", [L, 128, 256])
    g_ffn = din("g_ffn", [L, 128, D])
    g_fin = din("g_fin", [128, D])
    g_sub = din("g_sub", [L, 128, 128])
    lam_v = din("lam_v", [L, 128, 256])
    b_gate = din("b_gate", [L, 128, 24])
    tabrep = din("tabrep", [6, 33, 128])
    oh_b = din("oh_b", [33, LB])
    oh_a = din("oh_a", [33, 3 * LA])
    cos_d = din("cos32", [32, SL])
    sin_d = din("sin32", [32, SL])
    ident_d = din("ident", [128, 128])
    maskc_d = din("maskc", [128, 128])
    msel_d = din("msel", [128, 2])
    out_d = nc.dram_tensor("out", [SL, D], F32, kind="ExternalOutput")

    def dint(name, shape, dt):
        return nc.dram_tensor(name, list(shape), dt)

    XT = {}

    def xbuf(name, rows, cols):
        XT[name] = (dint(name, [rows, cols], BF16), dint(name + "g", [2 * rows, cols], BF16), rows)

    for n_ in ("QA", "KA", "QB", "KB"):
        xbuf(n_, 512, SL)
    for n_ in ("QC0", "QC1", "KC0", "KC1"):
        xbuf(n_, 384, SL)
    for n_ in ("VA0", "VA1", "VC0", "VC1"):
        xbuf(n_, SL, 260)
    for n_ in ("VB0", "VB1"):
        xbuf(n_, SL, 258)
    xbuf("YTA", 256, S)
    xbuf("YB", S, 256)
    xbuf("YC", S, 256)

    def XL(name):
        return XT[name][0]

    def XG(name):
        return XT[name][1]

    XMID = dscr("XMID", [SL, D], F32)
    XRES = dscr("XRES", [SL, D], F32)
    FB_D = dint("FB_D", [2, 128, LB], BF16)
    FA_D = dint("FA_D", [4, 3, 128, LA], BF16)

    st = ExitStack()
    ARENA_B = 206 * 1024
    arena_t = st.enter_context(nc.sbuf_tensor("arena", [128, ARENA_B // 2], BF16))
    A = Arena(arena_t, ARENA_B)
    idb_t = st.enter_context(nc.sbuf_tensor("idb", [128, 128], BF16))
    mkc_t = st.enter_context(nc.sbuf_tensor("mkc", [128, 128], BF16))
    onesf_t = st.enter_context(nc.sbuf_tensor("onesf", [128, 64], F32))
    sm_t = st.enter_context(nc.sbuf_tensor("smalls", [128, 64], F32))
    msel_t = st.enter_context(nc.sbuf_tensor("msel_sb", [128, 2], F32))
    msel = msel_t[:]
    idb = idb_t[:]
    mkc = mkc_t[:]
    onesf = onesf_t[:]
    psb = [st.enter_context(nc.psum_tensor("ps%d" % i, [128, 512], F32)) for i in range(8)]

    sm_i = [0]

    def small():
        k = sm_i[0] % 64
        sm_i[0] += 1
        return "sm#%d" % k, sm_t[:, k:k + 1]

    def dma(out, in_, reads, writes, eng="sp", waw=True):
        P.op(eng, lambda e: e.dma_start(out=out, in_=in_), reads=reads, writes=writes, dma=True, waw=waw)

    def sel(dst, dkey, c0, k0, c1, k1, np_=128, waw=True):
        act(c1, c1, AF.Copy, [k1, "msel"], [k1], scale=msel[0:np_, 1:2])
        stt(dst, c0, msel[0:np_, 0:1], c1, ALU.mult, ALU.add, [k0, k1, "msel"], [dkey], waw=waw)

    def allgather(name):
        src, dst = XL(name), XG(name)
        P.op("pool", lambda e: e.collective_compute("AllGather", ALU.bypass, replica_groups=groups, ins=[src.ap().opt()], outs=[dst.ap().opt()]),
             reads=[name], writes=[name + "g"], cc=True)

    def mm(out, lhsT, rhs, start, stop, reads, writes, skip=False):
        if skip:
            P.op("pe", lambda e: e.matmul(out, lhsT=lhsT, rhs=rhs, start=start, stop=stop, skip_group_check=True), reads=reads, writes=writes)
        else:
            P.op("pe", lambda e: e.matmul(out, lhsT=lhsT, rhs=rhs, start=start, stop=stop), reads=reads, writes=writes)

    def tr(out, in_, reads, writes):
        P.op("pe", lambda e: e.transpose(out=out, in_=in_, identity=idb), reads=list(reads) + ["idb"], writes=writes)

    def act(out, in_, func, reads, writes, scale=None, bias=None, accum=None, waw=True):
        kw = {}
        if scale is not None:
            kw["scale"] = scale
        if bias is not None:
            kw["bias"] = bias
        if accum is not None:
            kw["accum_out"] = accum
        P.op("act", lambda e: e.activation(out=out, in_=in_, func=func, **kw), reads=reads, writes=writes, waw=waw)

    def tt(eng, out, in0, in1, op, reads, writes, waw=True):
        P.op(eng, lambda e: e.tensor_tensor(out=out, in0=in0, in1=in1, op=op), reads=reads, writes=writes, waw=waw)

    def stt(out, in0, scalar, in1, op0, op1, reads, writes, waw=True):
        P.op("dve", lambda e: e.scalar_tensor_tensor(out=out, in0=in0, scalar=scalar, in1=in1, op0=op0, op1=op1), reads=reads, writes=writes, waw=waw)

    def tcopy(eng, out, in_, reads, writes, waw=True):
        P.op(eng, lambda e: e.tensor_copy(out=out, in_=in_), reads=reads, writes=writes, waw=waw)

    def recip(out, in_, reads, writes):
        P.op("dve", lambda e: e.reciprocal(out=out, in_=in_), reads=reads, writes=writes)

    def tsadd(out, in0, c, reads, writes):
        P.op("dve", lambda e: e.tensor_scalar(out=out, in0=in0, scalar1=c, scalar2=None, op0=ALU.add), reads=reads, writes=writes)

    def memset(eng, ap, v, writes):
        P.op(eng, lambda e: e.memset(ap, v), writes=writes)

    def wload(dst3, dst_key, src2d, nchunk):
        for c in range(nchunk):
            dma(dst3[:, c, :], src2d[c * 128:(c + 1) * 128, :], [], [dst_key], eng="pool", waw=False)

    evac = [0]

    def evac_copy(out, in_, reads, writes, waw=True):
        evac[0] += 1
        if evac[0] % 2:
            act(out, in_, AF.Copy, reads, writes, waw=waw)
        else:
            tcopy("dve", out, in_, reads, writes, waw=waw)

    def rms_scale(src, skey, Dn, junk, jk):
        sk, ss = small()
        rk, rs = small()
        act(junk, src, AF.Square, [skey], [jk, sk], scale=float(Dn) ** -0.5, accum=ss)
        tsadd(ss, ss, EPS, [sk], [sk])
        act(ss, ss, AF.Sqrt, [sk], [sk])
        recip(rs, ss, [sk], [rk])
        return rk, rs

    def phase0():
        A.reset()
        idf = A.alloc([128], F32)
        mkf = A.alloc([128], F32)
        dma(idf, ident_d[:, :], [], ["idf"])
        dma(mkf, maskc_d[:, :], [], ["mkf"])
        dma(msel, msel_d[:, :], [], ["msel"])
        tcopy("dve", idb, idf, ["idf"], ["idb"])
        tcopy("dve", mkc, mkf, ["mkf"], ["mkc"])
        memset("dve", onesf, 1.0, ["onesf"])
        ohb = A.alloc([LB], F32)
        oha = A.alloc([3 * LA], F32)
        dma(ohb[0:33], oh_b[:, :], [], ["ohb"])
        dma(oha[0:33], oh_a[:, :], [], ["oha"])
        tabs = Rot("tab", [A.alloc([128], F32) for _ in range(2)])
        stg = Rot("fstg", [A.alloc([LA], BF16) for _ in range(3)])
        psr = Rot("ps", [psb[i][:] for i in range(4)], P, [0, 1, 2, 3])
        for h in range(6):
            tk, tb = tabs.next()
            dma(tb[0:33], tabrep[h, :, :], [], [tk])
            if h < 4:
                chunks = [(oha, "oha", p * LA, FA_D[h, p, :, :], "FA_D") for p in range(3)]
            else:
                chunks = [(ohb, "ohb", c * LA, FB_D[h - 4, :, c * LA:(c + 1) * LA], "FB_D") for c in range(LB // LA)]
            for (src, skey, off, dst, dname) in chunks:
                pk, pp = psr.next()
                mm(pp[:, 0:LA], tb[0:33], src[0:33, off:off + LA], True, True, [tk, skey], [pk])
                sk, sg = stg.next()
                act(sg, pp[:, 0:LA], AF.Exp, [pk], [sk])
                dma(dst, sg, [sk], [dname], waw=False)

    def norm_T(src, skey, C, Dn, g_ap, gkey, dst3, dkey, rot):
        jk, junk = rot["junk"].next()
        xk, xn = rot["xn"].next()
        W = C * 128
        rk, rs = rms_scale(src, skey, Dn, junk[:, 0:W], jk)
        stt(xn[:, 0:W], src, rs, g_ap, ALU.mult, ALU.mult, [skey, rk, gkey], [xk])
        pk, pp = rot["pT"].next()
        ppb = pp.bitcast(BF16)
        for c in range(C):
            tr(ppb[:, c * 128:(c + 1) * 128], xn[:, c * 128:(c + 1) * 128], [xk], [pk])
        act(dst3, ppb[:, 0:W].rearrange("p (c t) -> p c t", t=128), AF.Copy, [pk], [dkey], waw=False)

    def std_rots():
        return {
            "junk": Rot("junk", [A.alloc([1024], BF16)]),
            "xn": Rot("xn", [A.alloc([1024], BF16) for _ in range(2)]),
            "pT": Rot("psT", [psb[6][:], psb[7][:]], P, [6, 7]),
        }

    def phase1(l, xsrc, xsrc_key):
        P.barrier()
        A.reset()
        win = A.alloc([8, DIN], BF16)
        wkrs = A.alloc([8, 32], BF16)
        wuq = A.alloc([6, 768], BF16)
        wuqs = A.alloc([6, 256], BF16)
        wukv = A.alloc([2, 1024], BF16)
        gm = A.alloc([D], F32)
        gq = A.alloc([768], F32)
        gkv = A.alloc([256], F32)
        dma(gm, g_mix[l, :, :], [], ["gm"])
        dma(gq, g_q[l, :, :], [], ["gq"])
        dma(gkv, g_kv[l, :, :], [], ["gkv"])
        wload(win, "win", w_in[l, :, :], 8)
        wload(wkrs, "wkrs", w_krs[l, :, :], 8)
        wload(wuq, "wuq", w_uqp[l, :, :], 6)
        wload(wuqs, "wuqs", w_uqs[l, :, :], 6)
        wload(wukv, "wukv", w_ukvp[l, :, :], 2)
        rot = std_rots()
        xt_r = Rot("xt", [A.alloc([D], F32) for _ in range(2)])
        hT_r = Rot("hT", [A.alloc([8, 512], BF16) for _ in range(2)])
        cq_r = Rot("cq", [A.alloc([768], F32) for _ in range(2)])
        ckv_r = Rot("ckv", [A.alloc([256], F32) for _ in range(2)])
        cqT_r = Rot("cqT", [A.alloc([6, 512], BF16)])
        ckvT_r = Rot("ckvT", [A.alloc([2, 512], BF16)])
        fst_r = Rot("fst", [A.alloc([512], BF16) for _ in range(4)])
        va_r = Rot("vast", [A.alloc([8, 65], BF16) for _ in range(2)])
        vc_r = Rot("vcst", [A.alloc([8, 65], BF16) for _ in range(2)])
        vb_r = Rot("vbst", [A.alloc([4, 129], BF16) for _ in range(2)])
        cs_r = Rot("cs", [A.alloc([512], F32) for _ in range(2)])
        sn_r = Rot("sn", [A.alloc([512], F32) for _ in range(2)])
        t1_r = Rot("t1", [A.alloc([512], F32) for _ in range(2)])
        t2_r = Rot("t2", [A.alloc([512], F32) for _ in range(2)])
        mm_r = Rot("psm", [psb[i][:] for i in range(6)], P, list(range(6)))
        for r_ in (va_r, vc_r, vb_r):
            for i_, ap_ in enumerate(r_.aps):
                memset("pool", ap_, 1.0, ["%s#%d" % (r_.name, i_)])

        def rope_evac(pm, pmk, psw, pswk, dst, dkey, ck, cs, sk, sn):
            k1, t1 = t1_r.next()
            k2, t2 = t2_r.next()
            tt("dve", t1[0:32], pm[0:32, :], cs[0:32], ALU.mult, [pmk, ck], [k1])
            tt("dve", t2[0:32], psw[0:32, :], sn[0:32], ALU.mult, [pswk, sk], [k2])
            tt("pool", dst, t1[0:32], t2[0:32], ALU.add, [k1, k2], [dkey], waw=False)

        def tokmm(hT, hk, ts_, col, n, pp, pk):
            for k in range(8):
                mm(pp[:, 0:n], hT[:, k, ts_], win[:, k, col:col + n], k == 0, k == 7, ["win", hk], [pk])

        for g in range(NGL):
            tok = slice(g * 512, (g + 1) * 512)
            hk, hT = hT_r.next()
            for t in range(4):
                xk, xt = xt_r.next()
                r0 = g * 512 + t * 128
                dma(xt, xsrc[r0:r0 + 128, :], [xsrc_key], [xk])
                norm_T(xt, xk, 8, D, gm, "gm", hT[:, :, t * 128:(t + 1) * 128], hk, rot)
            ck, cs = cs_r.next()
            sk, sn = sn_r.next()
            dma(cs[0:32], cos_d[:, tok], [], [ck])
            dma(sn[0:32], sin_d[:, tok], [], [sk])
            for (c0, dname) in ((0, "QA"), (512, "KA"), (1536, "QB"), (2048, "KB")):
                for c in range(4):
                    pk, pp = mm_r.next()
                    col = c0 + c * 128
                    for k in range(8):
                        mm(pp, win[:, k, col:col + 128], hT[:, k, :], k == 0, k == 7, ["win", hk], [pk])
                    fk, fs = fst_r.next()
                    evac_copy(fs, pp, [pk], [fk])
                    dma(XL(dname)[c * 128:(c + 1) * 128, tok], fs, [fk], [dname], waw=False)
            pk, pp = mm_r.next()
            for k in range(8):
                mm(pp[0:32, :], win[:, k, 4096:4128], hT[:, k, :], k == 0, k == 7, ["win", hk], [pk])
            pk2, pp2 = mm_r.next()
            for k in range(8):
                mm(pp2[0:32, :], wkrs[:, k, :], hT[:, k, :], k == 0, k == 7, ["wkrs", hk], [pk2])
            fk, fs = fst_r.next()
            rope_evac(pp, pk, pp2, pk2, fs[0:32], fk, ck, cs, sk, sn)
            for h in range(8):
                dma(XL("KC%d" % (h // 4))[(h % 4) * 96:(h % 4) * 96 + 32, tok], fs[0:32], [fk], ["KC%d" % (h // 4)], waw=False)
            cqk, cqT = cqT_r.next()
            ckk, ckvT = ckvT_r.next()
            for t in range(4):
                ts_ = slice(t * 128, (t + 1) * 128)
                r0 = g * 512 + t * 128
                pk, pp = mm_r.next()
                tokmm(hT, hk, ts_, 1024, 512, pp, pk)
                vk, vs = va_r.next()
                evac_copy(vs[:, :, 0:64], pp.rearrange("p (h d) -> p h d", d=64), [pk], [vk], waw=False)
                for c_ in range(2):
                    dma(XL("VA%d" % c_)[r0:r0 + 128, :], vs[:, 4 * c_:4 * c_ + 4, :].rearrange("p h d -> p (h d)"), [vk], ["VA%d" % c_], waw=False)
                pk, pp = mm_r.next()
                tokmm(hT, hk, ts_, 2560, 512, pp, pk)
                vk, vs = vb_r.next()
                evac_copy(vs[:, :, 0:128], pp.rearrange("p (h d) -> p h d", d=128), [pk], [vk], waw=False)
                for c_ in range(2):
                    dma(XL("VB%d" % c_)[r0:r0 + 128, :], vs[:, 2 * c_:2 * c_ + 2, :].rearrange("p h d -> p (h d)"), [vk], ["VB%d" % c_], waw=False)
                qk_, cq = cq_r.next()
                pk, pp = mm_r.next()
                tokmm(hT, hk, ts_, 3072, 512, pp, pk)
                evac_copy(cq[:, 0:512], pp, [pk], [qk_], waw=False)
                pk, pp = mm_r.next()
                tokmm(hT, hk, ts_, 3584, 256, pp, pk)
                evac_copy(cq[:, 512:768], pp[:, 0:256], [pk], [qk_], waw=False)
                norm_T(cq, qk_, 6, 768, gq, "gq", cqT[:, :, ts_], cqk, rot)
                kk_, ckv = ckv_r.next()
                pk, pp = mm_r.next()
                tokmm(hT, hk, ts_, 3840, 256, pp, pk)
                evac_copy(ckv, pp[:, 0:256], [pk], [kk_])
                norm_T(ckv, kk_, 2, 256, gkv, "gkv", ckvT[:, :, ts_], ckk, rot)
                pk, pp = mm_r.next()
                for k in range(2):
                    mm(pp, ckvT[:, k, ts_], wukv[:, k, 512:1024], k == 0, k == 1, ["wukv", ckk], [pk])
                vk, vs = vc_r.next()
                evac_copy(vs[:, :, 0:64], pp.rearrange("p (h d) -> p h d", d=64), [pk], [vk], waw=False)
                for c_ in range(2):
                    dma(XL("VC%d" % c_)[r0:r0 + 128, :], vs[:, 4 * c_:4 * c_ + 4, :].rearrange("p h d -> p (h d)"), [vk], ["VC%d" % c_], waw=False)
            for h in range(8):
                pk, pp = mm_r.next()
                for k in range(6):
                    mm(pp[0:96, :], wuq[:, k, h * 96:(h + 1) * 96], cqT[:, k, :], k == 0, k == 5, ["wuq", cqk], [pk])
                pk2, pp2 = mm_r.next()
                for k in range(6):
                    mm(pp2[0:32, :], wuqs[:, k, h * 32:(h + 1) * 32], cqT[:, k, :], k == 0, k == 5, ["wuqs", cqk], [pk2])
                fk, fs = fst_r.next()
                act(fs[32:64], pp[32:64, :], AF.Copy, [pk], [fk])
                tcopy("dve", fs[64:96], pp[64:96, :], [pk], [fk], waw=False)
                rope_evac(pp, pk, pp2, pk2, fs[0:32], fk, ck, cs, sk, sn)
                dma(XL("QC%d" % (h // 4))[(h % 4) * 96:(h % 4 + 1) * 96, tok], fs[0:96], [fk], ["QC%d" % (h // 4)], waw=False)
            for hp in range(4):
                pk, pp = mm_r.next()
                for k in range(2):
                    mm(pp, wukv[:, k, hp * 128:(hp + 1) * 128], ckvT[:, k, :], k == 0, k == 1, ["wukv", ckk], [pk])
                fk, fs = fst_r.next()
                evac_copy(fs, pp, [pk], [fk])
                for j in range(2):
                    h = hp * 2 + j
                    dma(XL("KC%d" % (h // 4))[(h % 4) * 96 + 32:(h % 4 + 1) * 96, tok], fs[j * 64:(j + 1) * 64], [fk], ["KC%d" % (h // 4)], waw=False)
        for n_ in ("QA", "KA", "VA0", "VA1", "QB", "KB", "VB0", "VB1", "QC0", "QC1", "KC0", "KC1", "VC0", "VC1"):
            allgather(n_)

    SKEW = 2

    def run_pipeline(steps):
        n = len(steps)
        for i in range(n + SKEW):
            if i < n:
                steps[i][0]()
            if i - SKEW >= 0:
                steps[i - SKEW][1]()

    def phase2a(l):
        P.barrier()
        A.reset()
        Vp = [A.alloc([NT, 260], BF16) for _ in range(3)]
        stg = [A.alloc([NT * 260], BF16) for _ in range(2)]
        for pi, (win_, dil) in enumerate(PATS):
            nb = S // dil // 128
            for c in range(2):
                cv = stg[c].rearrange("p (t c) -> p t c", c=260)
                for r in range(dil):
                    src = bass.AP(XG("VA%d" % c), r * 260, [[dil * 260, 128], [dil * 128 * 260, nb], [1, 260]])
                    dma(cv[:, r * nb:(r + 1) * nb, :], src, ["VA%dg" % c], ["stg%d" % c], waw=False)
            sel(Vp[pi].rearrange("p t c -> p (t c)"), "Vp%d" % pi, stg[0], "stg0", stg[1], "stg1")
        q_r = Rot("qa", [A.alloc([S], BF16) for _ in range(2)])
        k_r = Rot("ka", [A.alloc([S], BF16) for _ in range(2)])
        ea_r = Rot("ea", [A.alloc([3, 256], BF16) for _ in range(2)])
        acc_r = Rot("acc", [A.alloc([S], F32) for _ in range(2)])
        pt_r = Rot("pt", [A.alloc([256], BF16) for _ in range(4)])
        pm_r = Rot("pm", [A.alloc([256], BF16) for _ in range(4)])
        rl_r = Rot("rl", [A.alloc([512], F32) for _ in range(2)])
        ys_r = Rot("ys", [A.alloc([512], BF16) for _ in range(2)])
        ps_s = Rot("pss", [psb[i][:] for i in range(3)], P, [0, 1, 2])
        ps_o = Rot("pso", [psb[3 + i][:, 0:128] for i in range(3)], P, [3, 4, 5])
        ps_l = Rot("psl", [psb[6][:], psb[7][:]], P, [6, 7])
        steps = []

        def mk_front(kk, qk, ek, sk_, sp_, kap, qap, nq, tk, pt, mk, pm, eap):
            def f():
                mm(sp_[:, 0:nq], kap, qap, True, True, [kk, qk], [sk_])
                act(pt[:, 0:nq], sp_[:, 0:nq], AF.Exp, [sk_], [tk], scale=0.125)
                tt("dve", pm[:, 0:nq], pt[:, 0:nq], eap, ALU.mult, [tk, ek], [mk])
            return f

        def mk_back(b, nb, mk, pm, vt, vkey, okey, o_, okey_n, o_n, aap, pi, acck, tail):
            def f():
                mm(o_[0:65, :], vt, pm[:, 0:128], b == 0, True, [mk, vkey], [okey], skip=True)
                if b + 1 < nb:
                    mm(o_n[0:65, :], vt, pm[:, 128:256], True, False, [mk, vkey], [okey_n], skip=True)
                if pi == 0:
                    tcopy("dve", aap, o_[0:65, :], [okey], [acck], waw=False)
                else:
                    tt("dve", aap, aap, o_[0:65, :], ALU.add, [okey, acck], [acck], waw=False)
                if tail is not None:
                    tail()
            return f

        def mk_tail(h, acc, acck):
            def f():
                for c in range(NG):
                    cs_ = slice(c * 512, (c + 1) * 512)
                    lk, lp = ps_l.next()
                    mm(lp[0:64, :], onesf[64:65, 0:64], acc[64:65, cs_], True, True, [acck, "onesf"], [lk])
                    rk, rl = rl_r.next()
                    recip(rl[0:64], lp[0:64, :], [lk], [rk])
                    yk, ys = ys_r.next()
                    tt("dve", ys[0:64], acc[0:64, cs_], rl[0:64], ALU.mult, [rk, acck], [yk])
                    dma(XL("YTA")[h * 64:(h + 1) * 64, cs_], ys[0:64], [yk], ["YTA"], waw=False)
            return f

        def mk_loads(h, qk, qt, kk, kt, ek, ea):
            def f():
                for (dst, dk, xn_, so) in ((qt, qk, "QA", 0), (kt, kk, "KA", 4096)):
                    for c in range(2):
                        for r in range(2):
                            b0 = r * 512 + (4 * c + h) * 64
                            dma(stg[c][0:64, so + r * SL:so + (r + 1) * SL], XG(xn_)[b0:b0 + 64, :], [xn_ + "g"], ["stg%d" % c], waw=False)
                    sel(dst[0:64, :], dk, stg[0][0:64, so:so + S], "stg0", stg[1][0:64, so:so + S], "stg1", np_=64)
                for pi in range(3):
                    dma(ea[:, pi, :], bass.AP(FA_D, ((h * 3 + pi) * 128) * LA + 127, [[LA - 1, 128], [1, 256]]), ["FA_D"], [ek], waw=False)
            return f

        for h in range(4):
            qk, qt = q_r.next()
            kk, kt = k_r.next()
            ek, ea = ea_r.next()
            acck, acc = acc_r.next()
            loads = mk_loads(h, qk, qt, kk, kt, ek, ea)
            first = True
            for pi, (win_, dil) in enumerate(PATS):
                nb = S // dil // 128
                vkey = "Vp%d" % pi
                for r in range(dil):
                    okey_n, o_n = None, None
                    for b in range(nb):
                        nq = 256 if b + 1 < nb else 128
                        base = r + dil * 128 * b
                        kap = kt[0:64, base:base + dil * 127 + 1:dil]
                        qap = qt[0:64, base:base + dil * (nq - 1) + 1:dil]
                        sk_, sp_ = ps_s.next()
                        tk, pt = pt_r.next()
                        mk, pm = pm_r.next()
                        fr = mk_front(kk, qk, ek, sk_, sp_, kap, qap, nq, tk, pt, mk, pm, ea[:, pi, 0:nq])
                        if first:
                            fr = (lambda lo, f0: (lambda: (lo(), f0())))(loads, fr)
                            first = False
                        vt = Vp[pi][:, r * nb + b, h * 65:(h + 1) * 65]
                        if b == 0:
                            okey, o_ = ps_o.next()
                        else:
                            okey, o_ = okey_n, o_n
                        if b + 1 < nb:
                            okey_n, o_n = ps_o.next()
                        aap = acc[0:65, base:base + dil * 127 + 1:dil]
                        last = (pi == 2 and r == dil - 1 and b == nb - 1)
                        tail = mk_tail(h, acc, acck) if last else None
                        bk = mk_back(b, nb, mk, pm, vt, vkey, okey, o_, okey_n, o_n, aap, pi, acck, tail)
                        steps.append((fr, bk))
        run_pipeline(steps)
        allgather("YTA")

    def phase2bc(l, which):
        P.barrier()
        A.reset()
        isB = which == "B"
        nh = 2 if isB else 4
        dv = 128 if isB else 64
        dr = 128 if isB else 96
        vw = (dv + 1) * nh
        vpre, yname = ("VB", "YB") if isB else ("VC", "YC")
        scale = 0.125 if isB else 96.0 ** -0.5
        Vt = A.alloc([NT, vw], BF16)
        stg = [A.alloc([NT * 260], BF16) for _ in range(2)]
        for c in range(2):
            dma(stg[c][:, 0:NT * vw].rearrange("p (t c) -> p t c", c=vw), XG("%s%d" % (vpre, c)).ap().rearrange("(t p) c -> p t c", p=128), ["%s%dg" % (vpre, c)], ["stg%d" % c])
        sel(Vt.rearrange("p t c -> p (t c)"), "Vt", stg[0][:, 0:NT * vw], "stg0", stg[1][:, 0:NT * vw], "stg1")
        Ysb = A.alloc([NT, 256], BF16)
        q_r = Rot("qb", [A.alloc([S], BF16) for _ in range(2)])
        k_r = Rot("kb", [A.alloc([S], BF16) for _ in range(2)])
        e_r = Rot("eb", [A.alloc([512], BF16) for _ in range(4)])
        pt_r = Rot("ptb", [A.alloc([512], BF16) for _ in range(4)])
        pm_r = Rot("pmb", [A.alloc([512], BF16) for _ in range(8)])
        ps_s = Rot("pssb", [psb[i][:] for i in range(3)], P, [0, 1, 2])
        gs = nlam = knl = None
        if isB:
            gs = A.alloc([128], F32)
            lv = A.alloc([256], F32)
            lj = A.alloc([64], F32)
            t0_r = Rot("t0", [A.alloc([128], F32) for _ in range(2)])
            ob_r = Rot("ob", [A.alloc([128], F32) for _ in range(2)])
            jb_r = Rot("jb", [A.alloc([128], F32) for _ in range(2)])
            dma(gs, g_sub[l, :, :], [], ["gs"])
            dma(lv, lam_v[l, :, :], [], ["lv"])
            lam_init = 0.8 - 0.6 * math.exp(-0.3 * l)
            P.op("act", lambda e: e.mul(out=gs, in_=gs, mul=float(1.0 - lam_init)), reads=["gs"], writes=["gs"])
            k1, s1 = "lam_s1", A.alloc([1], F32)
            k2, s2 = "lam_s2", A.alloc([1], F32)
            knl, nlam = "nlam", A.alloc([1], F32)
            tt("dve", lj, lv[:, 0:64], lv[:, 64:128], ALU.mult, ["lv"], ["lj"])
            P.op("dve", lambda e: e.reduce_sum(out=s1, in_=lj, axis=mybir.AxisListType.X), reads=["lj"], writes=[k1])
            tt("dve", lj, lv[:, 128:192], lv[:, 192:256], ALU.mult, ["lv", k1], ["lj"])
            P.op("dve", lambda e: e.reduce_sum(out=s2, in_=lj, axis=mybir.AxisListType.X), reads=["lj"], writes=[k2])
            act(s1, s1, AF.Exp, [k1], [k1])
            act(s2, s2, AF.Exp, [k2], [k2])
            stt(nlam, s2, float(-lam_init), s1, ALU.add, ALU.subtract, [k1, k2], [knl])
        if isB:
            regs = [psb[3 + i // 3][:, (i % 3) * 129:(i % 3) * 129 + 129] for i in range(8)]
            okeys = ["oacc%d" % (3 + i // 3) for i in range(8)]
        else:
            regs = [psb[3 + (i // 4)][:, (i % 4) * 65:(i % 4) * 65 + 65] for i in range(8)]
            okeys = ["oacc%d" % (3 + i // 4) for i in range(8)]
        for b_ in (3, 4, 5):
            P.bank_of["oacc%d" % b_] = b_
        steps = []

        def mk_loads(h, qk, qt, kk, kt):
            def f():
                for (dst, dk, qk_sel, so) in ((qt, qk, "Q", 0), (kt, kk, "K", 4096)):
                    for c in range(2):
                        for r in range(2):
                            if isB:
                                xn_, b0 = qk_sel + "B", r * 512 + (2 * c + h) * 128
                            else:
                                xn_, b0 = "%sC%d" % (qk_sel, c), r * 384 + h * 96
                            dma(stg[c][0:dr, so + r * SL:so + (r + 1) * SL], XG(xn_)[b0:b0 + dr, :], [xn_ + "g"], ["stg%d" % c], waw=False)
                    sel(dst[0:dr, :], dk, stg[0][0:dr, so:so + S], "stg0", stg[1][0:dr, so:so + S], "stg1", np_=dr)
            return f

        def mk_front(h, g, j, qk, qt, kk, kt, bufs, loads):
            qlo = max(4 * g, j)
            W = (4 * g + 4 - qlo) * 128

            def f():
                if loads is not None:
                    loads()
                if isB:
                    ek, et = bufs["e"]
                    off = (h * 128) * LB + (qlo - j) * 128 + 127
                    dma(et[:, 0:W], bass.AP(FB_D, off, [[LB - 1, 128], [1, W]]), ["FB_D"], [ek])
                for w, (sk_, sp_, tk, pt, mk, pm) in enumerate(bufs["w"]):
                    rows = slice(w * 64, (w + 1) * 64) if isB else slice(0, 96)
                    mm(sp_[:, 0:W], kt[rows, j * 128:(j + 1) * 128], qt[rows, qlo * 128:(4 * g + 4) * 128], True, True, [kk, qk], [sk_])
                    act(pt[:, 0:W], sp_[:, 0:W], AF.Exp, [sk_], [tk], scale=float(scale))
                    if isB:
                        tt("dve", pm[:, 0:W], pt[:, 0:W], et[:, 0:W], ALU.mult, [tk, ek], [mk])
                    elif qlo == j:
                        tt("dve", pt[:, 0:128], pt[:, 0:128], mkc, ALU.mult, [tk, "mkc"], [tk])
            return f

        def mk_back(h, g, j, bufs, rsel):
            qlo = max(4 * g, j)

            def f():
                for w, (sk_, sp_, tk, pt, mk, pm) in enumerate(bufs["w"]):
                    for qb in range(qlo, 4 * g + 4):
                        ri = rsel[w * 4 + (qb - 4 * g)] if isB else rsel[qb - 4 * g]
                        c0 = (qb - qlo) * 128
                        stf = (j == 0) and (okeys[ri] not in bufs["started"])
                        bufs["started"].add(okeys[ri])
                        mm(regs[ri], pm[:, c0:c0 + 128], Vt[:, j, h * (dv + 1):(h + 1) * (dv + 1)], stf, j == qb, [mk, "Vt"], [okeys[ri]], skip=True)
                if j == 4 * g + 3:
                    epilogue(h, g, rsel)
            return f

        def epilogue(h, g, rsel):
            for qi in range(4):
                qb = 4 * g + qi
                if isB:
                    r0_, r1_ = regs[rsel[qi]], regs[rsel[4 + qi]]
                    ok0, ok1 = okeys[rsel[qi]], okeys[rsel[4 + qi]]
                    ka, ra = small()
                    kb_, rb = small()
                    recip(ra, r0_[:, 128:129], [ok0], [ka])
                    recip(rb, r1_[:, 128:129], [ok1], [kb_])
                    tt("dve", rb, rb, nlam, ALU.mult, [kb_, knl], [kb_])
                    tk0, t0 = t0_r.next()
                    act(t0, r0_[:, 0:128], AF.Copy, [ok0, ka], [tk0], scale=ra)
                    obk, ob = ob_r.next()
                    stt(ob, r1_[:, 0:128], rb, t0, ALU.mult, ALU.add, [ok1, kb_, tk0], [obk])
                    jk, jb = jb_r.next()
                    krs, rs = rms_scale(ob, obk, 128, jb, jk)
                    stt(Ysb[:, qb, h * 128:(h + 1) * 128], ob, rs, gs, ALU.mult, ALU.mult, [obk, krs, "gs"], ["Ysb"], waw=False)
                else:
                    rg = regs[rsel[qi]]
                    ok0 = okeys[rsel[qi]]
                    ka, ra = small()
                    recip(ra, rg[:, 64:65], [ok0], [ka])
                    act(Ysb[:, qb, h * 64:(h + 1) * 64], rg[:, 0:64], AF.Copy, [ok0, ka], ["Ysb"], scale=ra, waw=False)

        gpar = 0
        nw = 2 if isB else 1
        for h in range(nh):
            qk, qt = q_r.next()
            kk, kt = k_r.next()
            loads = mk_loads(h, qk, qt, kk, kt)
            for g in range(NG):
                if isB:
                    rsel = list(range(8))
                else:
                    rsel = [(gpar % 2) * 4 + i for i in range(4)]
                    gpar += 1
                started = set()
                for j in range(4 * g + 4):
                    bufs = {"w": [], "started": started}
                    if isB:
                        bufs["e"] = e_r.next()
                    for w in range(nw):
                        sk_, sp_ = ps_s.next()
                        tk, pt = pt_r.next()
                        if isB:
                            mk, pm = pm_r.next()
                        else:
                            mk, pm = tk, pt
                        bufs["w"].append((sk_, sp_, tk, pt, mk, pm))
                    steps.append((mk_front(h, g, j, qk, qt, kk, kt, bufs, loads), mk_back(h, g, j, bufs, rsel)))
                    loads = None
        run_pipeline(steps)
        dma(XL(yname).ap().rearrange("(t p) c -> p t c", p=128), Ysb, ["Ysb"], [yname])
        allgather(yname)

    def phase3a(l, xsrc, xsrc_key):
        P.barrier()
        A.reset()
        wg = A.alloc([8, 3 * D], BF16)
        wbr = [A.alloc([4, D], BF16) for _ in range(3)]
        wo = A.alloc([8, D], BF16)
        gm = A.alloc([D], F32)
        bg = A.alloc([24], F32)
        dma(gm, g_mix[l, :, :], [], ["gm"])
        dma(bg, b_gate[l, :, :], [], ["bg"])
        wload(wg, "wg", w_gate[l, :, :], 8)
        for i, wsrc in enumerate((w_bra, w_brb, w_brc)):
            wload(wbr[i], "wbr%d" % i, wsrc[l, :, :], 4)
        wload(wo, "wo", w_o[l, :, :], 8)
        rot = std_rots()
        xt_r = Rot("xt3", [A.alloc([D], F32) for _ in range(8)])
        hT_r = Rot("hT3", [A.alloc([8, 512], BF16)])
        yT_r = [Rot("yT%d" % i, [A.alloc([4, 512], BF16)]) for i in range(3)]
        yl_r = Rot("yl", [A.alloc([512], BF16) for _ in range(3)])
        ysg = [A.alloc([4, 512], BF16) for _ in range(2)]
        ylg = [A.alloc([512], BF16) for _ in range(2)]
        mT_r = Rot("mT", [A.alloc([8, 512], BF16)])
        gt_r = Rot("gt", [A.alloc([512], F32) for _ in range(3)])
        m_r = Rot("m", [A.alloc([512], F32) for _ in range(2)])
        t_r = Rot("t", [A.alloc([512], F32) for _ in range(2)])
        xo_r = Rot("xo", [A.alloc([D], F32) for _ in range(2)])
        mm_r = Rot("psm3", [psb[i][:] for i in range(6)], P, list(range(6)))
        for g in range(NGL):
            tok = slice(g * 512, (g + 1) * 512)
            hk, hT = hT_r.next()
            xts = []
            for t in range(4):
                xk, xt = xt_r.next()
                r0 = g * 512 + t * 128
                dma(xt, xsrc[r0:r0 + 128, :], [xsrc_key], [xk])
                norm_T(xt, xk, 8, D, gm, "gm", hT[:, :, t * 128:(t + 1) * 128], hk, rot)
                xts.append((xk, xt))
            yks = []
            yk, yT = yT_r[0].next()
            for c in range(2):
                dma(ysg[c], XG("YTA").ap().rearrange("(c p) s -> p c s", p=128)[:, :, c * SL + g * 512:c * SL + (g + 1) * 512], ["YTAg"], ["ysg%d" % c])
            sel(yT.rearrange("p c s -> p (c s)"), yk, ysg[0].rearrange("p c s -> p (c s)"), "ysg0", ysg[1].rearrange("p c s -> p (c s)"), "ysg1")
            yks.append((yk, yT))
            for bi, yname in enumerate(("YB", "YC")):
                yk, yT = yT_r[1 + bi].next()
                for t in range(4):
                    lk, yl = yl_r.next()
                    r0 = g * 512 + t * 128
                    for c in range(2):
                        for r in range(2):
                            t0_ = r * S + c * SL + r0
                            dma(ylg[c][:, r * 256:(r + 1) * 256], XG(yname)[t0_:t0_ + 128, :], [yname + "g"], ["ylg%d" % c], waw=False)
                    sel(yl, lk, ylg[0], "ylg0", ylg[1], "ylg1")
                    pk, pp = rot["pT"].next()
                    ppb = pp.bitcast(BF16)
                    for c in range(4):
                        tr(ppb[:, c * 128:(c + 1) * 128], yl[:, c * 128:(c + 1) * 128], [lk], [pk])
                    tcopy("dve", yT[:, :, t * 128:(t + 1) * 128], ppb[:, 0:512].rearrange("p (c t) -> p c t", t=128), [pk], [yk], waw=False)
                yks.append((yk, yT))
            mk, mT = mT_r.next()
            for fc in range(8):
                mkk, m = m_r.next()
                for br in range(3):
                    yk, yT = yks[br]
                    pgk, pg = mm_r.next()
                    col = br * D + fc * 128
                    for k in range(8):
                        mm(pg, wg[:, k, col:col + 128], hT[:, k, :], k == 0, k == 7, ["wg", hk], [pgk])
                    pbk, pb = mm_r.next()
                    for k in range(4):
                        mm(pb, wbr[br][:, k, fc * 128:(fc + 1) * 128], yT[:, k, :], k == 0, k == 3, ["wbr%d" % br, yk], [pbk])
                    gk, gt = gt_r.next()
                    act(gt, pg, AF.Sigmoid, [pgk, "bg"], [gk], bias=bg[:, br * 8 + fc:br * 8 + fc + 1])
                    if br == 0:
                        tt("dve", m, gt, pb, ALU.mult, [gk, pbk], [mkk])
                    else:
                        tk, tt_ = t_r.next()
                        tt("dve", tt_, gt, pb, ALU.mult, [gk, pbk], [tk])
                        if br == 1:
                            tt("pool", m, m, tt_, ALU.add, [mkk, tk], [mkk])
                        else:
                            tt("pool", mT[:, fc, :], m, tt_, ALU.add, [mkk, tk], [mk], waw=False)
            for t in range(4):
                xk, xt = xts[t]
                ok_, xo = xo_r.next()
                r0 = g * 512 + t * 128
                for cc in range(2):
                    pk, pp = mm_r.next()
                    for k in range(8):
                        mm(pp, mT[:, k, t * 128:(t + 1) * 128], wo[:, k, cc * 512:(cc + 1) * 512], k == 0, k == 7, ["wo", mk], [pk])
                    tt("dve", xo[:, cc * 512:(cc + 1) * 512], pp, xt[:, cc * 512:(cc + 1) * 512], ALU.add, [pk, xk], [ok_], waw=False)
                dma(XMID[r0:r0 + 128, :], xo, [ok_], ["XMID"], waw=False)

    def phase3b(l, last):
        P.barrier()
        A.reset()
        GT = 256
        wfg = A.alloc([8, FH], BF16)
        wfu = A.alloc([8, FH], BF16)
        wfd = A.alloc([22, D], BF16)
        gf = A.alloc([D], F32)
        dma(gf, g_ffn[l, :, :], [], ["gf"])
        gfin = None
        if last:
            gfin = A.alloc([D], F32)
            dma(gfin, g_fin[:, :], [], ["gfin"])
        wload(wfg, "wfg", w_fg[l, :, :], 8)
        wload(wfu, "wfu", w_fu[l, :, :], 8)
        wload(wfd, "wfd", w_fd[l, :, :], 22)
        rot = std_rots()
        xt_r = Rot("xt4", [A.alloc([D], F32) for _ in range(4)])
        hT_r = Rot("hT4", [A.alloc([8, GT], BF16)])
        aT_r = Rot("aT", [A.alloc([22, GT], BF16)])
        sg_r = Rot("sg", [A.alloc([GT], F32) for _ in range(2)])
        xo_r = Rot("xo4", [A.alloc([D], F32) for _ in range(2)])
        fo_r = Rot("fo4", [A.alloc([D], F32) for _ in range(2)])
        mm_r = Rot("psm4", [psb[i][:] for i in range(6)], P, list(range(6)))
        nt = GT // 128
        for g in range(SL // GT):
            hk, hT = hT_r.next()
            xts = []
            for t in range(nt):
                xk, xt = xt_r.next()
                r0 = g * GT + t * 128
                dma(xt, XMID[r0:r0 + 128, :], ["XMID"], [xk])
                norm_T(xt, xk, 8, D, gf, "gf", hT[:, :, t * 128:(t + 1) * 128], hk, rot)
                xts.append((xk, xt))
            ak, aT = aT_r.next()
            for hc in range(22):
                pgk, pg = mm_r.next()
                for k in range(8):
                    mm(pg[:, 0:GT], wfg[:, k, hc * 128:(hc + 1) * 128], hT[:, k, :], k == 0, k == 7, ["wfg", hk], [pgk])
                puk, pu = mm_r.next()
                for k in range(8):
                    mm(pu[:, 0:GT], wfu[:, k, hc * 128:(hc + 1) * 128], hT[:, k, :], k == 0, k == 7, ["wfu", hk], [puk])
                sk, sg = sg_r.next()
                act(sg, pg[:, 0:GT], AF.Silu, [pgk], [sk])
                tt("dve", aT[:, hc, :], sg, pu[:, 0:GT], ALU.mult, [sk, puk], [ak], waw=False)
            for t in range(nt):
                xk, xt = xts[t]
                ok_, xo = xo_r.next()
                r0 = g * GT + t * 128
                for cc in range(2):
                    pk, pp = mm_r.next()
                    for hc in range(22):
                        mm(pp, aT[:, hc, t * 128:(t + 1) * 128], wfd[:, hc, cc * 512:(cc + 1) * 512], hc == 0, hc == 21, ["wfd", ak], [pk])
                    tt("dve", xo[:, cc * 512:(cc + 1) * 512], pp, xt[:, cc * 512:(cc + 1) * 512], ALU.add, [pk, xk], [ok_], waw=False)
                if not last:
                    dma(XRES[r0:r0 + 128, :], xo, [ok_], ["XRES"], waw=False)
                else:
                    jk, junk = rot["junk"].next()
                    kr, rs = rms_scale(xo, ok_, D, junk, jk)
                    fk, fo = fo_r.next()
                    stt(fo, xo, rs, gfin, ALU.mult, ALU.mult, [ok_, kr, "gfin"], [fk])
                    dma(out_d[r0:r0 + 128, :], fo, [fk], ["out"], waw=False)

    phases = build.phases
    if "0" in phases:
        phase0()
    for l in range(L):
        xsrc, xkey = (x_in, "x") if l == 0 else (XRES, "XRES")
        if "1" in phases:
            phase1(l, xsrc, xkey)
        if "a" in phases:
            phase2a(l)
        if "b" in phases:
            phase2bc(l, "B")
        if "c" in phases:
            phase2bc(l, "C")
        if "3" in phases:
            phase3a(l, xsrc, xkey)
        if "4" in phases:
            phase3b(l, l == L - 1)
    finals = [("dma", "out")]
    if dbg:
        finals += [("dma", n) for n in ("XMID", "XRES")]
    P.emit(final_wait_streams=finals)
    st.close()
    return nc


build.phases = "01abc34"


def host_inputs(S, L, x_loc, p, rank):
    f = np.float32
    SL = S // 2
    LB = ((S + 127 + 383) // 384) * 384
    rep = lambda v: np.ascontiguousarray(np.broadcast_to(v[:, None, :], (v.shape[0], 128, v.shape[1])).astype(f))
    w_in = np.ascontiguousarray(p["w_in"][:L])
    kr = w_in[:, :, 4096:4128]
    w_krs = np.ascontiguousarray(np.concatenate([kr[:, :, 16:32], kr[:, :, 0:16]], axis=-1))
    wuq = p["w_uq"][:L].reshape(L, 768, 8, 96)
    w_uqp = np.ascontiguousarray(np.concatenate([wuq[..., 64:96], wuq[..., 0:64]], axis=-1).reshape(L, 768, 768))
    w_uqs = np.ascontiguousarray(np.concatenate([wuq[..., 80:96], wuq[..., 64:80]], axis=-1).reshape(L, 768, 256))
    wukv = p["w_ukv"][:L].reshape(L, 256, 8, 128)
    w_ukvp = np.ascontiguousarray(np.concatenate([wukv[..., 0:64].reshape(L, 256, 512), wukv[..., 64:128].reshape(L, 256, 512)], axis=-1))
    lam_v = np.concatenate([p["lambda_q1"][:L], p["lambda_k1"][:L], p["lambda_q2"][:L], p["lambda_k2"][:L]], axis=-1)
    bgt = np.ascontiguousarray(p["b_gate"][:L].reshape(L, 24, 128).transpose(0, 2, 1))
    tab = p["rel_bias_table"].astype(f)
    tabaug = np.concatenate([tab, np.full((1, 12), -30000.0, f)], axis=0)
    cols = [4 * rank + i for i in range(4)] + [8 + 2 * rank + i for i in range(2)]
    tabrep = np.ascontiguousarray(np.broadcast_to(tabaug.T[cols][:, :, None], (6, 33, 128)).astype(f))
    dist = np.arange(LB) - 127
    bk = np.where(dist >= 0, t5_bucket_np(dist), 32)
    oh_b = np.zeros((33, LB), f)
    oh_b[bk, np.arange(LB)] = 1.0
    oh_a = np.zeros((33, 3 * LA), f)
    for pi, (win_, dil) in enumerate(PATS):
        step = np.arange(LA) - 127
        ok = (step >= 0) & (step <= 128)
        b_ = np.where(ok, t5_bucket_np(step * dil), 32)
        oh_a[b_, pi * LA + np.arange(LA)] = 1.0
    pos = np.arange(rank * SL, (rank + 1) * SL).astype(f)
    inv = (np.float32(10000.0) ** (-np.arange(0, 32, 2, dtype=f) / np.float32(32))).astype(f)
    ang = (pos[:, None] * inv[None, :]).astype(f)
    cos, sin = np.cos(ang).astype(f).T, np.sin(ang).astype(f).T
    cos32 = np.ascontiguousarray(np.concatenate([cos, cos], axis=0))
    sin32 = np.ascontiguousarray(np.concatenate([-sin, sin], axis=0))
    kk = np.arange(128)
    m = {
        "x": np.ascontiguousarray(x_loc.astype(f)),
        "w_in": w_in, "w_krs": w_krs, "w_uqp": w_uqp, "w_uqs": w_uqs, "w_ukvp": w_ukvp,
        "w_gate": np.ascontiguousarray(p["w_gate"][:L]),
        "w_br_a": np.ascontiguousarray(p["w_br_a"][:L]), "w_br_b": np.ascontiguousarray(p["w_br_b"][:L]),
        "w_br_c": np.ascontiguousarray(p["w_br_c"][:L]), "w_o": np.ascontiguousarray(p["w_o"][:L]),
        "w_ffn_gate": np.ascontiguousarray(p["w_ffn_gate"][:L]), "w_ffn_up": np.ascontiguousarray(p["w_ffn_up"][:L]),
        "w_ffn_down": np.ascontiguousarray(p["w_ffn_down"][:L]),
        "g_mix": rep(p["ln_mix_g"][:L]), "g_q": rep(p["mla_q_norm_g"][:L]), "g_kv": rep(p["mla_kv_norm_g"][:L]),
        "g_ffn": rep(p["ln_ffn_g"][:L]), "g_fin": np.ascontiguousarray(np.broadcast_to(p["final_norm_g"][None, :], (128, D)).astype(f)),
        "g_sub": rep(p["diff_subln_g"][:L]), "lam_v": rep(lam_v), "b_gate": bgt.astype(f),
        "tabrep": tabrep, "oh_b": oh_b, "oh_a": oh_a, "cos32": cos32, "sin32": sin32,
        "ident": np.eye(128, dtype=f), "maskc": (kk[None, :] >= kk[:, None]).astype(f),
        "msel": np.ascontiguousarray(np.broadcast_to(np.eye(2, dtype=f)[rank][None, :], (128, 2))),
    }
    return m


_NC_CACHE = {}


def kernel(**inputs):
    p = {k: np.asarray(v) for k, v in inputs.items()}
    x = p["x"]
    B, S, _ = x.shape
    SL = S // 2
    L = p["w_in"].shape[0]
    key = (S, L)
    if key not in _NC_CACHE:
        _NC_CACHE[key] = build(S, L, ncores=2 * B)
    nc = _NC_CACHE[key]
    shared = [host_inputs(S, L, x[0, r * SL:(r + 1) * SL], p, r) for r in range(2)]
    in_maps = []
    for c in range(2 * B):
        b, r = c // 2, c % 2
        m = dict(shared[r])
        m["x"] = np.ascontiguousarray(x[b, r * SL:(r + 1) * SL].astype(np.float32))
        in_maps.append(m)
    res = run_bass_kernel_spmd(nc, in_maps, core_ids=list(range(2 * B)))
    out = np.empty((B, S, D), np.float32)
    for c in range(2 * B):
        b, r = c // 2, c % 2
        out[b, r * SL:(r + 1) * SL] = np.asarray(res.results[c]["out"])
    return out
```

```python
import math
import numpy as np
from contextlib import ExitStack
import concourse.bass as bass
import concourse.mybir as mybir
from concourse.bass_utils import run_bass_kernel_spmd

F32 = mybir.dt.float32
BF16 = mybir.dt.bfloat16
AF = mybir.ActivationFunctionType
ALU = mybir.AluOpType

D = 1024
DIN = 4128
FH = 2816
LA = 384
EPS = 1e-6
PATS = ((128, 1), (512, 4), (2048, 16))


class _Sem:
    def __init__(self, name):
        self.name = name
        self.h = None


class _Op:
    __slots__ = ("eng", "fn", "deps", "dma", "stream", "needed", "ev", "cc")

    def __init__(self, eng, fn, dma, stream):
        self.cc = False
        self.eng = eng
        self.fn = fn
        self.dma = dma
        self.stream = stream
        self.deps = set()
        self.needed = False
        self.ev = None


class Prog:
    def __init__(self, nc):
        self.nc = nc
        self.ops = []
        self.lastw = {}
        self.readers = {}
        self.sems = []
        self.last_of_stream = {}
        self.bar = None
        self.bar_done = set()
        self.bar_positions = []
        self.bank_of = {}
        self.bank_last = {}

    def barrier(self):
        self.bar = {k: v for k, v in self.last_of_stream.items() if not (isinstance(k, tuple) and k[0] == "cc")}
        self.bar_done = set()
        self.bar_positions.append(len(self.ops))

    def op(self, eng, fn, reads=(), writes=(), dma=False, waw=True, cc=False):
        i = len(self.ops)
        stream = ("dma", writes[0]) if dma else eng
        if cc:
            stream = ("cc", writes[0])
        o = _Op(eng, fn, dma, stream)
        o.cc = cc
        if self.bar is not None and eng not in self.bar_done:
            self.bar_done.add(eng)
            for s, j in self.bar.items():
                if s == eng and not dma and eng == "pe":
                    continue
                o.deps.add(j)
        for r in reads:
            w = self.lastw.get(r)
            if w is not None:
                self._dep(o, w, "raw")
        for r in writes:
            w = self.lastw.get(r)
            if w is not None and waw:
                self._dep(o, w, "waw")
            for x in self.readers.get(r, {}).values():
                self._dep(o, x, "war")
        for r in reads:
            self.readers.setdefault(r, {})[stream] = i
        for r in writes:
            self.lastw[r] = i
            self.readers[r] = {}
        if not dma:
            banks = set()
            for r in list(reads) + list(writes):
                b = self.bank_of.get(r)
                if b is not None:
                    banks.add(b)
            for b in banks:
                bl = self.bank_last.setdefault(b, {})
                for f_eng, j in bl.items():
                    if f_eng != eng:
                        o.deps.add(j)
                bl[eng] = i
        self.last_of_stream[stream] = i
        self.ops.append(o)
        return i

    def _dep(self, o, j, kind):
        p = self.ops[j]
        if not p.dma and not o.dma and p.eng == o.eng:
            if o.eng == "pe":
                return
            if kind == "war":
                return
        o.deps.add(j)

    def emit(self, final_wait_streams=()):
        nc = self.nc
        ops = self.ops
        MAXV = 8000
        for o in ops:
            for j in o.deps:
                ops[j].needed = True
        bars = sorted(set(self.bar_positions))
        seg_of = []
        bi = 0
        for i in range(len(ops)):
            while bi < len(bars) and bars[bi] <= i:
                bi += 1
            seg_of.append(bi)
        cnt = {}
        for i, o in enumerate(ops):
            if o.dma and not o.cc:
                k = (seg_of[i], o.stream)
                cnt[k] = cnt.get(k, 0) + 1
        free = []
        ccmap = {}
        dmap = {}
        emap = {}
        last_ev = {}

        def new_phys():
            s_ = _Sem("s%d" % len(self.sems))
            self.sems.append(s_)
            return [s_, 0]

        cur_seg = 0
        for i, o in enumerate(ops):
            if seg_of[i] != cur_seg:
                cur_seg = seg_of[i]
                for ph in dmap.values():
                    free.append(ph)
                dmap = {}
            if o.cc:
                ph = ccmap.get(o.stream)
                if ph is None:
                    ph = new_phys()
                    ccmap[o.stream] = ph
                ph[1] += 1
                o.ev = (ph[0], ph[1])
                last_ev[o.stream] = o.ev
            elif o.dma:
                ph = dmap.get(o.stream)
                if ph is None:
                    need = 16 * cnt[(cur_seg, o.stream)]
                    for fi, cand in enumerate(free):
                        if cand[1] + need <= MAXV:
                            ph = free.pop(fi)
                            break
                    if ph is None:
                        ph = new_phys()
                    dmap[o.stream] = ph
                ph[1] += 16
                o.ev = (ph[0], ph[1])
                last_ev[o.stream] = o.ev
            elif o.needed:
                ph = emap.get(o.stream)
                if ph is None or ph[1] + 1 > MAXV:
                    ph = new_phys()
                    emap[o.stream] = ph
                ph[1] += 1
                o.ev = (ph[0], ph[1])
        per_eng = {}
        for o in ops:
            per_eng.setdefault(o.eng, []).append(o)
        finals = [last_ev[s] for s in final_wait_streams if s in last_ev]
        self.nsem = len(self.sems)
        with ExitStack() as st:
            for s in self.sems:
                s.h = st.enter_context(nc.semaphore(s.name))
            block = st.enter_context(nc.Block())

            def run(eng_name, e):
                seen = {}
                for o in per_eng.get(eng_name, []):
                    need = {}
                    for j in o.deps:
                        s, v = ops[j].ev
                        if need.get(s, 0) < v:
                            need[s] = v
                    for s, v in need.items():
                        if seen.get(s, 0) < v:
                            e.wait_ge(s.h, v)
                            seen[s] = v
                    ins = o.fn(e)
                    if o.cc:
                        ins.then_inc(o.ev[0].h)
                    elif o.ev is not None:
                        ins.then_inc(o.ev[0].h, 16 if o.dma else 1)
                if eng_name == "sp":
                    for s, v in finals:
                        e.wait_ge(s.h, v)

            @block.tensor
            def _(e):
                run("pe", e)

            @block.scalar
            def _(e):
                run("act", e)

            @block.vector
            def _(e):
                run("dve", e)

            @block.gpsimd
            def _(e):
                run("pool", e)

            @block.sync
            def _(e):
                run("sp", e)


class Arena:
    def __init__(self, base, nbytes):
        self.base = base
        self.nbytes = nbytes
        self.off = 0

    def reset(self):
        self.off = 0

    def alloc(self, shape, dt):
        n = 1
        for s in shape:
            n *= s
        nb = n * (4 if dt == F32 else 2)
        nb = (nb + 63) // 64 * 64
        assert self.off + nb <= self.nbytes, (self.off, nb, self.nbytes)
        v = self.base[:, self.off // 2:(self.off + nb) // 2]
        self.off += nb
        if dt == F32:
            v = v.bitcast(F32)
        v = v[:, 0:n]
        if len(shape) == 2:
            v = v.rearrange("p (a b) -> p a b", b=shape[1])
        elif len(shape) == 3:
            v = v.rearrange("p (a b c) -> p a b c", b=shape[1], c=shape[2])
        return v


class Rot:
    def __init__(self, name, aps, P=None, banks=None):
        self.name = name
        self.aps = aps
        self.i = 0
        if banks is not None:
            for k, b in enumerate(banks):
                P.bank_of["%s#%d" % (name, k)] = b

    def next(self):
        k = self.i % len(self.aps)
        self.i += 1
        return "%s#%d" % (self.name, k), self.aps[k]


def t5_bucket_np(dist):
    dist = np.maximum(dist, 0)
    exact = 16
    lr = np.log(np.maximum(dist, 1).astype(np.float32) / np.float32(exact)) / np.float32(math.log(2048 / exact))
    large = np.minimum(exact + (lr.astype(np.float32) * np.float32(16)).astype(np.int32), 31)
    return np.where(dist < exact, dist, large)


def t5_bucket_exact(dist):
    return t5_bucket_np(np.asarray(dist))


FM_QA, FM_KA, FM_QB, FM_KB, FM_QC, FM_KC, FM_ROWS = 0, 512, 1024, 1536, 2048, 2816, 3584
TM_VA, TM_VB, TM_VC, TM_COLS = 0, 520, 1036, 1556


def build(S, L, dbg=False, ncores=8):
    NT = S // 128
    NG = S // 512
    SL = S // 2
    NGL = SL // 512
    groups = [[2 * i, 2 * i + 1] for i in range(ncores // 2)]
    LB = ((S + 127 + 383) // 384) * 384
    nc = bass.Bass("TRN2", target_bir_lowering=False)
    P = Prog(nc)

    def din(name, shape, dt=F32):
        return nc.dram_tensor(name, list(shape), dt, kind="ExternalInput")

    def dscr(name, shape, dt):
        return nc.dram_tensor(name, list(shape), dt, kind="ExternalOutput" if dbg else "Internal")

    x_in = din("x", [SL, D])
    w_in = din("w_in", [L, D, DIN])
    w_krs = din("w_krs", [L, D, 32])
    w_uqp = din("w_uqp", [L, 768, 768])
    w_uqs = din("w_uqs", [L, 768, 256])
    w_ukvp = din("w_ukvp", [L, 256, 1024])
    w_gate = din("w_gate", [L, D, 3 * D])
    w_bra = din("w_br_a", [L, 512, D])
    w_brb = din("w_br_b", [L, 512, D])
    w_brc = din("w_br_c", [L, 512, D])
    w_o = din("w_o", [L, D, D])
    w_fg = din("w_ffn_gate", [L, D, FH])
    w_fu = din("w_ffn_up", [L, D, FH])
    w_fd = din("w_ffn_down", [L, FH, D])
    g_mix = din("g_mix", [L, 128, D])
    g_q = din("g_q", [L, 128, 768])
    g_kv = din("g_kv", [L, 128, 256])
    g_ffn = din("g_ffn", [L, 128, D])
    g_fin = din("g_fin", [128, D])
    g_sub = din("g_sub", [L, 128, 128])
    lam_v = din("lam_v", [L, 128, 256])
    b_gate = din("b_gate", [L, 128, 24])
    tabrep = din("tabrep", [6, 33, 128])
    oh_b = din("oh_b", [33, LB])
    oh_a = din("oh_a", [33, 3 * LA])
    cos_d = din("cos32", [32, SL])
    sin_d = din("sin32", [32, SL])
    ident_d = din("ident", [128, 128])
    maskc_d = din("maskc", [128, 128])
    msel_d = din("msel", [128, 2])
    out_d = nc.dram_tensor("out", [SL, D], F32, kind="ExternalOutput")

    def dint(name, shape, dt):
        return nc.dram_tensor(name, list(shape), dt)

    XT = {}

    def xbuf(name, rows, cols):
        XT[name] = (dint(name, [rows, cols], BF16), dint(name + "g", [2 * rows, cols], BF16), rows)

    for n_ in ("QA", "KA", "QB", "KB"):
        xbuf(n_, 512, SL)
    for n_ in ("QC0", "QC1", "KC0", "KC1"):
        xbuf(n_, 384, SL)
    for n_ in ("VA0", "VA1", "VC0", "VC1"):
        xbuf(n_, SL, 260)
    for n_ in ("VB0", "VB1"):
        xbuf(n_, SL, 258)
    xbuf("YTA", 256, S)
    xbuf("YB", S, 256)
    xbuf("YC", S, 256)

    def XL(name):
        return XT[name][0]

    def XG(name):
        return XT[name][1]

    XMID = dscr("XMID", [SL, D], F32)
    XRES = dscr("XRES", [SL, D], F32)
    FB_D = dint("FB_D", [2, 128, LB], BF16)
    FA_D = dint("FA_D", [4, 3, 128, LA], BF16)

    st = ExitStack()
    ARENA_B = 206 * 1024
    arena_t = st.enter_context(nc.sbuf_tensor("arena", [128, ARENA_B // 2], BF16))
    A = Arena(arena_t, ARENA_B)
    idb_t = st.enter_context(nc.sbuf_tensor("idb", [128, 128], BF16))
    mkc_t = st.enter_context(nc.sbuf_tensor("mkc", [128, 128], BF16))
    onesf_t = st.enter_context(nc.sbuf_tensor("onesf", [128, 64], F32))
    sm_t = st.enter_context(nc.sbuf_tensor("smalls", [128, 64], F32))
    msel_t = st.enter_context(nc.sbuf_tensor("msel_sb", [128, 2], F32))
    msel = msel_t[:]
    idb = idb_t[:]
    mkc = mkc_t[:]
    onesf = onesf_t[:]
    psb = [st.enter_context(nc.psum_tensor("ps%d" % i, [128, 512], F32)) for i in range(8)]

    sm_i = [0]

    def small():
        k = sm_i[0] % 64
        sm_i[0] += 1
        return "sm#%d" % k, sm_t[:, k:k + 1]

    def dma(out, in_, reads, writes, eng="sp", waw=True):
        P.op(eng, lambda e: e.dma_start(out=out, in_=in_), reads=reads, writes=writes, dma=True, waw=waw)

    def sel(dst, dkey, c0, k0, c1, k1, np_=128, waw=True):
        act(c1, c1, AF.Copy, [k1, "msel"], [k1], scale=msel[0:np_, 1:2])
        stt(dst, c0, msel[0:np_, 0:1], c1, ALU.mult, ALU.add, [k0, k1, "msel"], [dkey], waw=waw)

    def allgather(name):
        src, dst = XL(name), XG(name)
        P.op("pool", lambda e: e.collective_compute("AllGather", ALU.bypass, replica_groups=groups, ins=[src.ap().opt()], outs=[dst.ap().opt()]),
             reads=[name], writes=[name + "g"], cc=True)

    def mm(out, lhsT, rhs, start, stop, reads, writes, skip=False):
        if skip:
            P.op("pe", lambda e: e.matmul(out, lhsT=lhsT, rhs=rhs, start=start, stop=stop, skip_group_check=True), reads=reads, writes=writes)
        else:
            P.op("pe", lambda e: e.matmul(out, lhsT=lhsT, rhs=rhs, start=start, stop=stop), reads=reads, writes=writes)

    def tr(out, in_, reads, writes):
        P.op("pe", lambda e: e.transpose(out=out, in_=in_, identity=idb), reads=list(reads) + ["idb"], writes=writes)

    def act(out, in_, func, reads, writes, scale=None, bias=None, accum=None, waw=True):
        kw = {}
        if scale is not None:
            kw["scale"] = scale
        if bias is not None:
            kw["bias"] = bias
        if accum is not None:
            kw["accum_out"] = accum
        P.op("act", lambda e: e.activation(out=out, in_=in_, func=func, **kw), reads=reads, writes=writes, waw=waw)

    def tt(eng, out, in0, in1, op, reads, writes, waw=True):
        P.op(eng, lambda e: e.tensor_tensor(out=out, in0=in0, in1=in1, op=op), reads=reads, writes=writes, waw=waw)

    def stt(out, in0, scalar, in1, op0, op1, reads, writes, waw=True):
        P.op("dve", lambda e: e.scalar_tensor_tensor(out=out, in0=in0, scalar=scalar, in1=in1, op0=op0, op1=op1), reads=reads, writes=writes, waw=waw)

    def tcopy(eng, out, in_, reads, writes, waw=True):
        P.op(eng, lambda e: e.tensor_copy(out=out, in_=in_), reads=reads, writes=writes, waw=waw)

    def recip(out, in_, reads, writes):
        P.op("dve", lambda e: e.reciprocal(out=out, in_=in_), reads=reads, writes=writes)

    def tsadd(out, in0, c, reads, writes):
        P.op("dve", lambda e: e.tensor_scalar(out=out, in0=in0, scalar1=c, scalar2=None, op0=ALU.add), reads=reads, writes=writes)

    def memset(eng, ap, v, writes):
        P.op(eng, lambda e: e.memset(ap, v), writes=writes)

    def wload(dst3, dst_key, src2d, nchunk):
        for c in range(nchunk):
            dma(dst3[:, c, :], src2d[c * 128:(c + 1) * 128, :], [], [dst_key], eng="pool", waw=False)

    def wload_cols(dst3, key_fn, src2d, nchunk, blocks):
        for (c0, c1) in blocks:
            for k in range(nchunk):
                dma(dst3[:, k, c0:c1], src2d[k * 128:(k + 1) * 128, c0:c1], [], [key_fn(c0)], eng="pool", waw=False)

    evac = [0]

    def evac_copy(out, in_, reads, writes, waw=True):
        evac[0] += 1
        if evac[0] % 2:
            act(out, in_, AF.Copy, reads, writes, waw=waw)
        else:
            tcopy("dve", out, in_, reads, writes, waw=waw)

    def rms_scale(src, skey, Dn, junk, jk):
        sk, ss = small()
        rk, rs = small()
        act(junk, src, AF.Square, [skey], [jk, sk], scale=float(Dn) ** -0.5, accum=ss)
        tsadd(ss, ss, EPS, [sk], [sk])
        act(ss, ss, AF.Sqrt, [sk], [sk])
        recip(rs, ss, [sk], [rk])
        return rk, rs

    def phase0():
        A.reset()
        idf = A.alloc([128], F32)
        mkf = A.alloc([128], F32)
        dma(idf, ident_d[:, :], [], ["idf"])
        dma(mkf, maskc_d[:, :], [], ["mkf"])
        dma(msel, msel_d[:, :], [], ["msel"])
        tcopy("dve", idb, idf, ["idf"], ["idb"])
        tcopy("dve", mkc, mkf, ["mkf"], ["mkc"])
        memset("dve", onesf, 1.0, ["onesf"])
        ohb = A.alloc([LB], F32)
        oha = A.alloc([3 * LA], F32)
        dma(ohb[0:33], oh_b[:, :], [], ["ohb"])
        dma(oha[0:33], oh_a[:, :], [], ["oha"])
        tabs = Rot("tab", [A.alloc([128], F32) for _ in range(2)])
        stg = Rot("fstg", [A.alloc([LA], BF16) for _ in range(3)])
        psr = Rot("ps", [psb[i][:] for i in range(4)], P, [0, 1, 2, 3])
        for h in range(6):
            tk, tb = tabs.next()
            dma(tb[0:33], tabrep[h, :, :], [], [tk])
            if h < 4:
                chunks = [(oha, "oha", p * LA, FA_D[h, p, :, :], "FA_D") for p in range(3)]
            else:
                chunks = [(ohb, "ohb", c * LA, FB_D[h - 4, :, c * LA:(c + 1) * LA], "FB_D") for c in range(LB // LA)]
            for (src, skey, off, dst, dname) in chunks:
                pk, pp = psr.next()
                mm(pp[:, 0:LA], tb[0:33], src[0:33, off:off + LA], True, True, [tk, skey], [pk])
                sk, sg = stg.next()
                act(sg, pp[:, 0:LA], AF.Exp, [pk], [sk])
                dma(dst, sg, [sk], [dname], waw=False)

    def norm_T(src, skey, C, Dn, g_ap, gkey, dst3, dkey, rot):
        jk, junk = rot["junk"].next()
        xk, xn = rot["xn"].next()
        W = C * 128
        rk, rs = rms_scale(src, skey, Dn, junk[:, 0:W], jk)
        stt(xn[:, 0:W], src, rs, g_ap, ALU.mult, ALU.mult, [skey, rk, gkey], [xk])
        pk, pp = rot["pT"].next()
        ppb = pp.bitcast(BF16)
        for c in range(C):
            tr(ppb[:, c * 128:(c + 1) * 128], xn[:, c * 128:(c + 1) * 128], [xk], [pk])
        act(dst3, ppb[:, 0:W].rearrange("p (c t) -> p c t", t=128), AF.Copy, [pk], [dkey], waw=False)

    def std_rots():
        return {
            "junk": Rot("junk", [A.alloc([1024], BF16)]),
            "xn": Rot("xn", [A.alloc([1024], BF16) for _ in range(2)]),
            "pT": Rot("psT", [psb[6][:], psb[7][:]], P, [6, 7]),
        }

    def phase1(l, xsrc, xsrc_key):
        P.barrier()
        A.reset()
        win = A.alloc([8, DIN], BF16)
        wkrs = A.alloc([8, 32], BF16)
        wuq = A.alloc([6, 768], BF16)
        wuqs = A.alloc([6, 256], BF16)
        wukv = A.alloc([2, 1024], BF16)
        gm = A.alloc([D], F32)
        gq = A.alloc([768], F32)
        gkv = A.alloc([256], F32)
        dma(gm, g_mix[l, :, :], [], ["gm"])
        dma(gq, g_q[l, :, :], [], ["gq"])
        dma(gkv, g_kv[l, :, :], [], ["gkv"])
        wkey = lambda c0: "win@%d" % (c0 // 512)
        wload_cols(win, wkey, w_in[l, :, :], 8, [(0, 512), (512, 1024), (1024, 1536), (1536, 2048), (2048, 2560), (2560, 3072), (4096, 4128), (3072, 3584), (3584, 4096)])
        wload(wkrs, "wkrs", w_krs[l, :, :], 8)
        wload(wuq, "wuq", w_uqp[l, :, :], 6)
        wload(wuqs, "wuqs", w_uqs[l, :, :], 6)
        wload(wukv, "wukv", w_ukvp[l, :, :], 2)
        rot = std_rots()
        xt_r = Rot("xt", [A.alloc([D], F32) for _ in range(2)])
        hT_r = Rot("hT", [A.alloc([8, 512], BF16) for _ in range(NGL)])
        cq_r = Rot("cq", [A.alloc([768], F32) for _ in range(2)])
        ckv_r = Rot("ckv", [A.alloc([256], F32) for _ in range(2)])
        cqT_r = Rot("cqT", [A.alloc([6, 512], BF16)])
        ckvT_r = Rot("ckvT", [A.alloc([2, 512], BF16)])
        fst_r = Rot("fst", [A.alloc([512], BF16) for _ in range(4)])
        va_r = Rot("vast", [A.alloc([8, 65], BF16) for _ in range(2)])
        vc_r = Rot("vcst", [A.alloc([8, 65], BF16) for _ in range(2)])
        vb_r = Rot("vbst", [A.alloc([4, 129], BF16) for _ in range(2)])
        cs_r = Rot("cs", [A.alloc([512], F32) for _ in range(2)])
        sn_r = Rot("sn", [A.alloc([512], F32) for _ in range(2)])
        t1_r = Rot("t1", [A.alloc([512], F32) for _ in range(2)])
        t2_r = Rot("t2", [A.alloc([512], F32) for _ in range(2)])
        mm_r = Rot("psm", [psb[i][:] for i in range(6)], P, list(range(6)))
        for r_ in (va_r, vc_r, vb_r):
            for i_, ap_ in enumerate(r_.aps):
                memset("pool", ap_, 1.0, ["%s#%d" % (r_.name, i_)])

        def rope_evac(pm, pmk, psw, pswk, dst, dkey, ck, cs, sk, sn):
            k1, t1 = t1_r.next()
            k2, t2 = t2_r.next()
            tt("dve", t1[0:32], pm[0:32, :], cs[0:32], ALU.mult, [pmk, ck], [k1])
            tt("dve", t2[0:32], psw[0:32, :], sn[0:32], ALU.mult, [pswk, sk], [k2])
            tt("dve", dst, t1[0:32], t2[0:32], ALU.add, [k1, k2], [dkey], waw=False)

        def tokmm(hT, hk, ts_, col, n, pp, pk):
            for k in range(8):
                mm(pp[:, 0:n], hT[:, k, ts_], win[:, k, col:col + n], k == 0, k == 7, [wkey(col), hk], [pk])

        def fm_proj(g, hk, hT, pairs):
            tok = slice(g * 512, (g + 1) * 512)
            for (c0, dname) in pairs:
                for c in range(4):
                    pk, pp = mm_r.next()
                    col = c0 + c * 128
                    for k in range(8):
                        mm(pp, win[:, k, col:col + 128], hT[:, k, :], k == 0, k == 7, [wkey(col), hk], [pk])
                    fk, fs = fst_r.next()
                    evac_copy(fs, pp, [pk], [fk])
                    dma(XL(dname)[c * 128:(c + 1) * 128, tok], fs, [fk], [dname], waw=False)

        def tm_v(g, hk, hT, col, rot_, nhh, dd, pre):
            for t in range(4):
                ts_ = slice(t * 128, (t + 1) * 128)
                r0 = g * 512 + t * 128
                pk, pp = mm_r.next()
                tokmm(hT, hk, ts_, col, 512, pp, pk)
                vk, vs = rot_.next()
                evac_copy(vs[:, :, 0:dd], pp.rearrange("p (h d) -> p h d", d=dd), [pk], [vk])
                for c_ in range(2):
                    dma(XL("%s%d" % (pre, c_))[r0:r0 + 128, :], vs[:, nhh * c_:nhh * c_ + nhh, :].rearrange("p h d -> p (h d)"), [vk], ["%s%d" % (pre, c_)], waw=False)

        hTs = []
        for g in range(NGL):
            hk, hT = hT_r.next()
            hTs.append((hk, hT))
            for t in range(4):
                xk, xt = xt_r.next()
                r0 = g * 512 + t * 128
                dma(xt, xsrc[r0:r0 + 128, :], [xsrc_key], [xk])
                norm_T(xt, xk, 8, D, gm, "gm", hT[:, :, t * 128:(t + 1) * 128], hk, rot)
            fm_proj(g, hk, hT, ((0, "QA"), (512, "KA")))
            tm_v(g, hk, hT, 1024, va_r, 4, 64, "VA")
        for n_ in ("QA", "KA", "VA0", "VA1"):
            allgather(n_)
        for g in range(NGL):
            hk, hT = hTs[g]
            fm_proj(g, hk, hT, ((1536, "QB"), (2048, "KB")))
            tm_v(g, hk, hT, 2560, vb_r, 2, 128, "VB")
        for n_ in ("QB", "KB", "VB0", "VB1"):
            allgather(n_)
        for g in range(NGL):
            hk, hT = hTs[g]
            tok = slice(g * 512, (g + 1) * 512)
            ck, cs = cs_r.next()
            sk, sn = sn_r.next()
            dma(cs[0:32], cos_d[:, tok], [], [ck])
            dma(sn[0:32], sin_d[:, tok], [], [sk])
            pk, pp = mm_r.next()
            for k in range(8):
                mm(pp[0:32, :], win[:, k, 4096:4128], hT[:, k, :], k == 0, k == 7, [wkey(4096), hk], [pk])
            pk2, pp2 = mm_r.next()
            for k in range(8):
                mm(pp2[0:32, :], wkrs[:, k, :], hT[:, k, :], k == 0, k == 7, ["wkrs", hk], [pk2])
            fk, fs = fst_r.next()
            rope_evac(pp, pk, pp2, pk2, fs[0:32], fk, ck, cs, sk, sn)
            for h in range(8):
                dma(XL("KC%d" % (h // 4))[(h % 4) * 96:(h % 4) * 96 + 32, tok], fs[0:32], [fk], ["KC%d" % (h // 4)], waw=False)
            cqk, cqT = cqT_r.next()
            ckk, ckvT = ckvT_r.next()
            for t in range(4):
                ts_ = slice(t * 128, (t + 1) * 128)
                r0 = g * 512 + t * 128
                qk_, cq = cq_r.next()
                pk, pp = mm_r.next()
                tokmm(hT, hk, ts_, 3072, 512, pp, pk)
                evac_copy(cq[:, 0:512], pp, [pk], [qk_], waw=False)
                pk, pp = mm_r.next()
                tokmm(hT, hk, ts_, 3584, 256, pp, pk)
                evac_copy(cq[:, 512:768], pp[:, 0:256], [pk], [qk_], waw=False)
                norm_T(cq, qk_, 6, 768, gq, "gq", cqT[:, :, ts_], cqk, rot)
                kk_, ckv = ckv_r.next()
                pk, pp = mm_r.next()
                tokmm(hT, hk, ts_, 3840, 256, pp, pk)
                evac_copy(ckv, pp[:, 0:256], [pk], [kk_])
                norm_T(ckv, kk_, 2, 256, gkv, "gkv", ckvT[:, :, ts_], ckk, rot)
                pk, pp = mm_r.next()
                for k in range(2):
                    mm(pp, ckvT[:, k, ts_], wukv[:, k, 512:1024], k == 0, k == 1, ["wukv", ckk], [pk])
                vk, vs = vc_r.next()
                evac_copy(vs[:, :, 0:64], pp.rearrange("p (h d) -> p h d", d=64), [pk], [vk])
                for c_ in range(2):
                    dma(XL("VC%d" % c_)[r0:r0 + 128, :], vs[:, 4 * c_:4 * c_ + 4, :].rearrange("p h d -> p (h d)"), [vk], ["VC%d" % c_], waw=False)
            for h in range(8):
                pk, pp = mm_r.next()
                for k in range(6):
                    mm(pp[0:96, :], wuq[:, k, h * 96:(h + 1) * 96], cqT[:, k, :], k == 0, k == 5, ["wuq", cqk], [pk])
                pk2, pp2 = mm_r.next()
                for k in range(6):
                    mm(pp2[0:32, :], wuqs[:, k, h * 32:(h + 1) * 32], cqT[:, k, :], k == 0, k == 5, ["wuqs", cqk], [pk2])
                fk, fs = fst_r.next()
                act(fs[32:64], pp[32:64, :], AF.Copy, [pk], [fk])
                tcopy("dve", fs[64:96], pp[64:96, :], [pk], [fk], waw=False)
                rope_evac(pp, pk, pp2, pk2, fs[0:32], fk, ck, cs, sk, sn)
                dma(XL("QC%d" % (h // 4))[(h % 4) * 96:(h % 4 + 1) * 96, tok], fs[0:96], [fk], ["QC%d" % (h // 4)], waw=False)
            for hp in range(4):
                pk, pp = mm_r.next()
                for k in range(2):
                    mm(pp, wukv[:, k, hp * 128:(hp + 1) * 128], ckvT[:, k, :], k == 0, k == 1, ["wukv", ckk], [pk])
                fk, fs = fst_r.next()
                evac_copy(fs, pp, [pk], [fk])
                for j in range(2):
                    h = hp * 2 + j
                    dma(XL("KC%d" % (h // 4))[(h % 4) * 96 + 32:(h % 4 + 1) * 96, tok], fs[j * 64:(j + 1) * 64], [fk], ["KC%d" % (h // 4)], waw=False)
        for n_ in ("QC0", "QC1", "KC0", "KC1", "VC0", "VC1"):
            allgather(n_)

    SKEW = 2

    def run_pipeline(steps):
        n = len(steps)
        for i in range(n + SKEW):
            if i < n:
                steps[i][0]()
            if i - SKEW >= 0:
                steps[i - SKEW][1]()

    def phase2a(l):
        P.barrier()
        A.reset()
        Vp = [A.alloc([NT, 260], BF16) for _ in range(3)]
        stg = [A.alloc([NT * 260], BF16) for _ in range(2)]
        for pi, (win_, dil) in enumerate(PATS):
            nb = S // dil // 128
            for c in range(2):
                cv = stg[c].rearrange("p (t c) -> p t c", c=260)
                for r in range(dil):
                    src = bass.AP(XG("VA%d" % c), r * 260, [[dil * 260, 128], [dil * 128 * 260, nb], [1, 260]])
                    dma(cv[:, r * nb:(r + 1) * nb, :], src, ["VA%dg" % c], ["stg%d" % c], waw=False)
            sel(Vp[pi].rearrange("p t c -> p (t c)"), "Vp%d" % pi, stg[0], "stg0", stg[1], "stg1")
        q_r = Rot("qa", [A.alloc([S], BF16) for _ in range(2)])
        k_r = Rot("ka", [A.alloc([S], BF16) for _ in range(2)])
        ea_r = Rot("ea", [A.alloc([3, 256], BF16) for _ in range(2)])
        acc_r = Rot("acc", [A.alloc([S], F32) for _ in range(2)])
        pt_r = Rot("pt", [A.alloc([256], BF16) for _ in range(4)])
        pm_r = Rot("pm", [A.alloc([256], BF16) for _ in range(4)])
        rl_r = Rot("rl", [A.alloc([512], F32) for _ in range(2)])
        ys_r = Rot("ys", [A.alloc([512], BF16) for _ in range(2)])
        ps_s = Rot("pss", [psb[i][:] for i in range(3)], P, [0, 1, 2])
        ps_o = Rot("pso", [psb[3 + i][:, 0:128] for i in range(3)], P, [3, 4, 5])
        ps_l = Rot("psl", [psb[6][:], psb[7][:]], P, [6, 7])
        steps = []

        def mk_front(kk, qk, ek, sk_, sp_, kap, qap, nq, tk, pt, mk, pm, eap):
            def f():
                mm(sp_[:, 0:nq], kap, qap, True, True, [kk, qk], [sk_])
                act(pt[:, 0:nq], sp_[:, 0:nq], AF.Exp, [sk_], [tk], scale=0.125)
                tt("dve", pm[:, 0:nq], pt[:, 0:nq], eap, ALU.mult, [tk, ek], [mk])
            return f

        def mk_back(b, nb, mk, pm, vt, vkey, okey, o_, okey_n, o_n, aap, pi, acck, tail):
            def f():
                mm(o_[0:65, :], vt, pm[:, 0:128], b == 0, True, [mk, vkey], [okey], skip=True)
                if b + 1 < nb:
                    mm(o_n[0:65, :], vt, pm[:, 128:256], True, False, [mk, vkey], [okey_n], skip=True)
                if pi == 0:
                    tcopy("dve", aap, o_[0:65, :], [okey], [acck], waw=False)
                else:
                    tt("dve", aap, aap, o_[0:65, :], ALU.add, [okey, acck], [acck], waw=False)
                if tail is not None:
                    tail()
            return f

        def mk_tail(h, acc, acck):
            def f():
                for c in range(NG):
                    cs_ = slice(c * 512, (c + 1) * 512)
                    lk, lp = ps_l.next()
                    mm(lp[0:64, :], onesf[64:65, 0:64], acc[64:65, cs_], True, True, [acck, "onesf"], [lk])
                    rk, rl = rl_r.next()
                    recip(rl[0:64], lp[0:64, :], [lk], [rk])
                    yk, ys = ys_r.next()
                    tt("dve", ys[0:64], acc[0:64, cs_], rl[0:64], ALU.mult, [rk, acck], [yk])
                    dma(XL("YTA")[h * 64:(h + 1) * 64, cs_], ys[0:64], [yk], ["YTA"], waw=False)
            return f

        def mk_loads(h, qk, qt, kk, kt, ek, ea):
            def f():
                for (dst, dk, xn_, so) in ((qt, qk, "QA", 0), (kt, kk, "KA", 4096)):
                    for c in range(2):
                        for r in range(2):
                            b0 = r * 512 + (4 * c + h) * 64
                            dma(stg[c][0:64, so + r * SL:so + (r + 1) * SL], XG(xn_)[b0:b0 + 64, :], [xn_ + "g"], ["stg%d" % c], waw=False)
                    sel(dst[0:64, :], dk, stg[0][0:64, so:so + S], "stg0", stg[1][0:64, so:so + S], "stg1", np_=64)
                for pi in range(3):
                    dma(ea[:, pi, :], bass.AP(FA_D, ((h * 3 + pi) * 128) * LA + 127, [[LA - 1, 128], [1, 256]]), ["FA_D"], [ek], waw=False)
            return f

        for h in range(4):
            qk, qt = q_r.next()
            kk, kt = k_r.next()
            ek, ea = ea_r.next()
            acck, acc = acc_r.next()
            loads = mk_loads(h, qk, qt, kk, kt, ek, ea)
            first = True
            for pi, (win_, dil) in enumerate(PATS):
                nb = S // dil // 128
                vkey = "Vp%d" % pi
                for r in range(dil):
                    okey_n, o_n = None, None
                    for b in range(nb):
                        nq = 256 if b + 1 < nb else 128
                        base = r + dil * 128 * b
                        kap = kt[0:64, base:base + dil * 127 + 1:dil]
                        qap = qt[0:64, base:base + dil * (nq - 1) + 1:dil]
                        sk_, sp_ = ps_s.next()
                        tk, pt = pt_r.next()
                        mk, pm = pm_r.next()
                        fr = mk_front(kk, qk, ek, sk_, sp_, kap, qap, nq, tk, pt, mk, pm, ea[:, pi, 0:nq])
                        if first:
                            fr = (lambda lo, f0: (lambda: (lo(), f0())))(loads, fr)
                            first = False
                        vt = Vp[pi][:, r * nb + b, h * 65:(h + 1) * 65]
                        if b == 0:
                            okey, o_ = ps_o.next()
                        else:
                            okey, o_ = okey_n, o_n
                        if b + 1 < nb:
                            okey_n, o_n = ps_o.next()
                        aap = acc[0:65, base:base + dil * 127 + 1:dil]
                        last = (pi == 2 and r == dil - 1 and b == nb - 1)
                        tail = mk_tail(h, acc, acck) if last else None
                        bk = mk_back(b, nb, mk, pm, vt, vkey, okey, o_, okey_n, o_n, aap, pi, acck, tail)
                        steps.append((fr, bk))
        run_pipeline(steps)
        allgather("YTA")

    def phase2bc(l, which):
        P.barrier()
        A.reset()
        isB = which == "B"
        nh = 2 if isB else 4
        dv = 128 if isB else 64
        dr = 128 if isB else 96
        vw = (dv + 1) * nh
        vpre, yname = ("VB", "YB") if isB else ("VC", "YC")
        scale = 0.125 if isB else 96.0 ** -0.5
        Vt = A.alloc([NT, vw], BF16)
        stg = [A.alloc([NT * 260], BF16) for _ in range(2)]
        for c in range(2):
            dma(stg[c][:, 0:NT * vw].rearrange("p (t c) -> p t c", c=vw), XG("%s%d" % (vpre, c)).ap().rearrange("(t p) c -> p t c", p=128), ["%s%dg" % (vpre, c)], ["stg%d" % c])
        sel(Vt.rearrange("p t c -> p (t c)"), "Vt", stg[0][:, 0:NT * vw], "stg0", stg[1][:, 0:NT * vw], "stg1")
        Ysb = A.alloc([NT, 256], BF16)
        q_r = Rot("qb", [A.alloc([S], BF16) for _ in range(2)])
        k_r = Rot("kb", [A.alloc([S], BF16) for _ in range(2)])
        e_r = Rot("eb", [A.alloc([512], BF16) for _ in range(4)])
        pt_r = Rot("ptb", [A.alloc([512], BF16) for _ in range(4)])
        pm_r = Rot("pmb", [A.alloc([512], BF16) for _ in range(8)])
        ps_s = Rot("pssb", [psb[i][:] for i in range(3)], P, [0, 1, 2])
        gs = nlam = knl = None
        if isB:
            gs = A.alloc([128], F32)
            lv = A.alloc([256], F32)
            lj = A.alloc([64], F32)
            t0_r = Rot("t0", [A.alloc([128], F32) for _ in range(2)])
            ob_r = Rot("ob", [A.alloc([128], F32) for _ in range(2)])
            jb_r = Rot("jb", [A.alloc([128], F32) for _ in range(2)])
            dma(gs, g_sub[l, :, :], [], ["gs"])
            dma(lv, lam_v[l, :, :], [], ["lv"])
            lam_init = 0.8 - 0.6 * math.exp(-0.3 * l)
            P.op("act", lambda e: e.mul(out=gs, in_=gs, mul=float(1.0 - lam_init)), reads=["gs"], writes=["gs"])
            k1, s1 = "lam_s1", A.alloc([1], F32)
            k2, s2 = "lam_s2", A.alloc([1], F32)
            knl, nlam = "nlam", A.alloc([1], F32)
            tt("dve", lj, lv[:, 0:64], lv[:, 64:128], ALU.mult, ["lv"], ["lj"])
            P.op("dve", lambda e: e.reduce_sum(out=s1, in_=lj, axis=mybir.AxisListType.X), reads=["lj"], writes=[k1])
            tt("dve", lj, lv[:, 128:192], lv[:, 192:256], ALU.mult, ["lv", k1], ["lj"])
            P.op("dve", lambda e: e.reduce_sum(out=s2, in_=lj, axis=mybir.AxisListType.X), reads=["lj"], writes=[k2])
            act(s1, s1, AF.Exp, [k1], [k1])
            act(s2, s2, AF.Exp, [k2], [k2])
            stt(nlam, s2, float(-lam_init), s1, ALU.add, ALU.subtract, [k1, k2], [knl])
        if isB:
            regs = [psb[3 + i // 3][:, (i % 3) * 129:(i % 3) * 129 + 129] for i in range(8)]
            okeys = ["oacc%d" % (3 + i // 3) for i in range(8)]
        else:
            regs = [psb[3 + (i // 4)][:, (i % 4) * 65:(i % 4) * 65 + 65] for i in range(8)]
            okeys = ["oacc%d" % (3 + i // 4) for i in range(8)]
        for b_ in (3, 4, 5):
            P.bank_of["oacc%d" % b_] = b_
        steps = []

        def mk_loads(h, qk, qt, kk, kt):
            def f():
                for (dst, dk, qk_sel, so) in ((qt, qk, "Q", 0), (kt, kk, "K", 4096)):
                    for c in range(2):
                        for r in range(2):
                            if isB:
                                xn_, b0 = qk_sel + "B", r * 512 + (2 * c + h) * 128
                            else:
                                xn_, b0 = "%sC%d" % (qk_sel, c), r * 384 + h * 96
                            dma(stg[c][0:dr, so + r * SL:so + (r + 1) * SL], XG(xn_)[b0:b0 + dr, :], [xn_ + "g"], ["stg%d" % c], waw=False)
                    sel(dst[0:dr, :], dk, stg[0][0:dr, so:so + S], "stg0", stg[1][0:dr, so:so + S], "stg1", np_=dr)
            return f

        def mk_front(h, g, j, qk, qt, kk, kt, bufs, loads):
            qlo = max(4 * g, j)
            W = (4 * g + 4 - qlo) * 128

            def f():
                if loads is not None:
                    loads()
                if isB:
                    ek, et = bufs["e"]
                    off = (h * 128) * LB + (qlo - j) * 128 + 127
                    dma(et[:, 0:W], bass.AP(FB_D, off, [[LB - 1, 128], [1, W]]), ["FB_D"], [ek])
                for w, (sk_, sp_, tk, pt, mk, pm) in enumerate(bufs["w"]):
                    rows = slice(w * 64, (w + 1) * 64) if isB else slice(0, 96)
                    mm(sp_[:, 0:W], kt[rows, j * 128:(j + 1) * 128], qt[rows, qlo * 128:(4 * g + 4) * 128], True, True, [kk, qk], [sk_])
                    act(pt[:, 0:W], sp_[:, 0:W], AF.Exp, [sk_], [tk], scale=float(scale))
                    if isB:
                        tt("dve", pm[:, 0:W], pt[:, 0:W], et[:, 0:W], ALU.mult, [tk, ek], [mk])
                    elif qlo == j:
                        tt("dve", pt[:, 0:128], pt[:, 0:128], mkc, ALU.mult, [tk, "mkc"], [tk])
            return f

        def mk_back(h, g, j, bufs, rsel):
            qlo = max(4 * g, j)

            def f():
                for w, (sk_, sp_, tk, pt, mk, pm) in enumerate(bufs["w"]):
                    for qb in range(qlo, 4 * g + 4):
                        ri = rsel[w * 4 + (qb - 4 * g)] if isB else rsel[qb - 4 * g]
                        c0 = (qb - qlo) * 128
                        stf = (j == 0) and (okeys[ri] not in bufs["started"])
                        bufs["started"].add(okeys[ri])
                        mm(regs[ri], pm[:, c0:c0 + 128], Vt[:, j, h * (dv + 1):(h + 1) * (dv + 1)], stf, j == qb, [mk, "Vt"], [okeys[ri]], skip=True)
                if j == 4 * g + 3:
                    epilogue(h, g, rsel)
            return f

        def epilogue(h, g, rsel):
            for qi in range(4):
                qb = 4 * g + qi
                if isB:
                    r0_, r1_ = regs[rsel[qi]], regs[rsel[4 + qi]]
                    ok0, ok1 = okeys[rsel[qi]], okeys[rsel[4 + qi]]
                    ka, ra = small()
                    kb_, rb = small()
                    recip(ra, r0_[:, 128:129], [ok0], [ka])
                    recip(rb, r1_[:, 128:129], [ok1], [kb_])
                    tt("dve", rb, rb, nlam, ALU.mult, [kb_, knl], [kb_])
                    tk0, t0 = t0_r.next()
                    act(t0, r0_[:, 0:128], AF.Copy, [ok0, ka], [tk0], scale=ra)
                    obk, ob = ob_r.next()
                    stt(ob, r1_[:, 0:128], rb, t0, ALU.mult, ALU.add, [ok1, kb_, tk0], [obk])
                    jk, jb = jb_r.next()
                    krs, rs = rms_scale(ob, obk, 128, jb, jk)
                    stt(Ysb[:, qb, h * 128:(h + 1) * 128], ob, rs, gs, ALU.mult, ALU.mult, [obk, krs, "gs"], ["Ysb"], waw=False)
                else:
                    rg = regs[rsel[qi]]
                    ok0 = okeys[rsel[qi]]
                    ka, ra = small()
                    recip(ra, rg[:, 64:65], [ok0], [ka])
                    act(Ysb[:, qb, h * 64:(h + 1) * 64], rg[:, 0:64], AF.Copy, [ok0, ka], ["Ysb"], scale=ra, waw=False)

        gpar = 0
        nw = 2 if isB else 1
        for h in range(nh):
            qk, qt = q_r.next()
            kk, kt = k_r.next()
            loads = mk_loads(h, qk, qt, kk, kt)
            for g in range(NG):
                if isB:
                    rsel = list(range(8))
                else:
                    rsel = [(gpar % 2) * 4 + i for i in range(4)]
                    gpar += 1
                started = set()
                for j in range(4 * g + 4):
                    bufs = {"w": [], "started": started}
                    if isB:
                        bufs["e"] = e_r.next()
                    for w in range(nw):
                        sk_, sp_ = ps_s.next()
                        tk, pt = pt_r.next()
                        if isB:
                            mk, pm = pm_r.next()
                        else:
                            mk, pm = tk, pt
                        bufs["w"].append((sk_, sp_, tk, pt, mk, pm))
                    steps.append((mk_front(h, g, j, qk, qt, kk, kt, bufs, loads), mk_back(h, g, j, bufs, rsel)))
                    loads = None
        run_pipeline(steps)
        dma(XL(yname).ap().rearrange("(t p) c -> p t c", p=128), Ysb, ["Ysb"], [yname])
        allgather(yname)

    def phase3a(l, xsrc, xsrc_key):
        P.barrier()
        A.reset()
        wg = A.alloc([8, 3 * D], BF16)
        wbr = [A.alloc([4, D], BF16) for _ in range(3)]
        wo = A.alloc([8, D], BF16)
        gm = A.alloc([D], F32)
        bg = A.alloc([24], F32)
        dma(gm, g_mix[l, :, :], [], ["gm"])
        dma(bg, b_gate[l, :, :], [], ["bg"])
        for f2 in range(4):
            wload_cols(wg, lambda c0, f2=f2: "wg@%d" % f2, w_gate[l, :, :], 8, [(br * D + f2 * 256, br * D + (f2 + 1) * 256) for br in range(3)])
            for i, wsrc in enumerate((w_bra, w_brb, w_brc)):
                wload_cols(wbr[i], lambda c0, f2=f2: "wbr@%d" % f2, wsrc[l, :, :], 4, [(f2 * 256, (f2 + 1) * 256)])
        wload(wo, "wo", w_o[l, :, :], 8)
        rot = std_rots()
        xt_r = Rot("xt3", [A.alloc([D], F32) for _ in range(3)])
        xr_r = Rot("xr3", [A.alloc([D], F32) for _ in range(2)])
        hT_r = Rot("hT3", [A.alloc([8, 512], BF16) for _ in range(2)])
        yT_r = [Rot("yT%d" % i, [A.alloc([4, 512], BF16) for _ in range(2)]) for i in range(3)]
        yl_r = Rot("yl", [A.alloc([512], BF16) for _ in range(3)])
        ysg = [A.alloc([4, 512], BF16) for _ in range(2)]
        ylg = [A.alloc([512], BF16) for _ in range(2)]
        mT_r = Rot("mT", [A.alloc([8, 512], BF16)])
        gt_r = Rot("gt", [A.alloc([512], F32) for _ in range(3)])
        m_r = Rot("m", [A.alloc([512], F32) for _ in range(2)])
        t_r = Rot("t", [A.alloc([512], F32) for _ in range(2)])
        xo_r = Rot("xo", [A.alloc([D], F32) for _ in range(2)])
        mm_r = Rot("psm3", [psb[i][:] for i in range(8)], P, list(range(8)))
        for g in range(NGL):
            tok = slice(g * 512, (g + 1) * 512)
            hk, hT = hT_r.next()
            xts = []
            for t in range(4):
                xk, xt = xt_r.next()
                r0 = g * 512 + t * 128
                dma(xt, xsrc[r0:r0 + 128, :], [xsrc_key], [xk])
                norm_T(xt, xk, 8, D, gm, "gm", hT[:, :, t * 128:(t + 1) * 128], hk, rot)
                xts.append((xk, xt))
            yks = []
            yk, yT = yT_r[0].next()
            for c in range(2):
                dma(ysg[c], XG("YTA").ap().rearrange("(c p) s -> p c s", p=128)[:, :, c * SL + g * 512:c * SL + (g + 1) * 512], ["YTAg"], ["ysg%d" % c])
            sel(yT.rearrange("p c s -> p (c s)"), yk, ysg[0].rearrange("p c s -> p (c s)"), "ysg0", ysg[1].rearrange("p c s -> p (c s)"), "ysg1")
            yks.append((yk, yT))
            for bi, yname in enumerate(("YB", "YC")):
                yk, yT = yT_r[1 + bi].next()
                for t in range(4):
                    lk, yl = yl_r.next()
                    r0 = g * 512 + t * 128
                    for c in range(2):
                        for r in range(2):
                            t0_ = r * S + c * SL + r0
                            dma(ylg[c][:, r * 256:(r + 1) * 256], XG(yname)[t0_:t0_ + 128, :], [yname + "g"], ["ylg%d" % c], waw=False)
                    sel(yl, lk, ylg[0], "ylg0", ylg[1], "ylg1")
                    pk, pp = rot["pT"].next()
                    ppb = pp.bitcast(BF16)
                    for c in range(4):
                        tr(ppb[:, c * 128:(c + 1) * 128], yl[:, c * 128:(c + 1) * 128], [lk], [pk])
                    tcopy("dve", yT[:, :, t * 128:(t + 1) * 128], ppb[:, 0:512].rearrange("p (c t) -> p c t", t=128), [pk], [yk], waw=False)
                yks.append((yk, yT))
            mk, mT = mT_r.next()
            for fc in range(8):
                mkk, m = m_r.next()
                for br in range(3):
                    yk, yT = yks[br]
                    pgk, pg = mm_r.next()
                    col = br * D + fc * 128
                    for k in range(8):
                        mm(pg, wg[:, k, col:col + 128], hT[:, k, :], k == 0, k == 7, ["wg@%d" % (fc // 2), hk], [pgk])
                    pbk, pb = mm_r.next()
                    for k in range(4):
                        mm(pb, wbr[br][:, k, fc * 128:(fc + 1) * 128], yT[:, k, :], k == 0, k == 3, ["wbr@%d" % (fc // 2), yk], [pbk])
                    gk, gt = gt_r.next()
                    act(gt, pg, AF.Sigmoid, [pgk, "bg"], [gk], bias=bg[:, br * 8 + fc:br * 8 + fc + 1])
                    if br == 0:
                        tt("dve", m, gt, pb, ALU.mult, [gk, pbk], [mkk])
                    else:
                        tk, tt_ = t_r.next()
                        tt("dve", tt_, gt, pb, ALU.mult, [gk, pbk], [tk])
                        if br == 1:
                            tt("dve", m, m, tt_, ALU.add, [mkk, tk], [mkk])
                        else:
                            tt("dve", mT[:, fc, :], m, tt_, ALU.add, [mkk, tk], [mk], waw=False)
            for t in range(4):
                xk, xt = xr_r.next()
                ok_, xo = xo_r.next()
                r0 = g * 512 + t * 128
                dma(xt, xsrc[r0:r0 + 128, :], [xsrc_key], [xk])
                for cc in range(2):
                    pk, pp = mm_r.next()
                    for k in range(8):
                        mm(pp, mT[:, k, t * 128:(t + 1) * 128], wo[:, k, cc * 512:(cc + 1) * 512], k == 0, k == 7, ["wo", mk], [pk])
                    tt("dve", xo[:, cc * 512:(cc + 1) * 512], pp, xt[:, cc * 512:(cc + 1) * 512], ALU.add, [pk, xk], [ok_], waw=False)
                dma(XMID[r0:r0 + 128, :], xo, [ok_], ["XMID"], waw=False)

    def phase3b(l, last):
        P.barrier()
        A.reset()
        GT = 256
        wfg = A.alloc([8, FH], BF16)
        wfu = A.alloc([8, FH], BF16)
        wfd = A.alloc([22, D], BF16)
        gf = A.alloc([D], F32)
        dma(gf, g_ffn[l, :, :], [], ["gf"])
        gfin = None
        if last:
            gfin = A.alloc([D], F32)
            dma(gfin, g_fin[:, :], [], ["gfin"])
        for h4 in range(6):
            c0_, c1_ = h4 * 512, min(FH, (h4 + 1) * 512)
            wload_cols(wfg, lambda c0, h4=h4: "wf@%d" % h4, w_fg[l, :, :], 8, [(c0_, c1_)])
            wload_cols(wfu, lambda c0, h4=h4: "wf@%d" % h4, w_fu[l, :, :], 8, [(c0_, c1_)])
        wload(wfd, "wfd", w_fd[l, :, :], 22)
        rot = std_rots()
        xt_r = Rot("xt4", [A.alloc([D], F32) for _ in range(4)])
        hT_r = Rot("hT4", [A.alloc([8, GT], BF16) for _ in range(2)])
        aT_r = Rot("aT", [A.alloc([22, GT], BF16)])
        sg_r = Rot("sg", [A.alloc([GT], F32) for _ in range(2)])
        xo_r = Rot("xo4", [A.alloc([D], F32) for _ in range(2)])
        fo_r = Rot("fo4", [A.alloc([D], F32) for _ in range(2)])
        mm_r = Rot("psm4", [psb[i][:] for i in range(6)], P, list(range(6)))
        nt = GT // 128
        for g in range(SL // GT):
            hk, hT = hT_r.next()
            xts = []
            for t in range(nt):
                xk, xt = xt_r.next()
                r0 = g * GT + t * 128
                dma(xt, XMID[r0:r0 + 128, :], ["XMID"], [xk])
                norm_T(xt, xk, 8, D, gf, "gf", hT[:, :, t * 128:(t + 1) * 128], hk, rot)
                xts.append((xk, xt))
            ak, aT = aT_r.next()
            for hc in range(22):
                pgk, pg = mm_r.next()
                for k in range(8):
                    mm(pg[:, 0:GT], wfg[:, k, hc * 128:(hc + 1) * 128], hT[:, k, :], k == 0, k == 7, ["wf@%d" % (hc // 4), hk], [pgk])
                puk, pu = mm_r.next()
                for k in range(8):
                    mm(pu[:, 0:GT], wfu[:, k, hc * 128:(hc + 1) * 128], hT[:, k, :], k == 0, k == 7, ["wf@%d" % (hc // 4), hk], [puk])
                sk, sg = sg_r.next()
                act(sg, pg[:, 0:GT], AF.Silu, [pgk], [sk])
                tt("dve", aT[:, hc, :], sg, pu[:, 0:GT], ALU.mult, [sk, puk], [ak], waw=False)
            for t in range(nt):
                xk, xt = xts[t]
                ok_, xo = xo_r.next()
                r0 = g * GT + t * 128
                for cc in range(2):
                    pk, pp = mm_r.next()
                    for hc in range(22):
                        mm(pp, aT[:, hc, t * 128:(t + 1) * 128], wfd[:, hc, cc * 512:(cc + 1) * 512], hc == 0, hc == 21, ["wfd", ak], [pk])
                    tt("dve", xo[:, cc * 512:(cc + 1) * 512], pp, xt[:, cc * 512:(cc + 1) * 512], ALU.add, [pk, xk], [ok_], waw=False)
                if not last:
                    dma(XRES[r0:r0 + 128, :], xo, [ok_], ["XRES"], waw=False)
                else:
                    jk, junk = rot["junk"].next()
                    kr, rs = rms_scale(xo, ok_, D, junk, jk)
                    fk, fo = fo_r.next()
                    stt(fo, xo, rs, gfin, ALU.mult, ALU.mult, [ok_, kr, "gfin"], [fk])
                    dma(out_d[r0:r0 + 128, :], fo, [fk], ["out"], waw=False)

    phases = build.phases
    if "0" in phases:
        phase0()
    for l in range(L):
        xsrc, xkey = (x_in, "x") if l == 0 else (XRES, "XRES")
        if "1" in phases:
            phase1(l, xsrc, xkey)
        if "a" in phases:
            phase2a(l)
        if "b" in phases:
            phase2bc(l, "B")
        if "c" in phases:
            phase2bc(l, "C")
        if "3" in phases:
            phase3a(l, xsrc, xkey)
        if "4" in phases:
            phase3b(l, l == L - 1)
    finals = [("dma", "out")]
    if dbg:
        P.barrier()
        for n_ in ("QA", "KA", "VA0", "QB", "VB1", "QC0", "KC1", "VC0", "YTA", "YB", "YC"):
            src = XG(n_)
            dd = nc.dram_tensor("dbg_" + n_, list(src.shape), BF16, kind="ExternalOutput")
            rows = src.shape[0]
            for r0 in range(0, rows, 2048):
                r1 = min(rows, r0 + 2048)
                dma(dd[r0:r1, :], src[r0:r1, :], [n_ + "g"], ["dbg_" + n_], waw=False)
            finals.append(("dma", "dbg_" + n_))
    if dbg:
        finals += [("dma", n) for n in ("XMID", "XRES")]
    P.emit(final_wait_streams=finals)
    st.close()
    return nc


build.phases = "01abc34"


def host_inputs(S, L, x_loc, p, rank):
    f = np.float32
    SL = S // 2
    LB = ((S + 127 + 383) // 384) * 384
    rep = lambda v: np.ascontiguousarray(np.broadcast_to(v[:, None, :], (v.shape[0], 128, v.shape[1])).astype(f))
    w_in = np.ascontiguousarray(p["w_in"][:L])
    kr = w_in[:, :, 4096:4128]
    w_krs = np.ascontiguousarray(np.concatenate([kr[:, :, 16:32], kr[:, :, 0:16]], axis=-1))
    wuq = p["w_uq"][:L].reshape(L, 768, 8, 96)
    w_uqp = np.ascontiguousarray(np.concatenate([wuq[..., 64:96], wuq[..., 0:64]], axis=-1).reshape(L, 768, 768))
    w_uqs = np.ascontiguousarray(np.concatenate([wuq[..., 80:96], wuq[..., 64:80]], axis=-1).reshape(L, 768, 256))
    wukv = p["w_ukv"][:L].reshape(L, 256, 8, 128)
    w_ukvp = np.ascontiguousarray(np.concatenate([wukv[..., 0:64].reshape(L, 256, 512), wukv[..., 64:128].reshape(L, 256, 512)], axis=-1))
    lam_v = np.concatenate([p["lambda_q1"][:L], p["lambda_k1"][:L], p["lambda_q2"][:L], p["lambda_k2"][:L]], axis=-1)
    bgt = np.ascontiguousarray(p["b_gate"][:L].reshape(L, 24, 128).transpose(0, 2, 1))
    tab = p["rel_bias_table"].astype(f)
    tabaug = np.concatenate([tab, np.full((1, 12), -30000.0, f)], axis=0)
    cols = [4 * rank + i for i in range(4)] + [8 + 2 * rank + i for i in range(2)]
    tabrep = np.ascontiguousarray(np.broadcast_to(tabaug.T[cols][:, :, None], (6, 33, 128)).astype(f))
    dist = np.arange(LB) - 127
    bk = np.where(dist >= 0, t5_bucket_np(dist), 32)
    oh_b = np.zeros((33, LB), f)
    oh_b[bk, np.arange(LB)] = 1.0
    oh_a = np.zeros((33, 3 * LA), f)
    for pi, (win_, dil) in enumerate(PATS):
        step = np.arange(LA) - 127
        ok = (step >= 0) & (step <= 128)
        b_ = np.where(ok, t5_bucket_np(step * dil), 32)
        oh_a[b_, pi * LA + np.arange(LA)] = 1.0
    pos = np.arange(rank * SL, (rank + 1) * SL).astype(f)
    inv = (np.float32(10000.0) ** (-np.arange(0, 32, 2, dtype=f) / np.float32(32))).astype(f)
    ang = (pos[:, None] * inv[None, :]).astype(f)
    cos, sin = np.cos(ang).astype(f).T, np.sin(ang).astype(f).T
    cos32 = np.ascontiguousarray(np.concatenate([cos, cos], axis=0))
    sin32 = np.ascontiguousarray(np.concatenate([-sin, sin], axis=0))
    kk = np.arange(128)
    m = {
        "x": np.ascontiguousarray(x_loc.astype(f)),
        "w_in": w_in, "w_krs": w_krs, "w_uqp": w_uqp, "w_uqs": w_uqs, "w_ukvp": w_ukvp,
        "w_gate": np.ascontiguousarray(p["w_gate"][:L]),
        "w_br_a": np.ascontiguousarray(p["w_br_a"][:L]), "w_br_b": np.ascontiguousarray(p["w_br_b"][:L]),
        "w_br_c": np.ascontiguousarray(p["w_br_c"][:L]), "w_o": np.ascontiguousarray(p["w_o"][:L]),
        "w_ffn_gate": np.ascontiguousarray(p["w_ffn_gate"][:L]), "w_ffn_up": np.ascontiguousarray(p["w_ffn_up"][:L]),
        "w_ffn_down": np.ascontiguousarray(p["w_ffn_down"][:L]),
        "g_mix": rep(p["ln_mix_g"][:L]), "g_q": rep(p["mla_q_norm_g"][:L]), "g_kv": rep(p["mla_kv_norm_g"][:L]),
        "g_ffn": rep(p["ln_ffn_g"][:L]), "g_fin": np.ascontiguousarray(np.broadcast_to(p["final_norm_g"][None, :], (128, D)).astype(f)),
        "g_sub": rep(p["diff_subln_g"][:L]), "lam_v": rep(lam_v), "b_gate": bgt.astype(f),
        "tabrep": tabrep, "oh_b": oh_b, "oh_a": oh_a, "cos32": cos32, "sin32": sin32,
        "ident": np.eye(128, dtype=f), "maskc": (kk[None, :] >= kk[:, None]).astype(f),
        "msel": np.ascontiguousarray(np.broadcast_to(np.eye(2, dtype=f)[rank][None, :], (128, 2))),
    }
    return m


_NC_CACHE = {}


def kernel(**inputs):
    p = {k: np.asarray(v) for k, v in inputs.items()}
    x = p["x"]
    B, S, _ = x.shape
    SL = S // 2
    L = p["w_in"].shape[0]
    key = (S, L)
    if key not in _NC_CACHE:
        _NC_CACHE[key] = build(S, L, ncores=2 * B)
    nc = _NC_CACHE[key]
    shared = [host_inputs(S, L, x[0, r * SL:(r + 1) * SL], p, r) for r in range(2)]
    in_maps = []
    for c in range(2 * B):
        b, r = c // 2, c % 2
        m = dict(shared[r])
        m["x"] = np.ascontiguousarray(x[b, r * SL:(r + 1) * SL].astype(np.float32))
        in_maps.append(m)
    res = run_bass_kernel_spmd(nc, in_maps, core_ids=list(range(2 * B)))
    out = np.empty((B, S, D), np.float32)
    for c in range(2 * B):
        b, r = c // 2, c % 2
        out[b, r * SL:(r + 1) * SL] = np.asarray(res.results[c]["out"])
    return out
```

```python
import math
import numpy as np
from contextlib import ExitStack
import concourse.bass as bass
import concourse.mybir as mybir
from concourse.bass_utils import run_bass_kernel_spmd

F32 = mybir.dt.float32
BF16 = mybir.dt.bfloat16
AF = mybir.ActivationFunctionType
ALU = mybir.AluOpType

D = 1024
DIN = 4128
FH = 2816
LA = 384
EPS = 1e-6
PATS = ((128, 1), (512, 4), (2048, 16))


class _Sem:
    def __init__(self, name):
        self.name = name
        self.h = None


class _Op:
    __slots__ = ("eng", "fn", "deps", "dma", "stream", "needed", "ev", "cc")

    def __init__(self, eng, fn, dma, stream):
        self.cc = False
        self.eng = eng
        self.fn = fn
        self.dma = dma
        self.stream = stream
        self.deps = set()
        self.needed = False
        self.ev = None


class Prog:
    def __init__(self, nc):
        self.nc = nc
        self.ops = []
        self.lastw = {}
        self.readers = {}
        self.sems = []
        self.last_of_stream = {}
        self.bar = None
        self.bar_done = set()
        self.bar_positions = []
        self.bank_of = {}
        self.bank_last = {}

    def barrier(self):
        self.bar = {k: v for k, v in self.last_of_stream.items() if not (isinstance(k, tuple) and k[0] == "cc")}
        self.bar_done = set()
        self.bar_positions.append(len(self.ops))

    def op(self, eng, fn, reads=(), writes=(), dma=False, waw=True, cc=False):
        i = len(self.ops)
        stream = ("dma", writes[0]) if dma else eng
        if cc:
            stream = ("cc", writes[0])
        o = _Op(eng, fn, dma, stream)
        o.cc = cc
        if self.bar is not None and eng not in self.bar_done:
            self.bar_done.add(eng)
            for s, j in self.bar.items():
                if s == eng and not dma and eng == "pe":
                    continue
                o.deps.add(j)
        for r in reads:
            for w in self.lastw.get(r, {}).values():
                self._dep(o, w, "raw")
        for r in writes:
            if waw:
                for w in self.lastw.get(r, {}).values():
                    self._dep(o, w, "waw")
            for x in self.readers.get(r, {}).values():
                self._dep(o, x, "war")
        for r in reads:
            self.readers.setdefault(r, {})[stream] = i
        for r in writes:
            if waw:
                self.lastw[r] = {stream: i}
            else:
                self.lastw.setdefault(r, {})[stream] = i
            self.readers[r] = {}
        if not dma:
            banks = set()
            for r in list(reads) + list(writes):
                b = self.bank_of.get(r)
                if b is not None:
                    banks.add(b)
            for b in banks:
                bl = self.bank_last.setdefault(b, {})
                for f_eng, j in bl.items():
                    if f_eng != eng:
                        o.deps.add(j)
                bl[eng] = i
        self.last_of_stream[stream] = i
        self.ops.append(o)
        return i

    def _dep(self, o, j, kind):
        p = self.ops[j]
        if not p.dma and not o.dma and p.eng == o.eng:
            if o.eng == "pe":
                return
            if kind == "war":
                return
        o.deps.add(j)

    def emit(self, final_wait_streams=()):
        nc = self.nc
        ops = self.ops
        MAXV = 8000
        for o in ops:
            for j in o.deps:
                ops[j].needed = True
        bars = sorted(set(self.bar_positions))
        seg_of = []
        bi = 0
        for i in range(len(ops)):
            while bi < len(bars) and bars[bi] <= i:
                bi += 1
            seg_of.append(bi)
        cnt = {}
        for i, o in enumerate(ops):
            if o.dma and not o.cc:
                k = (seg_of[i], o.stream)
                cnt[k] = cnt.get(k, 0) + 1
        free = []
        ccmap = {}
        dmap = {}
        emap = {}
        last_ev = {}

        def new_phys():
            s_ = _Sem("s%d" % len(self.sems))
            self.sems.append(s_)
            return [s_, 0]

        cur_seg = 0
        for i, o in enumerate(ops):
            if seg_of[i] != cur_seg:
                cur_seg = seg_of[i]
                for ph in dmap.values():
                    free.append(ph)
                dmap = {}
            if o.cc:
                ph = ccmap.get(o.stream)
                if ph is None:
                    ph = new_phys()
                    ccmap[o.stream] = ph
                ph[1] += 1
                o.ev = (ph[0], ph[1])
                last_ev[o.stream] = o.ev
            elif o.dma:
                ph = dmap.get(o.stream)
                if ph is None:
                    need = 16 * cnt[(cur_seg, o.stream)]
                    for fi, cand in enumerate(free):
                        if cand[1] + need <= MAXV:
                            ph = free.pop(fi)
                            break
                    if ph is None:
                        ph = new_phys()
                    dmap[o.stream] = ph
                ph[1] += 16
                o.ev = (ph[0], ph[1])
                last_ev[o.stream] = o.ev
            elif o.needed:
                ph = emap.get(o.stream)
                if ph is None or ph[1] + 1 > MAXV:
                    ph = new_phys()
                    emap[o.stream] = ph
                ph[1] += 1
                o.ev = (ph[0], ph[1])
        per_eng = {}
        for o in ops:
            per_eng.setdefault(o.eng, []).append(o)
        finals = [last_ev[s] for s in final_wait_streams if s in last_ev]
        self.nsem = len(self.sems)
        with ExitStack() as st:
            for s in self.sems:
                s.h = st.enter_context(nc.semaphore(s.name))
            block = st.enter_context(nc.Block())

            def run(eng_name, e):
                seen = {}
                for o in per_eng.get(eng_name, []):
                    need = {}
                    for j in o.deps:
                        s, v = ops[j].ev
                        if need.get(s, 0) < v:
                            need[s] = v
                    for s, v in need.items():
                        if seen.get(s, 0) < v:
                            e.wait_ge(s.h, v)
                            seen[s] = v
                    ins = o.fn(e)
                    if o.cc:
                        ins.then_inc(o.ev[0].h)
                    elif o.ev is not None:
                        ins.then_inc(o.ev[0].h, 16 if o.dma else 1)
                if eng_name == "sp":
                    for s, v in finals:
                        e.wait_ge(s.h, v)

            @block.tensor
            def _(e):
                run("pe", e)

            @block.scalar
            def _(e):
                run("act", e)

            @block.vector
            def _(e):
                run("dve", e)

            @block.gpsimd
            def _(e):
                run("pool", e)

            @block.sync
            def _(e):
                run("sp", e)


class Arena:
    def __init__(self, base, nbytes):
        self.base = base
        self.nbytes = nbytes
        self.off = 0

    def reset(self):
        self.off = 0

    def alloc(self, shape, dt):
        n = 1
        for s in shape:
            n *= s
        nb = n * (4 if dt == F32 else 2)
        nb = (nb + 63) // 64 * 64
        assert self.off + nb <= self.nbytes, (self.off, nb, self.nbytes)
        v = self.base[:, self.off // 2:(self.off + nb) // 2]
        self.off += nb
        if dt == F32:
            v = v.bitcast(F32)
        v = v[:, 0:n]
        if len(shape) == 2:
            v = v.rearrange("p (a b) -> p a b", b=shape[1])
        elif len(shape) == 3:
            v = v.rearrange("p (a b c) -> p a b c", b=shape[1], c=shape[2])
        return v


class Rot:
    def __init__(self, name, aps, P=None, banks=None):
        self.name = name
        self.aps = aps
        self.i = 0
        if banks is not None:
            for k, b in enumerate(banks):
                P.bank_of["%s#%d" % (name, k)] = b

    def next(self):
        k = self.i % len(self.aps)
        self.i += 1
        return "%s#%d" % (self.name, k), self.aps[k]


def t5_bucket_np(dist):
    dist = np.maximum(dist, 0)
    exact = 16
    lr = np.log(np.maximum(dist, 1).astype(np.float32) / np.float32(exact)) / np.float32(math.log(2048 / exact))
    large = np.minimum(exact + (lr.astype(np.float32) * np.float32(16)).astype(np.int32), 31)
    return np.where(dist < exact, dist, large)


def t5_bucket_exact(dist):
    return t5_bucket_np(np.asarray(dist))


FM_QA, FM_KA, FM_QB, FM_KB, FM_QC, FM_KC, FM_ROWS = 0, 512, 1024, 1536, 2048, 2816, 3584
TM_VA, TM_VB, TM_VC, TM_COLS = 0, 520, 1036, 1556


def build(S, L, dbg=False, ncores=8):
    NT = S // 128
    NG = S // 512
    SL = S // 2
    NGL = SL // 512
    groups = [[2 * i, 2 * i + 1] for i in range(ncores // 2)]
    LB = ((S + 127 + 383) // 384) * 384
    nc = bass.Bass("TRN2", target_bir_lowering=False)
    P = Prog(nc)

    def din(name, shape, dt=F32):
        return nc.dram_tensor(name, list(shape), dt, kind="ExternalInput")

    def dscr(name, shape, dt):
        return nc.dram_tensor(name, list(shape), dt, kind="ExternalOutput" if dbg else "Internal")

    x_in = din("x", [SL, D])
    w_in = din("w_in", [L, D, DIN])
    w_krs = din("w_krs", [L, D, 32])
    w_uqp = din("w_uqp", [L, 768, 768])
    w_uqs = din("w_uqs", [L, 768, 256])
    w_ukvp = din("w_ukvp", [L, 256, 1024])
    w_gate = din("w_gate", [L, D, 3 * D])
    w_bra = din("w_br_a", [L, 512, D])
    w_brb = din("w_br_b", [L, 512, D])
    w_brc = din("w_br_c", [L, 512, D])
    w_o = din("w_o", [L, D, D])
    w_fg = din("w_ffn_gate", [L, D, FH])
    w_fu = din("w_ffn_up", [L, D, FH])
    w_fd = din("w_ffn_down", [L, FH, D])
    g_mix = din("g_mix", [L, 128, D])
    g_q = din("g_q", [L, 128, 768])
    g_kv = din("g_kv", [L, 128, 256])
    g_ffn = din("g_ffn", [L, 128, D])
    g_fin = din("g_fin", [128, D])
    g_sub = din("g_sub", [L, 128, 128])
    lam_v = din("lam_v", [L, 128, 256])
    b_gate = din("b_gate", [L, 128, 24])
    tabrep = din("tabrep", [6, 33, 128])
    oh_b = din("oh_b", [33, LB])
    oh_a = din("oh_a", [33, 3 * LA])
    cos_d = din("cos32", [32, SL])
    sin_d = din("sin32", [32, SL])
    ident_d = din("ident", [128, 128])
    maskc_d = din("maskc", [128, 128])
    msel_d = din("msel", [128, 2])
    out_d = nc.dram_tensor("out", [SL, D], F32, kind="ExternalOutput")

    def dint(name, shape, dt):
        return nc.dram_tensor(name, list(shape), dt)

    XT = {}

    def xbuf(name, rows, cols):
        XT[name] = (dint(name, [rows, cols], BF16), dint(name + "g", [2 * rows, cols], BF16), rows)

    for n_ in ("QA", "KA", "QB", "KB"):
        xbuf(n_, 512, SL)
    for n_ in ("QC0", "QC1", "KC0", "KC1"):
        xbuf(n_, 384, SL)
    for n_ in ("VA0", "VA1", "VC0", "VC1"):
        xbuf(n_, SL, 260)
    for n_ in ("VB0", "VB1"):
        xbuf(n_, SL, 258)
    xbuf("YTA", 256, S)
    xbuf("YB", S, 256)
    xbuf("YC", S, 256)

    def XL(name):
        return XT[name][0]

    def XG(name):
        return XT[name][1]

    XMID = dscr("XMID", [SL, D], F32)
    XRES = dscr("XRES", [SL, D], F32)
    FB_D = dint("FB_D", [2, 128, LB], BF16)
    FA_D = dint("FA_D", [4, 3, 128, LA], BF16)

    st = ExitStack()
    ARENA_B = 206 * 1024
    arena_t = st.enter_context(nc.sbuf_tensor("arena", [128, ARENA_B // 2], BF16))
    A = Arena(arena_t, ARENA_B)
    idb_t = st.enter_context(nc.sbuf_tensor("idb", [128, 128], BF16))
    mkc_t = st.enter_context(nc.sbuf_tensor("mkc", [128, 128], BF16))
    onesf_t = st.enter_context(nc.sbuf_tensor("onesf", [128, 64], F32))
    sm_t = st.enter_context(nc.sbuf_tensor("smalls", [128, 64], F32))
    msel_t = st.enter_context(nc.sbuf_tensor("msel_sb", [128, 2], F32))
    msel = msel_t[:]
    idb = idb_t[:]
    mkc = mkc_t[:]
    onesf = onesf_t[:]
    psb = [st.enter_context(nc.psum_tensor("ps%d" % i, [128, 512], F32)) for i in range(8)]

    sm_i = [0]

    def small():
        k = sm_i[0] % 64
        sm_i[0] += 1
        return "sm#%d" % k, sm_t[:, k:k + 1]

    def dma(out, in_, reads, writes, eng="sp", waw=True):
        P.op(eng, lambda e: e.dma_start(out=out, in_=in_), reads=reads, writes=writes, dma=True, waw=waw)

    def sel(dst, dkey, c0, k0, c1, k1, np_=128, waw=True):
        act(c1, c1, AF.Copy, [k1, "msel"], [k1], scale=msel[0:np_, 1:2])
        stt(dst, c0, msel[0:np_, 0:1], c1, ALU.mult, ALU.add, [k0, k1, "msel"], [dkey], waw=waw)

    def allgather(name):
        src, dst = XL(name), XG(name)
        P.op("pool", lambda e: e.collective_compute("AllGather", ALU.bypass, replica_groups=groups, ins=[src.ap().opt()], outs=[dst.ap().opt()]),
             reads=[name], writes=[name + "g"], cc=True)

    def mm(out, lhsT, rhs, start, stop, reads, writes, skip=False):
        if skip:
            P.op("pe", lambda e: e.matmul(out, lhsT=lhsT, rhs=rhs, start=start, stop=stop, skip_group_check=True), reads=reads, writes=writes)
        else:
            P.op("pe", lambda e: e.matmul(out, lhsT=lhsT, rhs=rhs, start=start, stop=stop), reads=reads, writes=writes)

    def tr(out, in_, reads, writes):
        P.op("pe", lambda e: e.transpose(out=out, in_=in_, identity=idb), reads=list(reads) + ["idb"], writes=writes)

    def act(out, in_, func, reads, writes, scale=None, bias=None, accum=None, waw=True):
        kw = {}
        if scale is not None:
            kw["scale"] = scale
        if bias is not None:
            kw["bias"] = bias
        if accum is not None:
            kw["accum_out"] = accum
        P.op("act", lambda e: e.activation(out=out, in_=in_, func=func, **kw), reads=reads, writes=writes, waw=waw)

    def tt(eng, out, in0, in1, op, reads, writes, waw=True):
        P.op(eng, lambda e: e.tensor_tensor(out=out, in0=in0, in1=in1, op=op), reads=reads, writes=writes, waw=waw)

    def stt(out, in0, scalar, in1, op0, op1, reads, writes, waw=True):
        P.op("dve", lambda e: e.scalar_tensor_tensor(out=out, in0=in0, scalar=scalar, in1=in1, op0=op0, op1=op1), reads=reads, writes=writes, waw=waw)

    def tcopy(eng, out, in_, reads, writes, waw=True):
        P.op(eng, lambda e: e.tensor_copy(out=out, in_=in_), reads=reads, writes=writes, waw=waw)

    def recip(out, in_, reads, writes):
        P.op("dve", lambda e: e.reciprocal(out=out, in_=in_), reads=reads, writes=writes)

    def tsadd(out, in0, c, reads, writes):
        P.op("dve", lambda e: e.tensor_scalar(out=out, in0=in0, scalar1=c, scalar2=None, op0=ALU.add), reads=reads, writes=writes)

    def memset(eng, ap, v, writes):
        P.op(eng, lambda e: e.memset(ap, v), writes=writes)

    def wload(dst3, dst_key, src2d, nchunk):
        for c in range(nchunk):
            dma(dst3[:, c, :], src2d[c * 128:(c + 1) * 128, :], [], [dst_key], eng="pool", waw=False)

    def wload_cols(dst3, key_fn, src2d, nchunk, blocks):
        for (c0, c1) in blocks:
            for k in range(nchunk):
                dma(dst3[:, k, c0:c1], src2d[k * 128:(k + 1) * 128, c0:c1], [], [key_fn(c0)], eng="pool", waw=False)

    evac = [0]

    def evac_copy(out, in_, reads, writes, waw=True):
        evac[0] += 1
        if evac[0] % 2:
            act(out, in_, AF.Copy, reads, writes, waw=waw)
        else:
            tcopy("dve", out, in_, reads, writes, waw=waw)

    def rms_scale(src, skey, Dn, junk, jk):
        sk, ss = small()
        rk, rs = small()
        act(junk, src, AF.Square, [skey], [jk, sk], scale=float(Dn) ** -0.5, accum=ss)
        tsadd(ss, ss, EPS, [sk], [sk])
        act(ss, ss, AF.Sqrt, [sk], [sk])
        recip(rs, ss, [sk], [rk])
        return rk, rs

    def phase0():
        A.reset()
        idf = A.alloc([128], F32)
        mkf = A.alloc([128], F32)
        dma(idf, ident_d[:, :], [], ["idf"])
        dma(mkf, maskc_d[:, :], [], ["mkf"])
        dma(msel, msel_d[:, :], [], ["msel"])
        tcopy("dve", idb, idf, ["idf"], ["idb"])
        tcopy("dve", mkc, mkf, ["mkf"], ["mkc"])
        memset("dve", onesf, 1.0, ["onesf"])
        ohb = A.alloc([LB], F32)
        oha = A.alloc([3 * LA], F32)
        dma(ohb[0:33], oh_b[:, :], [], ["ohb"])
        dma(oha[0:33], oh_a[:, :], [], ["oha"])
        tabs = Rot("tab", [A.alloc([128], F32) for _ in range(2)])
        stg = Rot("fstg", [A.alloc([LA], BF16) for _ in range(3)])
        psr = Rot("ps", [psb[i][:] for i in range(4)], P, [0, 1, 2, 3])
        for h in range(6):
            tk, tb = tabs.next()
            dma(tb[0:33], tabrep[h, :, :], [], [tk])
            if h < 4:
                chunks = [(oha, "oha", p * LA, FA_D[h, p, :, :], "FA_D") for p in range(3)]
            else:
                chunks = [(ohb, "ohb", c * LA, FB_D[h - 4, :, c * LA:(c + 1) * LA], "FB_D") for c in range(LB // LA)]
            for (src, skey, off, dst, dname) in chunks:
                pk, pp = psr.next()
                mm(pp[:, 0:LA], tb[0:33], src[0:33, off:off + LA], True, True, [tk, skey], [pk])
                sk, sg = stg.next()
                act(sg, pp[:, 0:LA], AF.Exp, [pk], [sk])
                dma(dst, sg, [sk], [dname], waw=False)

    def norm_T_front(src, skey, C, Dn, g_ap, gkey, rot):
        jk, junk = rot["junk"].next()
        xk, xn = rot["xn"].next()
        W = C * 128
        rk, rs = rms_scale(src, skey, Dn, junk[:, 0:W], jk)
        stt(xn[:, 0:W], src, rs, g_ap, ALU.mult, ALU.mult, [skey, rk, gkey], [xk])
        return (xk, xn, C)

    def norm_T(src, skey, C, Dn, g_ap, gkey, dst3, dkey, rot):
        norm_T_back(norm_T_front(src, skey, C, Dn, g_ap, gkey, rot), dst3, dkey, rot)

    def norm_T_back(state, dst3, dkey, rot):
        xk, xn, C = state
        W = C * 128
        pk, pp = rot["pT"].next()
        ppb = pp.bitcast(BF16)
        for c in range(C):
            tr(ppb[:, c * 128:(c + 1) * 128], xn[:, c * 128:(c + 1) * 128], [xk], [pk])
        act(dst3, ppb[:, 0:W].rearrange("p (c t) -> p c t", t=128), AF.Copy, [pk], [dkey], waw=False)

    def std_rots():
        return {
            "junk": Rot("junk", [A.alloc([1024], BF16)]),
            "xn": Rot("xn", [A.alloc([1024], BF16) for _ in range(4)]),
            "pT": Rot("psT", [psb[6][:], psb[7][:]], P, [6, 7]),
        }

    def phase1(l, xsrc, xsrc_key):
        P.barrier()
        A.reset()
        win = A.alloc([8, DIN], BF16)
        wkrs = A.alloc([8, 32], BF16)
        wuq = A.alloc([6, 768], BF16)
        wuqs = A.alloc([6, 256], BF16)
        wukv = A.alloc([2, 1024], BF16)
        gm = A.alloc([D], F32)
        gq = A.alloc([768], F32)
        gkv = A.alloc([256], F32)
        dma(gm, g_mix[l, :, :], [], ["gm"])
        dma(gq, g_q[l, :, :], [], ["gq"])
        dma(gkv, g_kv[l, :, :], [], ["gkv"])
        wkey = lambda c0: "win@%d" % (c0 // 512)
        wload_cols(win, wkey, w_in[l, :, :], 8, [(0, 512), (512, 1024), (1024, 1536), (1536, 2048), (2048, 2560), (2560, 3072), (4096, 4128), (3072, 3584), (3584, 4096)])
        wload(wkrs, "wkrs", w_krs[l, :, :], 8)
        wload(wuq, "wuq", w_uqp[l, :, :], 6)
        wload(wuqs, "wuqs", w_uqs[l, :, :], 6)
        wload(wukv, "wukv", w_ukvp[l, :, :], 2)
        rot = std_rots()
        xt_r = Rot("xt", [A.alloc([D], F32) for _ in range(2)])
        hT_r = Rot("hT", [A.alloc([8, 512], BF16) for _ in range(NGL)])
        cq_r = Rot("cq", [A.alloc([768], F32) for _ in range(2)])
        ckv_r = Rot("ckv", [A.alloc([256], F32) for _ in range(2)])
        cqT_r = Rot("cqT", [A.alloc([6, 512], BF16)])
        ckvT_r = Rot("ckvT", [A.alloc([2, 512], BF16)])
        fst_r = Rot("fst", [A.alloc([512], BF16) for _ in range(4)])
        va_r = Rot("vast", [A.alloc([8, 65], BF16) for _ in range(2)])
        vc_r = Rot("vcst", [A.alloc([8, 65], BF16) for _ in range(2)])
        vb_r = Rot("vbst", [A.alloc([4, 129], BF16) for _ in range(2)])
        cs_r = Rot("cs", [A.alloc([512], F32) for _ in range(2)])
        sn_r = Rot("sn", [A.alloc([512], F32) for _ in range(2)])
        t1_r = Rot("t1", [A.alloc([512], F32) for _ in range(2)])
        t2_r = Rot("t2", [A.alloc([512], F32) for _ in range(2)])
        mm_r = Rot("psm", [psb[i][:] for i in range(6)], P, list(range(6)))
        for r_ in (va_r, vc_r, vb_r):
            for i_, ap_ in enumerate(r_.aps):
                memset("pool", ap_, 1.0, ["%s#%d" % (r_.name, i_)])

        def rope_evac(pm, pmk, psw, pswk, dst, dkey, ck, cs, sk, sn):
            k1, t1 = t1_r.next()
            k2, t2 = t2_r.next()
            tt("dve", t1[0:32], pm[0:32, :], cs[0:32], ALU.mult, [pmk, ck], [k1])
            tt("dve", t2[0:32], psw[0:32, :], sn[0:32], ALU.mult, [pswk, sk], [k2])
            tt("dve", dst, t1[0:32], t2[0:32], ALU.add, [k1, k2], [dkey], waw=False)

        def tokmm(hT, hk, ts_, col, n, pp, pk):
            for k in range(8):
                mm(pp[:, 0:n], hT[:, k, ts_], win[:, k, col:col + n], k == 0, k == 7, [wkey(col), hk], [pk])

        def fm_proj(g, hk, hT, pairs):
            tok = slice(g * 512, (g + 1) * 512)
            for (c0, dname) in pairs:
                for c in range(4):
                    pk, pp = mm_r.next()
                    col = c0 + c * 128
                    for k in range(8):
                        mm(pp, win[:, k, col:col + 128], hT[:, k, :], k == 0, k == 7, [wkey(col), hk], [pk])
                    fk, fs = fst_r.next()
                    evac_copy(fs, pp, [pk], [fk])
                    dma(XL(dname)[c * 128:(c + 1) * 128, tok], fs, [fk], [dname], waw=False)

        def tm_v(g, hk, hT, col, rot_, nhh, dd, pre):
            for t in range(4):
                ts_ = slice(t * 128, (t + 1) * 128)
                r0 = g * 512 + t * 128
                pk, pp = mm_r.next()
                tokmm(hT, hk, ts_, col, 512, pp, pk)
                vk, vs = rot_.next()
                evac_copy(vs[:, :, 0:dd], pp.rearrange("p (h d) -> p h d", d=dd), [pk], [vk])
                for c_ in range(2):
                    dma(XL("%s%d" % (pre, c_))[r0:r0 + 128, :], vs[:, nhh * c_:nhh * c_ + nhh, :].rearrange("p h d -> p (h d)"), [vk], ["%s%d" % (pre, c_)], waw=False)

        hTs = []
        for g in range(NGL):
            hk, hT = hT_r.next()
            hTs.append((hk, hT))
            for t in range(4):
                xk, xt = xt_r.next()
                r0 = g * 512 + t * 128
                dma(xt, xsrc[r0:r0 + 128, :], [xsrc_key], [xk])
                norm_T(xt, xk, 8, D, gm, "gm", hT[:, :, t * 128:(t + 1) * 128], hk, rot)
            fm_proj(g, hk, hT, ((0, "QA"), (512, "KA")))
            tm_v(g, hk, hT, 1024, va_r, 4, 64, "VA")
        for n_ in ("QA", "KA", "VA0", "VA1"):
            allgather(n_)
        for g in range(NGL):
            hk, hT = hTs[g]
            fm_proj(g, hk, hT, ((1536, "QB"), (2048, "KB")))
            tm_v(g, hk, hT, 2560, vb_r, 2, 128, "VB")
        for n_ in ("QB", "KB", "VB0", "VB1"):
            allgather(n_)
        for g in range(NGL):
            hk, hT = hTs[g]
            tok = slice(g * 512, (g + 1) * 512)
            ck, cs = cs_r.next()
            sk, sn = sn_r.next()
            dma(cs[0:32], cos_d[:, tok], [], [ck])
            dma(sn[0:32], sin_d[:, tok], [], [sk])
            pk, pp = mm_r.next()
            for k in range(8):
                mm(pp[0:32, :], win[:, k, 4096:4128], hT[:, k, :], k == 0, k == 7, [wkey(4096), hk], [pk])
            pk2, pp2 = mm_r.next()
            for k in range(8):
                mm(pp2[0:32, :], wkrs[:, k, :], hT[:, k, :], k == 0, k == 7, ["wkrs", hk], [pk2])
            fk, fs = fst_r.next()
            rope_evac(pp, pk, pp2, pk2, fs[0:32], fk, ck, cs, sk, sn)
            for h in range(8):
                dma(XL("KC%d" % (h // 4))[(h % 4) * 96:(h % 4) * 96 + 32, tok], fs[0:32], [fk], ["KC%d" % (h // 4)], waw=False)
            cqk, cqT = cqT_r.next()
            ckk, ckvT = ckvT_r.next()
            def c_front(t):
                ts_ = slice(t * 128, (t + 1) * 128)
                qk_, cq = cq_r.next()
                pk, pp = mm_r.next()
                tokmm(hT, hk, ts_, 3072, 512, pp, pk)
                evac_copy(cq[:, 0:512], pp, [pk], [qk_], waw=False)
                pk, pp = mm_r.next()
                tokmm(hT, hk, ts_, 3584, 256, pp, pk)
                evac_copy(cq[:, 512:768], pp[:, 0:256], [pk], [qk_], waw=False)
                st_q = norm_T_front(cq, qk_, 6, 768, gq, "gq", rot)
                kk_, ckv = ckv_r.next()
                pk, pp = mm_r.next()
                tokmm(hT, hk, ts_, 3840, 256, pp, pk)
                evac_copy(ckv, pp[:, 0:256], [pk], [kk_])
                st_k = norm_T_front(ckv, kk_, 2, 256, gkv, "gkv", rot)
                return (st_q, st_k)

            def c_back(t, st):
                ts_ = slice(t * 128, (t + 1) * 128)
                r0 = g * 512 + t * 128
                norm_T_back(st[0], cqT[:, :, ts_], cqk, rot)
                norm_T_back(st[1], ckvT[:, :, ts_], ckk, rot)
                pk, pp = mm_r.next()
                for k in range(2):
                    mm(pp, ckvT[:, k, ts_], wukv[:, k, 512:1024], k == 0, k == 1, ["wukv", ckk], [pk])
                vk, vs = vc_r.next()
                evac_copy(vs[:, :, 0:64], pp.rearrange("p (h d) -> p h d", d=64), [pk], [vk])
                for c_ in range(2):
                    dma(XL("VC%d" % c_)[r0:r0 + 128, :], vs[:, 4 * c_:4 * c_ + 4, :].rearrange("p h d -> p (h d)"), [vk], ["VC%d" % c_], waw=False)

            sts = {}
            sts[0] = c_front(0)
            for t in range(4):
                if t + 1 < 4:
                    sts[t + 1] = c_front(t + 1)
                c_back(t, sts[t])
            for h in range(8):
                pk, pp = mm_r.next()
                for k in range(6):
                    mm(pp[0:96, :], wuq[:, k, h * 96:(h + 1) * 96], cqT[:, k, :], k == 0, k == 5, ["wuq", cqk], [pk])
                pk2, pp2 = mm_r.next()
                for k in range(6):
                    mm(pp2[0:32, :], wuqs[:, k, h * 32:(h + 1) * 32], cqT[:, k, :], k == 0, k == 5, ["wuqs", cqk], [pk2])
                fk, fs = fst_r.next()
                act(fs[32:64], pp[32:64, :], AF.Copy, [pk], [fk])
                tcopy("dve", fs[64:96], pp[64:96, :], [pk], [fk], waw=False)
                rope_evac(pp, pk, pp2, pk2, fs[0:32], fk, ck, cs, sk, sn)
                dma(XL("QC%d" % (h // 4))[(h % 4) * 96:(h % 4 + 1) * 96, tok], fs[0:96], [fk], ["QC%d" % (h // 4)], waw=False)
            for hp in range(4):
                pk, pp = mm_r.next()
                for k in range(2):
                    mm(pp, wukv[:, k, hp * 128:(hp + 1) * 128], ckvT[:, k, :], k == 0, k == 1, ["wukv", ckk], [pk])
                fk, fs = fst_r.next()
                evac_copy(fs, pp, [pk], [fk])
                for j in range(2):
                    h = hp * 2 + j
                    dma(XL("KC%d" % (h // 4))[(h % 4) * 96 + 32:(h % 4 + 1) * 96, tok], fs[j * 64:(j + 1) * 64], [fk], ["KC%d" % (h // 4)], waw=False)
        for n_ in ("QC0", "QC1", "KC0", "KC1", "VC0", "VC1"):
            allgather(n_)

    SKEW = 2

    def run_pipeline(steps):
        n = len(steps)
        for i in range(n + SKEW):
            if i < n:
                steps[i][0]()
            if i - SKEW >= 0:
                steps[i - SKEW][1]()

    def phase2a(l):
        P.barrier()
        A.reset()
        Vp = [A.alloc([NT, 260], BF16) for _ in range(3)]
        stg = [A.alloc([NT * 260], BF16) for _ in range(2)]
        for pi, (win_, dil) in enumerate(PATS):
            nb = S // dil // 128
            for c in range(2):
                cv = stg[c].rearrange("p (t c) -> p t c", c=260)
                for r in range(dil):
                    src = bass.AP(XG("VA%d" % c), r * 260, [[dil * 260, 128], [dil * 128 * 260, nb], [1, 260]])
                    dma(cv[:, r * nb:(r + 1) * nb, :], src, ["VA%dg" % c], ["stg%d" % c], waw=False)
            sel(Vp[pi].rearrange("p t c -> p (t c)"), "Vp%d" % pi, stg[0], "stg0", stg[1], "stg1")
        q_r = Rot("qa", [A.alloc([S], BF16) for _ in range(2)])
        k_r = Rot("ka", [A.alloc([S], BF16) for _ in range(2)])
        ea_r = Rot("ea", [A.alloc([3, 256], BF16) for _ in range(2)])
        acc_r = Rot("acc", [A.alloc([S], F32) for _ in range(2)])
        pt_r = Rot("pt", [A.alloc([256], BF16) for _ in range(4)])
        pm_r = Rot("pm", [A.alloc([256], BF16) for _ in range(4)])
        rl_r = Rot("rl", [A.alloc([512], F32) for _ in range(2)])
        ys_r = Rot("ys", [A.alloc([512], BF16) for _ in range(2)])
        ps_s = Rot("pss", [psb[i][:] for i in range(3)], P, [0, 1, 2])
        ps_o = Rot("pso", [psb[3 + i][:, 0:128] for i in range(3)], P, [3, 4, 5])
        ps_l = Rot("psl", [psb[6][:], psb[7][:]], P, [6, 7])
        steps = []

        def mk_front(kk, qk, ek, sk_, sp_, kap, qap, nq, tk, pt, mk, pm, eap):
            def f():
                mm(sp_[:, 0:nq], kap, qap, True, True, [kk, qk], [sk_])
                act(pt[:, 0:nq], sp_[:, 0:nq], AF.Exp, [sk_], [tk], scale=0.125)
                tt("dve", pm[:, 0:nq], pt[:, 0:nq], eap, ALU.mult, [tk, ek], [mk])
            return f

        def mk_back(b, nb, mk, pm, vt, vkey, okey, o_, okey_n, o_n, aap, pi, acck, tail):
            def f():
                mm(o_[0:65, :], vt, pm[:, 0:128], b == 0, True, [mk, vkey], [okey], skip=True)
                if b + 1 < nb:
                    mm(o_n[0:65, :], vt, pm[:, 128:256], True, False, [mk, vkey], [okey_n], skip=True)
                if pi == 0:
                    tcopy("dve", aap, o_[0:65, :], [okey], [acck], waw=False)
                else:
                    tt("dve", aap, aap, o_[0:65, :], ALU.add, [okey, acck], [acck], waw=False)
                if tail is not None:
                    tail()
            return f

        def mk_tail(h, acc, acck):
            def f():
                for c in range(NG):
                    cs_ = slice(c * 512, (c + 1) * 512)
                    lk, lp = ps_l.next()
                    mm(lp[0:64, :], onesf[64:65, 0:64], acc[64:65, cs_], True, True, [acck, "onesf"], [lk])
                    rk, rl = rl_r.next()
                    recip(rl[0:64], lp[0:64, :], [lk], [rk])
                    yk, ys = ys_r.next()
                    tt("dve", ys[0:64], acc[0:64, cs_], rl[0:64], ALU.mult, [rk, acck], [yk])
                    dma(XL("YTA")[h * 64:(h + 1) * 64, cs_], ys[0:64], [yk], ["YTA"], waw=False)
            return f

        def mk_loads(h, qk, qt, kk, kt, ek, ea):
            def f():
                for (dst, dk, xn_, so) in ((qt, qk, "QA", 0), (kt, kk, "KA", 4096)):
                    for c in range(2):
                        for r in range(2):
                            b0 = r * 512 + (4 * c + h) * 64
                            dma(stg[c][0:64, so + r * SL:so + (r + 1) * SL], XG(xn_)[b0:b0 + 64, :], [xn_ + "g"], ["stg%d" % c], waw=False)
                    sel(dst[0:64, :], dk, stg[0][0:64, so:so + S], "stg0", stg[1][0:64, so:so + S], "stg1", np_=64)
                for pi in range(3):
                    dma(ea[:, pi, :], bass.AP(FA_D, ((h * 3 + pi) * 128) * LA + 127, [[LA - 1, 128], [1, 256]]), ["FA_D"], [ek], waw=False)
            return f

        for h in range(4):
            qk, qt = q_r.next()
            kk, kt = k_r.next()
            ek, ea = ea_r.next()
            acck, acc = acc_r.next()
            loads = mk_loads(h, qk, qt, kk, kt, ek, ea)
            first = True
            for pi, (win_, dil) in enumerate(PATS):
                nb = S // dil // 128
                vkey = "Vp%d" % pi
                for r in range(dil):
                    okey_n, o_n = None, None
                    for b in range(nb):
                        nq = 256 if b + 1 < nb else 128
                        base = r + dil * 128 * b
                        kap = kt[0:64, base:base + dil * 127 + 1:dil]
                        qap = qt[0:64, base:base + dil * (nq - 1) + 1:dil]
                        sk_, sp_ = ps_s.next()
                        tk, pt = pt_r.next()
                        mk, pm = pm_r.next()
                        fr = mk_front(kk, qk, ek, sk_, sp_, kap, qap, nq, tk, pt, mk, pm, ea[:, pi, 0:nq])
                        if first:
                            fr = (lambda lo, f0: (lambda: (lo(), f0())))(loads, fr)
                            first = False
                        vt = Vp[pi][:, r * nb + b, h * 65:(h + 1) * 65]
                        if b == 0:
                            okey, o_ = ps_o.next()
                        else:
                            okey, o_ = okey_n, o_n
                        if b + 1 < nb:
                            okey_n, o_n = ps_o.next()
                        aap = acc[0:65, base:base + dil * 127 + 1:dil]
                        last = (pi == 2 and r == dil - 1 and b == nb - 1)
                        tail = mk_tail(h, acc, acck) if last else None
                        bk = mk_back(b, nb, mk, pm, vt, vkey, okey, o_, okey_n, o_n, aap, pi, acck, tail)
                        steps.append((fr, bk))
        run_pipeline(steps)
        allgather("YTA")

    def phase2bc(l, which):
        P.barrier()
        A.reset()
        isB = which == "B"
        nh = 2 if isB else 4
        dv = 128 if isB else 64
        dr = 128 if isB else 96
        vw = (dv + 1) * nh
        vpre, yname = ("VB", "YB") if isB else ("VC", "YC")
        scale = 0.125 if isB else 96.0 ** -0.5
        Vt = A.alloc([NT, vw], BF16)
        stg = [A.alloc([NT * 260], BF16) for _ in range(2)]
        for c in range(2):
            dma(stg[c][:, 0:NT * vw].rearrange("p (t c) -> p t c", c=vw), XG("%s%d" % (vpre, c)).ap().rearrange("(t p) c -> p t c", p=128), ["%s%dg" % (vpre, c)], ["stg%d" % c])
        sel(Vt.rearrange("p t c -> p (t c)"), "Vt", stg[0][:, 0:NT * vw], "stg0", stg[1][:, 0:NT * vw], "stg1")
        Ysb = A.alloc([NT, 256], BF16)
        q_r = Rot("qb", [A.alloc([S], BF16) for _ in range(2)])
        k_r = Rot("kb", [A.alloc([S], BF16) for _ in range(2)])
        e_r = Rot("eb", [A.alloc([512], BF16) for _ in range(4)])
        pt_r = Rot("ptb", [A.alloc([512], BF16) for _ in range(4)])
        pm_r = Rot("pmb", [A.alloc([512], BF16) for _ in range(8)])
        ps_s = Rot("pssb", [psb[i][:] for i in range(3)], P, [0, 1, 2])
        gs = nlam = knl = None
        if isB:
            gs = A.alloc([128], F32)
            lv = A.alloc([256], F32)
            lj = A.alloc([64], F32)
            t0_r = Rot("t0", [A.alloc([128], F32) for _ in range(2)])
            ob_r = Rot("ob", [A.alloc([128], F32) for _ in range(2)])
            jb_r = Rot("jb", [A.alloc([128], F32) for _ in range(2)])
            dma(gs, g_sub[l, :, :], [], ["gs"])
            dma(lv, lam_v[l, :, :], [], ["lv"])
            lam_init = 0.8 - 0.6 * math.exp(-0.3 * l)
            P.op("act", lambda e: e.mul(out=gs, in_=gs, mul=float(1.0 - lam_init)), reads=["gs"], writes=["gs"])
            k1, s1 = "lam_s1", A.alloc([1], F32)
            k2, s2 = "lam_s2", A.alloc([1], F32)
            knl, nlam = "nlam", A.alloc([1], F32)
            tt("dve", lj, lv[:, 0:64], lv[:, 64:128], ALU.mult, ["lv"], ["lj"])
            P.op("dve", lambda e: e.reduce_sum(out=s1, in_=lj, axis=mybir.AxisListType.X), reads=["lj"], writes=[k1])
            tt("dve", lj, lv[:, 128:192], lv[:, 192:256], ALU.mult, ["lv", k1], ["lj"])
            P.op("dve", lambda e: e.reduce_sum(out=s2, in_=lj, axis=mybir.AxisListType.X), reads=["lj"], writes=[k2])
            act(s1, s1, AF.Exp, [k1], [k1])
            act(s2, s2, AF.Exp, [k2], [k2])
            stt(nlam, s2, float(-lam_init), s1, ALU.add, ALU.subtract, [k1, k2], [knl])
        if isB:
            regs = [psb[3 + i // 3][:, (i % 3) * 129:(i % 3) * 129 + 129] for i in range(8)]
            okeys = ["oacc%d" % (3 + i // 3) for i in range(8)]
        else:
            regs = [psb[3 + (i // 4)][:, (i % 4) * 65:(i % 4) * 65 + 65] for i in range(8)]
            okeys = ["oacc%d" % (3 + i // 4) for i in range(8)]
        for b_ in (3, 4, 5):
            P.bank_of["oacc%d" % b_] = b_
        steps = []

        def mk_loads(h, qk, qt, kk, kt):
            def f():
                for (dst, dk, qk_sel, so) in ((qt, qk, "Q", 0), (kt, kk, "K", 4096)):
                    for c in range(2):
                        for r in range(2):
                            if isB:
                                xn_, b0 = qk_sel + "B", r * 512 + (2 * c + h) * 128
                            else:
                                xn_, b0 = "%sC%d" % (qk_sel, c), r * 384 + h * 96
                            dma(stg[c][0:dr, so + r * SL:so + (r + 1) * SL], XG(xn_)[b0:b0 + dr, :], [xn_ + "g"], ["stg%d" % c], waw=False)
                    sel(dst[0:dr, :], dk, stg[0][0:dr, so:so + S], "stg0", stg[1][0:dr, so:so + S], "stg1", np_=dr)
            return f

        def mk_front(h, g, j, qk, qt, kk, kt, bufs, loads):
            qlo = max(4 * g, j)
            W = (4 * g + 4 - qlo) * 128

            def f():
                if loads is not None:
                    loads()
                if isB:
                    ek, et = bufs["e"]
                    off = (h * 128) * LB + (qlo - j) * 128 + 127
                    dma(et[:, 0:W], bass.AP(FB_D, off, [[LB - 1, 128], [1, W]]), ["FB_D"], [ek])
                for w, (sk_, sp_, tk, pt, mk, pm) in enumerate(bufs["w"]):
                    rows = slice(w * 64, (w + 1) * 64) if isB else slice(0, 96)
                    mm(sp_[:, 0:W], kt[rows, j * 128:(j + 1) * 128], qt[rows, qlo * 128:(4 * g + 4) * 128], True, True, [kk, qk], [sk_])
                    act(pt[:, 0:W], sp_[:, 0:W], AF.Exp, [sk_], [tk], scale=float(scale))
                    if isB:
                        tt("dve", pm[:, 0:W], pt[:, 0:W], et[:, 0:W], ALU.mult, [tk, ek], [mk])
                    elif qlo == j:
                        tt("dve", pt[:, 0:128], pt[:, 0:128], mkc, ALU.mult, [tk, "mkc"], [tk])
            return f

        def mk_back(h, g, j, bufs, rsel):
            qlo = max(4 * g, j)

            def f():
                for w, (sk_, sp_, tk, pt, mk, pm) in enumerate(bufs["w"]):
                    for qb in range(qlo, 4 * g + 4):
                        ri = rsel[w * 4 + (qb - 4 * g)] if isB else rsel[qb - 4 * g]
                        c0 = (qb - qlo) * 128
                        stf = (j == 0) and (okeys[ri] not in bufs["started"])
                        bufs["started"].add(okeys[ri])
                        mm(regs[ri], pm[:, c0:c0 + 128], Vt[:, j, h * (dv + 1):(h + 1) * (dv + 1)], stf, j == qb, [mk, "Vt"], [okeys[ri]], skip=True)
                if j == 4 * g + 3:
                    epilogue(h, g, rsel)
            return f

        def epilogue(h, g, rsel):
            for qi in range(4):
                qb = 4 * g + qi
                if isB:
                    r0_, r1_ = regs[rsel[qi]], regs[rsel[4 + qi]]
                    ok0, ok1 = okeys[rsel[qi]], okeys[rsel[4 + qi]]
                    ka, ra = small()
                    kb_, rb = small()
                    recip(ra, r0_[:, 128:129], [ok0], [ka])
                    recip(rb, r1_[:, 128:129], [ok1], [kb_])
                    tt("dve", rb, rb, nlam, ALU.mult, [kb_, knl], [kb_])
                    tk0, t0 = t0_r.next()
                    act(t0, r0_[:, 0:128], AF.Copy, [ok0, ka], [tk0], scale=ra)
                    obk, ob = ob_r.next()
                    stt(ob, r1_[:, 0:128], rb, t0, ALU.mult, ALU.add, [ok1, kb_, tk0], [obk])
                    jk, jb = jb_r.next()
                    krs, rs = rms_scale(ob, obk, 128, jb, jk)
                    stt(Ysb[:, qb, h * 128:(h + 1) * 128], ob, rs, gs, ALU.mult, ALU.mult, [obk, krs, "gs"], ["Ysb"], waw=False)
                else:
                    rg = regs[rsel[qi]]
                    ok0 = okeys[rsel[qi]]
                    ka, ra = small()
                    recip(ra, rg[:, 64:65], [ok0], [ka])
                    act(Ysb[:, qb, h * 64:(h + 1) * 64], rg[:, 0:64], AF.Copy, [ok0, ka], ["Ysb"], scale=ra, waw=False)

        gpar = 0
        nw = 2 if isB else 1
        for h in range(nh):
            qk, qt = q_r.next()
            kk, kt = k_r.next()
            loads = mk_loads(h, qk, qt, kk, kt)
            for g in range(NG):
                if isB:
                    rsel = list(range(8))
                else:
                    rsel = [(gpar % 2) * 4 + i for i in range(4)]
                    gpar += 1
                started = set()
                for j in range(4 * g + 4):
                    bufs = {"w": [], "started": started}
                    if isB:
                        bufs["e"] = e_r.next()
                    for w in range(nw):
                        sk_, sp_ = ps_s.next()
                        tk, pt = pt_r.next()
                        if isB:
                            mk, pm = pm_r.next()
                        else:
                            mk, pm = tk, pt
                        bufs["w"].append((sk_, sp_, tk, pt, mk, pm))
                    steps.append((mk_front(h, g, j, qk, qt, kk, kt, bufs, loads), mk_back(h, g, j, bufs, rsel)))
                    loads = None
        run_pipeline(steps)
        dma(XL(yname).ap().rearrange("(t p) c -> p t c", p=128), Ysb, ["Ysb"], [yname])
        allgather(yname)

    def phase3a(l, xsrc, xsrc_key):
        P.barrier()
        A.reset()
        wg = A.alloc([8, 3 * D], BF16)
        wbr = [A.alloc([4, D], BF16) for _ in range(3)]
        wo = A.alloc([8, D], BF16)
        gm = A.alloc([D], F32)
        bg = A.alloc([24], F32)
        dma(gm, g_mix[l, :, :], [], ["gm"])
        dma(bg, b_gate[l, :, :], [], ["bg"])
        for f2 in range(4):
            wload_cols(wg, lambda c0, f2=f2: "wg@%d" % f2, w_gate[l, :, :], 8, [(br * D + f2 * 256, br * D + (f2 + 1) * 256) for br in range(3)])
            for i, wsrc in enumerate((w_bra, w_brb, w_brc)):
                wload_cols(wbr[i], lambda c0, f2=f2: "wbr@%d" % f2, wsrc[l, :, :], 4, [(f2 * 256, (f2 + 1) * 256)])
        wload(wo, "wo", w_o[l, :, :], 8)
        rot = std_rots()
        xt_r = Rot("xt3", [A.alloc([D], F32) for _ in range(3)])
        xr_r = Rot("xr3", [A.alloc([D], F32) for _ in range(2)])
        hT_r = Rot("hT3", [A.alloc([8, 512], BF16) for _ in range(2)])
        yT_r = [Rot("yT%d" % i, [A.alloc([4, 512], BF16) for _ in range(2)]) for i in range(3)]
        yl_r = Rot("yl", [A.alloc([512], BF16) for _ in range(3)])
        ysg = [A.alloc([4, 512], BF16) for _ in range(2)]
        ylg = [A.alloc([512], BF16) for _ in range(2)]
        mT_r = Rot("mT", [A.alloc([8, 512], BF16)])
        gt_r = Rot("gt", [A.alloc([512], F32) for _ in range(3)])
        m_r = Rot("m", [A.alloc([512], F32) for _ in range(2)])
        t_r = Rot("t", [A.alloc([512], F32) for _ in range(2)])
        xo_r = Rot("xo", [A.alloc([D], F32) for _ in range(2)])
        mm_r = Rot("psm3", [psb[i][:] for i in range(8)], P, list(range(8)))
        for g in range(NGL):
            tok = slice(g * 512, (g + 1) * 512)
            hk, hT = hT_r.next()
            xts = []
            for t in range(4):
                xk, xt = xt_r.next()
                r0 = g * 512 + t * 128
                dma(xt, xsrc[r0:r0 + 128, :], [xsrc_key], [xk])
                norm_T(xt, xk, 8, D, gm, "gm", hT[:, :, t * 128:(t + 1) * 128], hk, rot)
                xts.append((xk, xt))
            yks = []
            yk, yT = yT_r[0].next()
            for c in range(2):
                dma(ysg[c], XG("YTA").ap().rearrange("(c p) s -> p c s", p=128)[:, :, c * SL + g * 512:c * SL + (g + 1) * 512], ["YTAg"], ["ysg%d" % c])
            sel(yT.rearrange("p c s -> p (c s)"), yk, ysg[0].rearrange("p c s -> p (c s)"), "ysg0", ysg[1].rearrange("p c s -> p (c s)"), "ysg1")
            yks.append((yk, yT))
            for bi, yname in enumerate(("YB", "YC")):
                yk, yT = yT_r[1 + bi].next()
                for t in range(4):
                    lk, yl = yl_r.next()
                    r0 = g * 512 + t * 128
                    for c in range(2):
                        for r in range(2):
                            t0_ = r * S + c * SL + r0
                            dma(ylg[c][:, r * 256:(r + 1) * 256], XG(yname)[t0_:t0_ + 128, :], [yname + "g"], ["ylg%d" % c], waw=False)
                    sel(yl, lk, ylg[0], "ylg0", ylg[1], "ylg1")
                    pk, pp = rot["pT"].next()
                    ppb = pp.bitcast(BF16)
                    for c in range(4):
                        tr(ppb[:, c * 128:(c + 1) * 128], yl[:, c * 128:(c + 1) * 128], [lk], [pk])
                    tcopy("dve", yT[:, :, t * 128:(t + 1) * 128], ppb[:, 0:512].rearrange("p (c t) -> p c t", t=128), [pk], [yk], waw=False)
                yks.append((yk, yT))
            mk, mT = mT_r.next()
            for fc in range(8):
                mkk, m = m_r.next()
                for br in range(3):
                    yk, yT = yks[br]
                    pgk, pg = mm_r.next()
                    col = br * D + fc * 128
                    for k in range(8):
                        mm(pg, wg[:, k, col:col + 128], hT[:, k, :], k == 0, k == 7, ["wg@%d" % (fc // 2), hk], [pgk])
                    pbk, pb = mm_r.next()
                    for k in range(4):
                        mm(pb, wbr[br][:, k, fc * 128:(fc + 1) * 128], yT[:, k, :], k == 0, k == 3, ["wbr@%d" % (fc // 2), yk], [pbk])
                    gk, gt = gt_r.next()
                    act(gt, pg, AF.Sigmoid, [pgk, "bg"], [gk], bias=bg[:, br * 8 + fc:br * 8 + fc + 1])
                    if br == 0:
                        tt("dve", m, gt, pb, ALU.mult, [gk, pbk], [mkk])
                    else:
                        tk, tt_ = t_r.next()
                        tt("dve", tt_, gt, pb, ALU.mult, [gk, pbk], [tk])
                        if br == 1:
                            tt("dve", m, m, tt_, ALU.add, [mkk, tk], [mkk])
                        else:
                            tt("dve", mT[:, fc, :], m, tt_, ALU.add, [mkk, tk], [mk], waw=False)
            for t in range(4):
                xk, xt = xr_r.next()
                ok_, xo = xo_r.next()
                r0 = g * 512 + t * 128
                dma(xt, xsrc[r0:r0 + 128, :], [xsrc_key], [xk])
                for cc in range(2):
                    pk, pp = mm_r.next()
                    for k in range(8):
                        mm(pp, mT[:, k, t * 128:(t + 1) * 128], wo[:, k, cc * 512:(cc + 1) * 512], k == 0, k == 7, ["wo", mk], [pk])
                    tt("dve", xo[:, cc * 512:(cc + 1) * 512], pp, xt[:, cc * 512:(cc + 1) * 512], ALU.add, [pk, xk], [ok_], waw=False)
                dma(XMID[r0:r0 + 128, :], xo, [ok_], ["XMID"], waw=False)

    def phase3b(l, last):
        P.barrier()
        A.reset()
        GT = 256
        wfg = A.alloc([8, FH], BF16)
        wfu = A.alloc([8, FH], BF16)
        wfd = A.alloc([22, D], BF16)
        gf = A.alloc([D], F32)
        dma(gf, g_ffn[l, :, :], [], ["gf"])
        gfin = None
        if last:
            gfin = A.alloc([D], F32)
            dma(gfin, g_fin[:, :], [], ["gfin"])
        for h4 in range(6):
            c0_, c1_ = h4 * 512, min(FH, (h4 + 1) * 512)
            wload_cols(wfg, lambda c0, h4=h4: "wf@%d" % h4, w_fg[l, :, :], 8, [(c0_, c1_)])
            wload_cols(wfu, lambda c0, h4=h4: "wf@%d" % h4, w_fu[l, :, :], 8, [(c0_, c1_)])
        wload(wfd, "wfd", w_fd[l, :, :], 22)
        rot = std_rots()
        xt_r = Rot("xt4", [A.alloc([D], F32) for _ in range(4)])
        hT_r = Rot("hT4", [A.alloc([8, GT], BF16) for _ in range(2)])
        aT_r = Rot("aT", [A.alloc([22, GT], BF16)])
        sg_r = Rot("sg", [A.alloc([GT], F32) for _ in range(2)])
        xo_r = Rot("xo4", [A.alloc([D], F32) for _ in range(2)])
        fo_r = Rot("fo4", [A.alloc([D], F32) for _ in range(2)])
        mm_r = Rot("psm4", [psb[i][:] for i in range(6)], P, list(range(6)))
        nt = GT // 128
        for g in range(SL // GT):
            hk, hT = hT_r.next()
            xts = []
            for t in range(nt):
                xk, xt = xt_r.next()
                r0 = g * GT + t * 128
                dma(xt, XMID[r0:r0 + 128, :], ["XMID"], [xk])
                norm_T(xt, xk, 8, D, gf, "gf", hT[:, :, t * 128:(t + 1) * 128], hk, rot)
                xts.append((xk, xt))
            ak, aT = aT_r.next()
            for hc in range(22):
                pgk, pg = mm_r.next()
                for k in range(8):
                    mm(pg[:, 0:GT], wfg[:, k, hc * 128:(hc + 1) * 128], hT[:, k, :], k == 0, k == 7, ["wf@%d" % (hc // 4), hk], [pgk])
                puk, pu = mm_r.next()
                for k in range(8):
                    mm(pu[:, 0:GT], wfu[:, k, hc * 128:(hc + 1) * 128], hT[:, k, :], k == 0, k == 7, ["wf@%d" % (hc // 4), hk], [puk])
                sk, sg = sg_r.next()
                act(sg, pg[:, 0:GT], AF.Silu, [pgk], [sk])
                tt("dve", aT[:, hc, :], sg, pu[:, 0:GT], ALU.mult, [sk, puk], [ak], waw=False)
            for t in range(nt):
                xk, xt = xts[t]
                ok_, xo = xo_r.next()
                r0 = g * GT + t * 128
                for cc in range(2):
                    pk, pp = mm_r.next()
                    for hc in range(22):
                        mm(pp, aT[:, hc, t * 128:(t + 1) * 128], wfd[:, hc, cc * 512:(cc + 1) * 512], hc == 0, hc == 21, ["wfd", ak], [pk])
                    tt("dve", xo[:, cc * 512:(cc + 1) * 512], pp, xt[:, cc * 512:(cc + 1) * 512], ALU.add, [pk, xk], [ok_], waw=False)
                if not last:
                    dma(XRES[r0:r0 + 128, :], xo, [ok_], ["XRES"], waw=False)
                else:
                    jk, junk = rot["junk"].next()
                    kr, rs = rms_scale(xo, ok_, D, junk, jk)
                    fk, fo = fo_r.next()
                    stt(fo, xo, rs, gfin, ALU.mult, ALU.mult, [ok_, kr, "gfin"], [fk])
                    dma(out_d[r0:r0 + 128, :], fo, [fk], ["out"], waw=False)

    phases = build.phases
    if "0" in phases:
        phase0()
    for l in range(L):
        xsrc, xkey = (x_in, "x") if l == 0 else (XRES, "XRES")
        if "1" in phases:
            phase1(l, xsrc, xkey)
        if "a" in phases:
            phase2a(l)
        if "b" in phases:
            phase2bc(l, "B")
        if "c" in phases:
            phase2bc(l, "C")
        if "3" in phases:
            phase3a(l, xsrc, xkey)
        if "4" in phases:
            phase3b(l, l == L - 1)
    finals = [("dma", "out")]
    if dbg:
        P.barrier()
        for n_ in ("QA", "KA", "VA0", "QB", "VB1", "QC0", "KC1", "VC0", "YTA", "YB", "YC"):
            src = XG(n_)
            dd = nc.dram_tensor("dbg_" + n_, list(src.shape), BF16, kind="ExternalOutput")
            rows = src.shape[0]
            for r0 in range(0, rows, 2048):
                r1 = min(rows, r0 + 2048)
                dma(dd[r0:r1, :], src[r0:r1, :], [n_ + "g"], ["dbg_" + n_], waw=False)
            finals.append(("dma", "dbg_" + n_))
    if dbg:
        finals += [("dma", n) for n in ("XMID", "XRES")]
    P.emit(final_wait_streams=finals)
    st.close()
    return nc


build.phases = "01abc34"


def host_inputs(S, L, x_loc, p, rank):
    f = np.float32
    SL = S // 2
    LB = ((S + 127 + 383) // 384) * 384
    rep = lambda v: np.ascontiguousarray(np.broadcast_to(v[:, None, :], (v.shape[0], 128, v.shape[1])).astype(f))
    w_in = np.ascontiguousarray(p["w_in"][:L])
    kr = w_in[:, :, 4096:4128]
    w_krs = np.ascontiguousarray(np.concatenate([kr[:, :, 16:32], kr[:, :, 0:16]], axis=-1))
    wuq = p["w_uq"][:L].reshape(L, 768, 8, 96)
    w_uqp = np.ascontiguousarray(np.concatenate([wuq[..., 64:96], wuq[..., 0:64]], axis=-1).reshape(L, 768, 768))
    w_uqs = np.ascontiguousarray(np.concatenate([wuq[..., 80:96], wuq[..., 64:80]], axis=-1).reshape(L, 768, 256))
    wukv = p["w_ukv"][:L].reshape(L, 256, 8, 128)
    w_ukvp = np.ascontiguousarray(np.concatenate([wukv[..., 0:64].reshape(L, 256, 512), wukv[..., 64:128].reshape(L, 256, 512)], axis=-1))
    lam_v = np.concatenate([p["lambda_q1"][:L], p["lambda_k1"][:L], p["lambda_q2"][:L], p["lambda_k2"][:L]], axis=-1)
    bgt = np.ascontiguousarray(p["b_gate"][:L].reshape(L, 24, 128).transpose(0, 2, 1))
    tab = p["rel_bias_table"].astype(f)
    tabaug = np.concatenate([tab, np.full((1, 12), -30000.0, f)], axis=0)
    cols = [4 * rank + i for i in range(4)] + [8 + 2 * rank + i for i in range(2)]
    tabrep = np.ascontiguousarray(np.broadcast_to(tabaug.T[cols][:, :, None], (6, 33, 128)).astype(f))
    dist = np.arange(LB) - 127
    bk = np.where(dist >= 0, t5_bucket_np(dist), 32)
    oh_b = np.zeros((33, LB), f)
    oh_b[bk, np.arange(LB)] = 1.0
    oh_a = np.zeros((33, 3 * LA), f)
    for pi, (win_, dil) in enumerate(PATS):
        step = np.arange(LA) - 127
        ok = (step >= 0) & (step <= 128)
        b_ = np.where(ok, t5_bucket_np(step * dil), 32)
        oh_a[b_, pi * LA + np.arange(LA)] = 1.0
    pos = np.arange(rank * SL, (rank + 1) * SL).astype(f)
    inv = (np.float32(10000.0) ** (-np.arange(0, 32, 2, dtype=f) / np.float32(32))).astype(f)
    ang = (pos[:, None] * inv[None, :]).astype(f)
    cos, sin = np.cos(ang).astype(f).T, np.sin(ang).astype(f).T
    cos32 = np.ascontiguousarray(np.concatenate([cos, cos], axis=0))
    sin32 = np.ascontiguousarray(np.concatenate([-sin, sin], axis=0))
    kk = np.arange(128)
    m = {
        "x": np.ascontiguousarray(x_loc.astype(f)),
        "w_in": w_in, "w_krs": w_krs, "w_uqp": w_uqp, "w_uqs": w_uqs, "w_ukvp": w_ukvp,
        "w_gate": np.ascontiguousarray(p["w_gate"][:L]),
        "w_br_a": np.ascontiguousarray(p["w_br_a"][:L]), "w_br_b": np.ascontiguousarray(p["w_br_b"][:L]),
        "w_br_c": np.ascontiguousarray(p["w_br_c"][:L]), "w_o": np.ascontiguousarray(p["w_o"][:L]),
        "w_ffn_gate": np.ascontiguousarray(p["w_ffn_gate"][:L]), "w_ffn_up": np.ascontiguousarray(p["w_ffn_up"][:L]),
        "w_ffn_down": np.ascontiguousarray(p["w_ffn_down"][:L]),
        "g_mix": rep(p["ln_mix_g"][:L]), "g_q": rep(p["mla_q_norm_g"][:L]), "g_kv": rep(p["mla_kv_norm_g"][:L]),
        "g_ffn": rep(p["ln_ffn_g"][:L]), "g_fin": np.ascontiguousarray(np.broadcast_to(p["final_norm_g"][None, :], (128, D)).astype(f)),
        "g_sub": rep(p["diff_subln_g"][:L]), "lam_v": rep(lam_v), "b_gate": bgt.astype(f),
        "tabrep": tabrep, "oh_b": oh_b, "oh_a": oh_a, "cos32": cos32, "sin32": sin32,
        "ident": np.eye(128, dtype=f), "maskc": (kk[None, :] >= kk[:, None]).astype(f),
        "msel": np.ascontiguousarray(np.broadcast_to(np.eye(2, dtype=f)[rank][None, :], (128, 2))),
    }
    return m


_NC_CACHE = {}


def kernel(**inputs):
    p = {k: np.asarray(v) for k, v in inputs.items()}
    x = p["x"]
    B, S, _ = x.shape
    SL = S // 2
    L = p["w_in"].shape[0]
    key = (S, L)
    if key not in _NC_CACHE:
        _NC_CACHE[key] = build(S, L, ncores=2 * B)
    nc = _NC_CACHE[key]
    shared = [host_inputs(S, L, x[0, r * SL:(r + 1) * SL], p, r) for r in range(2)]
    in_maps = []
    for c in range(2 * B):
        b, r = c // 2, c % 2
        m = dict(shared[r])
        m["x"] = np.ascontiguousarray(x[b, r * SL:(r + 1) * SL].astype(np.float32))
        in_maps.append(m)
    res = run_bass_kernel_spmd(nc, in_maps, core_ids=list(range(2 * B)))
    out = np.empty((B, S, D), np.float32)
    for c in range(2 * B):
        b, r = c // 2, c % 2
        out[b, r * SL:(r + 1) * SL] = np.asarray(res.results[c]["out"])
    return out
```
